# Optimizing a Trainium2 kernel written in Bass

```python
import jax, jax.numpy as jnp
from jax import lax
import numpy as np

D_MODEL = 1024
BATCH = 4
SEQ = 8192
DEPTH = 2
DEC_BATCH = 32
DEC_SEQ = 1
PAST_LEN = 16384
PAGE_SIZE = 128

N_A_LAYERS = DEPTH // 2
N_B_LAYERS = DEPTH - N_A_LAYERS
H_A = 4
D_INNER_A = D_MODEL
DK_A = D_INNER_A // H_A
DV_A = D_INNER_A // H_A
CONV_W = 4
CHUNK_A = 64
FORGET_BIAS = 3.0
GROUPS_B = ((128, 1), (512, 4), (2048, 16))
N_GROUPS_B = len(GROUPS_B)
H_G = 8
HD_B = 128
D_GROUP_B = H_G * HD_B
D_FF = 4 * D_MODEL
EPS = 1e-6

kernel_name = 'hybrid_mlstm_dilated_yoco_step'


def rmsnorm(x, g):
    xf = x.astype(jnp.float32)
    y = xf * lax.rsqrt(jnp.mean(xf * xf, axis=-1, keepdims=True) + EPS)
    return (y * g.astype(jnp.float32)).astype(x.dtype)


def ada_mod(c, w, b, n):
    m = (jax.nn.silu(c) @ w + b)[:, None, :]
    return jnp.split(m, n, axis=-1)


def to_heads(x, h):
    b_, t_, _ = x.shape
    return x.reshape(b_, t_, h, -1).transpose(0, 2, 1, 3)


def causal_conv(u, buf, w, b):
    t = u.shape[1]
    full = jnp.concatenate([buf.astype(u.dtype), u], axis=1)
    y = sum(full[:, j:j + t] * w[j] for j in range(CONV_W)) + b
    return y, full[:, t:]


def mlstm_scan(q, k, v, logi, logf, c0, n0, m0):
    bsz, h, t, _ = q.shape
    L = min(CHUNK_A, t)
    nc = -(-t // L)
    pad = nc * L - t

    def chunks(a, fill):
        a = jnp.pad(a, [(0, 0), (0, 0), (0, pad)] + [(0, 0)] * (a.ndim - 3), constant_values=fill)
        a = a.reshape(a.shape[:2] + (nc, L) + a.shape[3:])
        return jnp.moveaxis(a, 2, 0)

    xs = (chunks(q, 0.0), chunks(k, 0.0), chunks(v, 0.0), chunks(logi, -jnp.inf), chunks(logf, 0.0))
    causal = jnp.tril(jnp.ones((L, L), dtype=bool))

    def step(carry, inp):
        cm, nv, m = carry
        qc, kc, vc, li, lf = inp
        b = jnp.cumsum(lf, axis=-1)
        dmat = b[..., :, None] - b[..., None, :] + li[..., None, :]
        dmat = jnp.where(causal, dmat, -jnp.inf)
        inter = b + m[..., None]
        m_t = jnp.maximum(inter, jnp.max(dmat, axis=-1))
        w_inter = jnp.exp(inter - m_t)
        w_intra = jnp.exp(dmat - m_t[..., None])
        s = jnp.einsum('bhtk,bhsk->bhts', qc, kc) * w_intra
        num = w_inter[..., None] * jnp.einsum('bhtk,bhkv->bhtv', qc, cm) + jnp.einsum('bhts,bhsv->bhtv', s, vc)
        den = w_inter * jnp.einsum('bhtk,bhk->bht', qc, nv) + jnp.sum(s, axis=-1)
        hout = num / jnp.maximum(jnp.abs(den), jnp.exp(-m_t))[..., None]
        wl = w_intra[..., -1, :]
        c_new = w_inter[..., -1, None, None] * cm + jnp.einsum('bhs,bhsk,bhsv->bhkv', wl, kc, vc)
        n_new = w_inter[..., -1, None] * nv + jnp.einsum('bhs,bhsk->bhk', wl, kc)
        return (c_new, n_new, m_t[..., -1]), hout

    (cf, nf, mf), hs = lax.scan(step, (c0, n0, m0), xs)
    hs = jnp.moveaxis(hs, 0, 2).reshape(bsz, h, nc * L, -1)[:, :, :t]
    return hs, cf, nf, mf


def mlstm_mixer(h, conv_buf, c0, n0, m0, w_in, b_gate, w_conv, b_conv, w_out):
    bsz, t, _ = h.shape
    f32 = jnp.float32
    proj = h @ w_in
    qk_pre = proj[..., :2 * D_INNER_A]
    v = proj[..., 2 * D_INNER_A:3 * D_INNER_A]
    o_pre = proj[..., 3 * D_INNER_A:4 * D_INNER_A]
    gates = proj[..., 4 * D_INNER_A:].astype(f32) + b_gate.astype(f32)
    qk, conv_new = causal_conv(qk_pre, conv_buf, w_conv, b_conv)
    qk = jax.nn.silu(qk)
    q = to_heads(qk[..., :D_INNER_A], H_A).astype(f32)
    k = to_heads(qk[..., D_INNER_A:], H_A).astype(f32) * (DK_A ** -0.5)
    vh = to_heads(v, H_A).astype(f32)
    logi = gates[..., :H_A].transpose(0, 2, 1)
    logf = jax.nn.log_sigmoid(gates[..., H_A:]).transpose(0, 2, 1)
    hh, cf, nf, mf = mlstm_scan(q, k, vh, logi, logf, c0.astype(f32), n0.astype(f32), m0.astype(f32))
    hh = hh.transpose(0, 2, 1, 3).reshape(bsz, t, D_INNER_A).astype(h.dtype)
    out = (jax.nn.sigmoid(o_pre) * hh) @ w_out
    return out, cf, nf, mf, conv_new


def dilated_band_attention(q, k, v, window, dil):
    bsz, t, h, dh = q.shape
    M = window // dil
    ls = t // dil
    nb = -(-ls // M)
    pad = nb * M - ls

    def to_sub(a):
        a = a.reshape(bsz, ls, dil, h, dh).transpose(0, 2, 1, 3, 4).reshape(bsz * dil, ls, h, dh)
        a = jnp.pad(a, ((0, 0), (0, pad), (0, 0), (0, 0)))
        return a.reshape(bsz * dil, nb, M, h, dh)

    qs, ks, vs = to_sub(q), to_sub(k), to_sub(v)
    pw = ((0, 0), (1, 0), (0, 0), (0, 0), (0, 0))
    kk = jnp.concatenate([jnp.pad(ks, pw)[:, :-1], ks], axis=2)
    vv = jnp.concatenate([jnp.pad(vs, pw)[:, :-1], vs], axis=2)
    s = jnp.einsum('znqhd,znkhd->znhqk', qs, kk).astype(jnp.float32) * (dh ** -0.5)
    iq = jnp.arange(M)[:, None] + M
    ik = jnp.arange(2 * M)[None, :]
    rel = iq - ik
    band = (rel >= 0) & (rel <= M)
    first_ok = (jnp.arange(nb)[:, None] * M - M + jnp.arange(2 * M)[None, :]) >= 0
    valid = band[None] & first_ok[:, None, :]
    s = jnp.where(valid[None, :, None], s, -jnp.inf)
    lse = jax.nn.logsumexp(s, axis=-1)
    p = jnp.exp(s - lse[..., None])
    o = jnp.einsum('znhqk,znkhd->znqhd', p.astype(v.dtype), vv)
    o = o.reshape(bsz * dil, nb * M, h, dh)[:, :ls]
    o = o.reshape(bsz, dil, ls, h, dh).transpose(0, 2, 1, 3, 4).reshape(bsz, t, h, dh)
    lse = lse.transpose(0, 1, 3, 2).reshape(bsz * dil, nb * M, h)[:, :ls]
    lse = lse.reshape(bsz, dil, ls, h).transpose(0, 2, 1, 3).reshape(bsz, t, h)
    return o, lse


def dilated_gather_attention(q, k_new, v_new, k_cache, v_cache, window, dil):
    wb = k_cache.shape[1]
    s_len, dh = q.shape[1], q.shape[-1]
    M = window // dil
    kf = jnp.concatenate([k_cache.astype(k_new.dtype), k_new], axis=1)
    vf = jnp.concatenate([v_cache.astype(v_new.dtype), v_new], axis=1)
    idx = wb + jnp.arange(s_len)[:, None] - dil * jnp.arange(M + 1)[None, :]
    valid = idx >= 0
    idx = jnp.maximum(idx, 0)
    kg = kf[:, idx]
    vg = vf[:, idx]
    s = jnp.einsum('bshd,bsmhd->bshm', q, kg).astype(jnp.float32) * (dh ** -0.5)
    s = jnp.where(valid[None, :, None, :], s, -jnp.inf)
    lse = jax.nn.logsumexp(s, axis=-1)
    p = jnp.exp(s - lse[..., None])
    o = jnp.einsum('bshm,bsmhd->bshd', p.astype(vg.dtype), vg)
    return o, lse


def shared_kv(x, c, g_kv, w_ada_kv, b_ada_kv, w_kv):
    shift, scale = ada_mod(c, w_ada_kv, b_ada_kv, 2)
    hkv = rmsnorm(x, g_kv) * (1 + scale) + shift
    bsz, t, _ = x.shape
    kv = (hkv @ w_kv).reshape(bsz, t, 2, N_GROUPS_B, H_G, HD_B)
    return [(kv[:, :, 0, g], kv[:, :, 1, g]) for g in range(N_GROUPS_B)]


def dilated_mixer(h, kv, kv_caches, w_q, w_o):
    bsz, t, _ = h.shape
    q = (h @ w_q).reshape(bsz, t, N_GROUPS_B, H_G, HD_B)
    outs, lses = [], []
    for g, (win, dil) in enumerate(GROUPS_B):
        k_g, v_g = kv[g]
        if kv_caches is None:
            o, l = dilated_band_attention(q[:, :, g], k_g, v_g, win, dil)
        else:
            o, l = dilated_gather_attention(q[:, :, g], k_g, v_g, kv_caches[g][0], kv_caches[g][1], win, dil)
        outs.append(o)
        lses.append(l)
    wts = jax.nn.softmax(jnp.stack(lses, 0), axis=0)
    o = jnp.einsum('gbth,gbthd->bthd', wts, jnp.stack(outs, 0).astype(jnp.float32)).astype(h.dtype)
    return o.reshape(bsz, t, D_GROUP_B) @ w_o


def trunk(x, c, a_states, kv_caches, p):
    c_all, n_all, m_all, conv_all = a_states
    cs, ns, ms, convs = [], [], [], []
    kv = None
    for layer in range(DEPTH):
        sh1, sc1, gt1, sh2, sc2, gt2 = ada_mod(c, p['w_ada'][layer], p['b_ada'][layer], 6)
        gn = p['g_norm'][layer]
        h = rmsnorm(x, gn[0]) * (1 + sc1) + sh1
        if layer < N_A_LAYERS:
            a = layer
            h, cf, nf, mf, cb = mlstm_mixer(h, conv_all[a], c_all[a], n_all[a], m_all[a],
                                            p['w_a_in'][a], p['b_a_gate'][a], p['w_a_conv'][a],
                                            p['b_a_conv'][a], p['w_a_out'][a])
            cs.append(cf)
            ns.append(nf)
            ms.append(mf)
            convs.append(cb)
        else:
            if kv is None:
                kv = shared_kv(x, c, p['g_kv'], p['w_ada_kv'], p['b_ada_kv'], p['w_kv'])
            bl = layer - N_A_LAYERS
            h = dilated_mixer(h, kv, kv_caches, p['w_b_q'][bl], p['w_b_o'][bl])
        x = x + gt1 * rmsnorm(h, gn[1])
        h = rmsnorm(x, gn[2]) * (1 + sc2) + sh2
        h = jnp.square(jax.nn.relu(h @ p['w_mlp_up'][layer])) @ p['w_mlp_down'][layer]
        x = x + gt2 * rmsnorm(h, gn[3])
    dt = x.dtype
    states = (jnp.stack(cs).astype(dt), jnp.stack(ns).astype(dt), jnp.stack(ms).astype(dt), jnp.stack(convs).astype(dt))
    return x, states, kv


def setup_inputs(seed: int = 0) -> dict:
    key = jax.random.key(seed)
    ks = jax.random.split(key, 40)
    f32 = jnp.float32

    def nrm(k, shape, s):
        return jax.random.normal(k, shape, f32) * s

    d = D_MODEL
    inp = {}
    inp['x_prompt'] = nrm(ks[0], (BATCH, SEQ, d), 1.0)
    inp['x_sample'] = nrm(ks[1], (DEC_BATCH, DEC_SEQ, d), 1.0)
    inp['c_prompt'] = nrm(ks[2], (BATCH, d), 1.0)
    inp['c_sample'] = nrm(ks[3], (DEC_BATCH, d), 1.0)
    inp['state_C'] = nrm(ks[4], (N_A_LAYERS, DEC_BATCH, H_A, DK_A, DV_A), 0.05)
    inp['state_n'] = nrm(ks[5], (N_A_LAYERS, DEC_BATCH, H_A, DK_A), 0.05)
    inp['state_m'] = nrm(ks[6], (N_A_LAYERS, DEC_BATCH, H_A), 1.0)
    inp['state_conv'] = nrm(ks[7], (N_A_LAYERS, DEC_BATCH, CONV_W - 1, 2 * D_INNER_A), 1.0)
    for g, (win, dil) in enumerate(GROUPS_B):
        wb = min(win, PAST_LEN)
        inp['cache_k_g%d' % g] = nrm(ks[8 + 2 * g], (DEC_BATCH, wb, H_G, HD_B), 1.0)
        inp['cache_v_g%d' % g] = nrm(ks[9 + 2 * g], (DEC_BATCH, wb, H_G, HD_B), 1.0)
    inp['w_ada'] = nrm(ks[14], (DEPTH, d, 6 * d), 0.5 * d ** -0.5)
    inp['b_ada'] = nrm(ks[15], (DEPTH, 6 * d), 0.02)
    inp['g_norm'] = 1.0 + nrm(ks[16], (DEPTH, 4, d), 0.02)
    inp['w_mlp_up'] = nrm(ks[17], (DEPTH, d, D_FF), d ** -0.5)
    inp['w_mlp_down'] = nrm(ks[18], (DEPTH, D_FF, d), D_FF ** -0.5)
    inp['w_a_in'] = nrm(ks[19], (N_A_LAYERS, d, 4 * D_INNER_A + 2 * H_A), d ** -0.5)
    b_i = nrm(ks[20], (N_A_LAYERS, H_A), 0.1)
    b_f = FORGET_BIAS + nrm(ks[21], (N_A_LAYERS, H_A), 0.1)
    inp['b_a_gate'] = jnp.concatenate([b_i, b_f], axis=-1)
    inp['w_a_conv'] = nrm(ks[22], (N_A_LAYERS, CONV_W, 2 * D_INNER_A), CONV_W ** -0.5)
    inp['b_a_conv'] = nrm(ks[23], (N_A_LAYERS, 2 * D_INNER_A), 0.02)
    inp['w_a_out'] = nrm(ks[24], (N_A_LAYERS, D_INNER_A, d), D_INNER_A ** -0.5)
    inp['g_kv'] = 1.0 + nrm(ks[25], (d,), 0.02)
    inp['w_ada_kv'] = nrm(ks[26], (d, 2 * d), 0.5 * d ** -0.5)
    inp['b_ada_kv'] = nrm(ks[27], (2 * d,), 0.02)
    inp['w_kv'] = nrm(ks[28], (d, 2 * N_GROUPS_B * D_GROUP_B), d ** -0.5)
    inp['w_b_q'] = nrm(ks[29], (N_B_LAYERS, d, N_GROUPS_B * D_GROUP_B), d ** -0.5)
    inp['w_b_o'] = nrm(ks[30], (N_B_LAYERS, D_GROUP_B, d), D_GROUP_B ** -0.5)
    return inp


def reference(x_prompt, x_sample, c_prompt, c_sample, state_C, state_n, state_m, state_conv,
              cache_k_g0, cache_v_g0, cache_k_g1, cache_v_g1, cache_k_g2, cache_v_g2,
              w_ada, b_ada, g_norm, w_mlp_up, w_mlp_down, w_a_in, b_a_gate, w_a_conv, b_a_conv,
              w_a_out, g_kv, w_ada_kv, b_ada_kv, w_kv, w_b_q, w_b_o):
    p = {'w_ada': w_ada, 'b_ada': b_ada, 'g_norm': g_norm, 'w_mlp_up': w_mlp_up,
         'w_mlp_down': w_mlp_down, 'w_a_in': w_a_in, 'b_a_gate': b_a_gate, 'w_a_conv': w_a_conv,
         'b_a_conv': b_a_conv, 'w_a_out': w_a_out, 'g_kv': g_kv, 'w_ada_kv': w_ada_kv,
         'b_ada_kv': b_ada_kv, 'w_kv': w_kv, 'w_b_q': w_b_q, 'w_b_o': w_b_o}
    bp, t = x_prompt.shape[0], x_prompt.shape[1]
    f32 = jnp.float32
    init = (jnp.zeros((N_A_LAYERS, bp, H_A, DK_A, DV_A), f32),
            jnp.zeros((N_A_LAYERS, bp, H_A, DK_A), f32),
            jnp.zeros((N_A_LAYERS, bp, H_A), f32),
            jnp.zeros((N_A_LAYERS, bp, CONV_W - 1, 2 * D_INNER_A), x_prompt.dtype))
    y_prompt, (p_C, p_n, p_m, p_conv), kv_p = trunk(x_prompt, c_prompt, init, None, p)
    caches = [(cache_k_g0, cache_v_g0), (cache_k_g1, cache_v_g1), (cache_k_g2, cache_v_g2)]
    y_sample, (s_C, s_n, s_m, s_conv), kv_s = trunk(x_sample, c_sample, (state_C, state_n, state_m, state_conv), caches, p)
    r0 = min(GROUPS_B[0][0], t)
    r1 = min(GROUPS_B[1][0], t)
    r2 = min(GROUPS_B[2][0], t)
    return (y_prompt, y_sample, p_C, p_n, p_m, p_conv, s_C, s_n, s_m, s_conv,
            kv_p[0][0][:, -r0:], kv_p[0][1][:, -r0:], kv_p[1][0][:, -r1:], kv_p[1][1][:, -r1:],
            kv_p[2][0][:, -r2:], kv_p[2][1][:, -r2:],
            kv_s[0][0], kv_s[0][1], kv_s[1][0], kv_s[1][1], kv_s[2][0], kv_s[2][1])
```

```python
import numpy as np
import concourse.bass as bass
import concourse.mybir as mybir
from concourse.bass_utils import run_bass_kernel_spmd

F32 = mybir.dt.float32
BF16 = mybir.dt.bfloat16
AF = mybir.ActivationFunctionType
ALU = mybir.AluOpType

D = 1024
KC = 8
NT = 256
DFF = 4096
EPS = 1e-6
NS = 4
GROUPS = ((128, 1), (512, 4), (2048, 16))
LN16 = float(np.log(16.0))


class Sch:
    def __init__(s, nc, dry=False):
        s.nc = nc
        s.dry = dry
        s.E = {'pe': nc.tensor, 'act': nc.scalar, 'dve': nc.vector, 'pool': nc.gpsimd, 'sp': nc.sync}
        s.sem = {e: nc.alloc_semaphore(name='sem_' + e) for e in s.E}
        s.cnt = {e: 0 for e in s.E}
        s.waited = {e: {} for e in s.E}
        s.lastw = {}
        s.readers = {}
        s.dsem = {}
        s.excl = set()

    @staticmethod
    def _k(r):
        if isinstance(r, (str, tuple)):
            return r
        if hasattr(r, 'tensor'):
            return r.tensor.name
        return r.name

    def _split(s, reads, writes):
        reads = [s._k(r) for r in reads]
        writes = [s._k(w) for w in writes]
        ex = [r for r in reads if r in s.excl and r not in writes]
        return [r for r in reads if r not in ex], writes + ex

    def _tokens(s, reads, writes):
        reads, writes = s._split(reads, writes)
        toks = []
        for r in reads:
            t = s.lastw.get(r)
            if t:
                toks.append(t + (True,))
        for w in writes:
            t = s.lastw.get(w)
            if t:
                toks.append(t + (False,))
            toks.extend(t_ + (False,) for t_ in s.readers.get(w, ()))
        return toks

    def _wait(s, e, toks):
        need = {}
        for (key, sem, val, is_raw) in toks:
            if key == e and e == 'pe':
                continue
            if key == e and e in ('act', 'dve') and not is_raw:
                continue
            if s.waited[e].get(key, 0) >= val:
                continue
            if need.get(key, (None, 0))[1] < val:
                need[key] = (sem, val)
        for key, (sem, val) in need.items():
            s.E[e].wait_ge(sem, val)
            s.waited[e][key] = val

    def _commit(s, tok, reads, writes):
        reads, writes = s._split(reads, writes)
        for w in writes:
            s.lastw[w] = tok
            s.readers[w] = []
        for r in reads:
            if r not in writes:
                lst = s.readers.setdefault(r, [])
                lst[:] = [t for t in lst if t[0] != tok[0]]
                lst.append(tok)

    def op(s, e, fn, reads=(), writes=()):
        if s.dry:
            return
        s._wait(e, s._tokens(reads, writes))
        inst = fn(s.E[e])
        s.cnt[e] += 1
        inst.then_inc(s.sem[e], 1)
        s._commit((e, s.sem[e], s.cnt[e]), reads, writes)

    def dma(s, q, out, in_, reads=(), writes=(), chan=None, **kw):
        if s.dry:
            return
        if q == 'sp' and type(out.tensor).__name__.startswith('DRam'):
            q = 'pool'
        s._wait(q, s._tokens(reads, writes))
        inst = s.E[q].dma_start(out, in_, **kw)
        chan = s._k(chan)
        if chan not in s.dsem:
            s.dsem[chan] = [s.nc.alloc_semaphore(name='dsem_%d' % len(s.dsem)), 0]
        ds = s.dsem[chan]
        ds[1] += 16
        inst.then_inc(ds[0], 16)
        s._commit((('dma', chan), ds[0], ds[1]), reads, writes)

    def finish(s):
        if s.dry:
            return
        for chan, (sem, val) in s.dsem.items():
            if val:
                s.E['sp'].wait_ge(sem, val)
        for e in ('pe', 'act', 'dve', 'pool'):
            if s.cnt[e]:
                s.E['sp'].wait_ge(s.sem[e], s.cnt[e])


def build(T, do_samples=True, wseq=None):
    assert T % NT == 0
    if wseq is None:
        wseq = build(T, do_samples, wseq=[])
    dry = (len(wseq) == 0)
    nc = bass.Bass("TRN2", target_bir_lowering=False)
    S = Sch(nc, dry=dry)
    wreq = []
    ntile = T // NT

    def din(name, shape):
        return nc.dram_tensor(name, list(shape), F32, kind="ExternalInput").ap()

    def dout(name, shape):
        return nc.dram_tensor(name, list(shape), F32, kind="ExternalOutput").ap()

    def dscr(name, shape, dt=F32):
        return nc.dram_tensor(name, list(shape), dt, kind="Internal").ap()

    xp = din("xp", [T, D])
    xs = din("xs", [NS, D])
    c5 = din("c5", [1 + NS, D])
    stC = din("stC", [NS, 4, 256, 256])
    stn = din("stn", [NS, 4, 256])
    stm = din("stm", [NS, 4])
    stconv = din("stconv", [NS, 3, 2048])
    cache = {}
    for g, (win, dil) in enumerate(GROUPS):
        cache[('k', g)] = din("ck%d" % g, [NS, win, D])
        cache[('v', g)] = din("cv%d" % g, [NS, win, D])
    w_ada = din("w_ada", [2, D, 6 * D])
    b_ada = din("b_ada", [2, 6 * D])
    g_norm = din("g_norm", [2, 4, D])
    w_up = din("w_mlp_up", [2, D, DFF])
    w_down = din("w_mlp_down", [2, DFF, D])
    w_in = din("w_a_in", [D, 4104])
    b_gate = din("b_a_gate", [8])
    wcb = din("wcb", [5, 2048])
    w_out = din("w_a_out", [D, D])
    g_kv = din("g_kv", [D])
    w_ada_kv = din("w_ada_kv", [D, 2 * D])
    b_ada_kv = din("b_ada_kv", [2 * D])
    w_kv = din("w_kv", [D, 6 * D])
    w_q = din("w_b_q", [D, 3 * D])
    w_o = din("w_b_o", [D, D])
    identd = din("ident", [128, 128])
    maskAd = din("maskA", [128, 128])
    maskBd = din("maskB", [128, 128])
    vtabd = din("vtab", [128, 129])

    yp = dout("yp", [T, D])
    ys = dout("ys", [NS, D])
    pC = dout("pC", [4, 256, 256])
    pn = dout("pn", [4, 256])
    pm = dout("pm", [4, 1])
    pconv = dout("pconv", [3, 2048])
    sCo = dout("sC", [NS, 4, 256, 256])
    sno = dout("sn", [NS, 4, 256])
    smo = dout("sm", [NS, 4])
    sconvo = dout("sconv", [NS, 3, 2048])
    pko, pvo, sko, svo = [], [], [], []
    for g, (win, dil) in enumerate(GROUPS):
        r = min(win, T)
        pko.append(dout("pk%d" % g, [r, D]))
        pvo.append(dout("pv%d" % g, [r, D]))
        sko.append(dout("sk%d" % g, [NS, D]))
        svo.append(dout("sv%d" % g, [NS, D]))

    modscr = dscr("modscr", [2, 1 + NS, 6 * D])
    modd = dscr("modd", [3, 3, 1 + NS, D])
    modkvscr = dscr("modkvscr", [1 + NS, 2 * D])
    kvscr = dscr("kvscr", [T, 6 * D], BF16)
    kvscr_s = dscr("kvscr_s", [NS, 6 * D], BF16)
    oscr = dscr("oscr", [3, T, 8 * 129])
    oscr_s = dscr("oscr_s", [3, NS, 8 * 129])

    _n = [0]

    def sb(shape, dt=F32, name=None):
        _n[0] += 1
        return nc.alloc_sbuf_tensor(name or ("t%d" % _n[0]), list(shape), dt)

    def ps(shape, dt=F32, name=None):
        _n[0] += 1
        t = nc.alloc_psum_tensor(name or ("p%d" % _n[0]), list(shape), dt)
        S.excl.add(t.name)
        return t

    pmix = [ps([128, 1024]), ps([128, 1024])]
    psAB = [ps([128, 512]), ps([128, 512])]
    pT = ps([128, 1024], BF16)
    pS = ps([128, 512])
    _ab = [0]

    def nextps():
        _ab[0] ^= 1
        return psAB[_ab[0]]

    ident_f = sb([128, 128])
    ident_b = sb([128, 128], BF16)
    maskA = sb([128, 128], BF16)
    maskB = sb([128, 128], BF16)
    vtab = sb([128, 129])
    ones_f = sb([128, 128])
    ones_b = sb([128, 128], BF16)
    NWS = 4
    wring = [sb([128, 8, 512], BF16, "wring%d" % i) for i in range(NWS)]
    xblk = [sb([128, D], F32, "xblk%d" % i) for i in range(2)]
    M0 = sb([128, D], F32, "modM0")
    M1 = sb([128, D], F32, "modM1")
    M2 = sb([128, D], F32, "modM2")
    mods = {"gm1": M0, "sh1": M1, "gg1": M2, "gm2": M0, "sh2": M1, "gg2": M2, "gmkv": M0, "shkv": M1}
    modtmp = M2
    hT = sb([128, 8, NT], BF16, "hT")

    tmpf = sb([128, D], F32, "tmpf")
    relut_v = tmpf[:, 0:2 * NT].rearrange("p (a b) -> p a b", b=NT)
    hn = sb([128, D], BF16, "hn")
    junk = hn
    colA = sb([128, 8], F32, "colA")
    qkpre = sb([128, 16, 3 + NT + 16], BF16, "qkpre")
    qkT = sb([128, 16, NT], BF16, "qkT")
    vext = [sb([128, 4, 257], BF16, "vext%d" % i) for i in range(2)]
    sigT = sb([128, 8, NT], BF16, "sigT")
    Cst = sb([128, 4, 2, 257], F32, "Cst")
    Cbf = sb([128, 4, 2, 257], BF16, "Cbf")
    Ctmp = None
    Kpt = [sb([128, 256], BF16, "Kp%d" % h_) for h_ in range(4)]
    PpTt = [sb([128, 128], BF16, "PpT%d" % h_) for h_ in range(4)]
    Kpv = [t_[:] for t_ in Kpt]
    PpTv = [t_[:] for t_ in PpTt]
    Kp = Kpt[0]
    hh = sb([128, D], BF16, "hh")
    hhm = [sb([128, D], BF16, "hhm%d" % i) for i in range(2)]
    xstage = sb([128, D], F32, "xstage")
    gatedT = sb([128, 8, 128], BF16, "gatedT")
    big = sb([128, 4096], F32, "big")
    hid = big[:].bitcast(BF16).rearrange("p (a b) -> p a b", b=NT)
    relut = None
    Wg = sb([128, 8, 8], BF16, "Wg")
    diagW = sb([128, 4, 16, 128], BF16, "diagW")
    wcT = sb([128, 16, 5], F32, "wcT")
    bg = sb([4, 2], F32, "bg")
    R = {k: sb([4, NT], F32, "row_" + k) for k in ("li", "fp", "lf", "B", "a", "g", "one")}
    R["wk"] = sb([4, 128], F32, "row_wk")
    R["e"] = sb([4, 128], F32, "row_e")
    R["t"] = R["fp"]
    rcol = {k: sb([4, 4], F32, "rcol_" + k) for k in ("Bprev", "gprev", "nb", "dec", "t", "m")}
    cols = sb([128, 16], F32, "cols")
    kvtoks = [sb([128, 512], F32, "kvtok%d" % i) for i in range(2)]
    kvbfs = [sb([128, 512], BF16, "kvbf%d" % i) for i in range(2)]
    kvtok = kvtoks[0]
    kvrot = {'t': 0, 'b': 0}
    QT = hid[:, 0:24, :]
    ASET = []
    for i_ in range(2):
        ASET.append(dict(
            Khist=sb([128, D], BF16, "Khist%d" % i_), Kcur=sb([128, D], BF16, "Kcur%d" % i_),
            Vh=sb([128, 8, 129], BF16, "Vh%d" % i_), Vc=sb([128, 8, 129], BF16, "Vc%d" % i_),
            KTh=sb([128, 8, 128], BF16, "KTh%d" % i_), KTc=sb([128, 8, 128], BF16, "KTc%d" % i_),
            Eh=sb([128, 8, 128], BF16, "Eh%d" % i_), Ec=sb([128, 8, 128], BF16, "Ec%d" % i_),
            Osb=sb([128, 8 * 129], F32, "Osb%d" % i_)))
    Khist = ASET[0]["Khist"]
    astep = {'i': 0}
    Og = big[:, 0:3096].rearrange("p (a b) -> p a b", b=1032)
    c5sb = modtmp
    wc5 = big[:, 0:2048]
    c5T = sb([128, 8, 8], BF16, "c5T")
    badab = M1[0:8, 0:512]
    modrow = M1[0:8, 512:1024]

    op = S.op
    dma = S.dma

    dma('sp', ident_f[:], identd, writes=[ident_f], chan=ident_f)
    dma('pool', ident_b[:], identd, writes=[ident_b], chan=ident_b)
    dma('pool', maskA[:], maskAd, writes=[maskA], chan=maskA)
    dma('pool', maskB[:], maskBd, writes=[maskB], chan=maskB)
    dma('sp', vtab[:], vtabd, writes=[vtab], chan=vtab)
    op('dve', lambda e: e.memset(ones_f[:], 1.0), writes=[ones_f])
    op('dve', lambda e: e.memset(ones_b[:], 1.0), writes=[ones_b])
    op('dve', lambda e: e.memset(qkpre[:], 0.0), writes=[qkpre])
    op('dve', lambda e: e.memset(Cst[:], 0.0), writes=[Cst])
    op('dve', lambda e: e.memset(Cbf[:], 0.0), writes=[Cbf])
    op('dve', lambda e: e.memset(R["one"][:], 1.0), writes=[R["one"]])
    for i in range(2):
        op('dve', lambda e, i=i: e.memset(vext[i][:], 1.0), writes=[vext[i]])
    for A_ in ASET:
        op('dve', lambda e, A_=A_: e.memset(A_["Vh"][:], 1.0), writes=[A_["Vh"]])
        op('dve', lambda e, A_=A_: e.memset(A_["Vc"][:], 1.0), writes=[A_["Vc"]])
        op('dve', lambda e, A_=A_: e.memset(A_["Khist"][:], 0.0), writes=[A_["Khist"]])
        op('dve', lambda e, A_=A_: e.memset(A_["Kcur"][:], 0.0), writes=[A_["Kcur"]])
    op('dve', lambda e: e.memset(rcol["Bprev"][:], 0.0), writes=[rcol["Bprev"]])
    op('dve', lambda e: e.memset(rcol["gprev"][:], 0.0), writes=[rcol["gprev"]])
    op('dve', lambda e: e.memset(c5T[:], 0.0), writes=[c5T])
    with nc.allow_non_contiguous_dma(reason="tiny"):
        dma('pool', Wg[:], w_in.rearrange("(kc p) n -> p kc n", p=128)[:, :, 4096:4104], writes=[Wg], chan=Wg)
        dma('sp', bg[:], b_gate.rearrange("(two h) -> h two", two=2), writes=[bg], chan=bg)
    dma('sp', wc5[0:5, 0:2048], wcb, writes=[wc5], chan=wc5)
    for fc in range(16):
        op('pe', lambda e, fc=fc: e.transpose(pS[:, fc * 5:fc * 5 + 5], wc5[0:5, fc * 128:(fc + 1) * 128], ident_f[0:5, 0:5]),
           reads=[wc5, ident_f], writes=[pS])
    op('act', lambda e: e.activation(wcT[:].rearrange("p a b -> p (a b)"), pS[:, 0:80], AF.Copy), reads=[pS], writes=[wcT])
    for j in range(4):
        for fc in range(16):
            op('dve', lambda e, j=j, fc=fc: e.tensor_scalar(diagW[:, j, fc, :], ident_f[:], wcT[:, fc, j:j + 1], None, ALU.mult),
               reads=[ident_f, wcT], writes=[diagW])

    wstate = {'i': 0}

    def wk(w2, c0, width=512):
        return w2.rearrange("(kc p) n -> p kc n", p=128)[:, :, c0:c0 + width]

    def wslab_raw(src3):
        i = wstate['i'] % NWS
        wstate['i'] += 1
        t = wring[i]
        w = src3.shape[2]
        dma('pool', t[:, :, 0:w], src3, writes=[t], chan=t)
        return t

    WSRC = {}
    WSRC['w_in'] = [wk(w_in, s_ * 512) for s_ in range(8)]
    WSRC['w_out'] = [wk(w_out, s_ * 512) for s_ in range(2)]
    for l_ in range(2):
        WSRC['up%d' % l_] = [wk(w_up[l_], s_ * 512) for s_ in range(8)]
        wd_ = w_down[l_].rearrange("(fg kc p) n -> fg p kc n", p=128, kc=8)
        WSRC['down%d' % l_] = [wd_[fg][:, :, half * 512:(half + 1) * 512] for half in range(2) for fg in range(4)]
    WSRC['w_kv'] = [wk(w_kv, s_ * 512) for s_ in range(12)]
    WSRC['w_q'] = [wk(w_q, s_ * 512) for s_ in range(6)]
    WSRC['w_o'] = [wk(w_o, s_ * 512) for s_ in range(2)]
    WSC = {}
    for name_ in ('w_in', 'w_out', 'up0', 'down0', 'w_kv', 'w_q', 'w_o', 'up1', 'down1'):
        WSC[name_] = dscr("wsc_" + name_, [len(WSRC[name_]), 128, 8, 512], BF16)

    def emit_conversions():
        for name_ in ('w_in', 'w_out', 'up0', 'down0', 'w_kv', 'w_q', 'w_o', 'up1', 'down1'):
            for s_, src in enumerate(WSRC[name_]):
                kcv = wstate['i']
                wstate['i'] += 1
                dma('pool', WSC[name_][s_], src, writes=[('dram', 'wsc', name_, s_), ('cvt', kcv % 3)], chan=('cvt', kcv % 3))

    AHEAD = 2
    wissued = {'n': 0}

    def wslab(name, s_):
        k = len(wreq)
        wreq.append((name, s_))
        if dry:
            return wring[0]
        assert wseq[k] == (name, s_), (k, wseq[k], name, s_)
        while wissued['n'] < min(len(wseq), k + 1 + AHEAD):
            j = wissued['n']
            nm, sj = wseq[j]
            tj = wring[j % NWS]
            dma('sp', tj[:], WSC[nm][sj], reads=[('dram', 'wsc', nm, sj)], writes=[tj], chan=tj)
            wissued['n'] += 1
        return wring[k % NWS]

    dma('sp', c5sb[0:5, :], c5, writes=[c5sb], chan=c5sb)
    op('act', lambda e: e.activation(tmpf[0:5, :], c5sb[0:5, :], AF.Sigmoid), reads=[c5sb], writes=[tmpf])
    op('dve', lambda e: e.tensor_tensor(tmpf[0:5, :], tmpf[0:5, :], c5sb[0:5, :], ALU.mult), reads=[tmpf, c5sb], writes=[tmpf])
    for kc in range(8):
        op('pe', lambda e, kc=kc: e.transpose(pS[:, kc * 8:kc * 8 + 5], tmpf[0:5, kc * 128:(kc + 1) * 128], ident_f[0:5, 0:5]),
           reads=[tmpf, ident_f], writes=[pS])
    op('act', lambda e: e.activation(c5T[:, :, 0:5], pS[:, 0:64].rearrange("p (a b) -> p a b", b=8)[:, :, 0:5], AF.Copy),
       reads=[pS], writes=[c5T])

    ada_jobs = []
    for (wmat, bvec, ncols, dst) in ((w_ada[0], b_ada[0], 6 * D, modscr[0]), (w_ada[1], b_ada[1], 6 * D, modscr[1]),
                                     (w_ada_kv, b_ada_kv, 2 * D, modkvscr)):
        for c0 in range(0, ncols, 512):
            ada_jobs.append((wmat, bvec, c0, dst))
    ada_bufs = [(M1[0:8, 0:512], M1[0:8, 512:1024], M1), (M2[0:8, 0:512], M2[0:8, 512:1024], M2)]
    ada_issued = {'n': 0}

    def ada_issue(upto):
        while ada_issued['n'] < min(len(ada_jobs), upto):
            j = ada_issued['n']
            wmat, bvec, c0, dst = ada_jobs[j]
            tj = wring[j % NWS]
            dma('pool', tj[:], wk(wmat, c0), writes=[tj], chan=tj)
            ada_issued['n'] += 1

    for j, (wmat, bvec, c0, dst) in enumerate(ada_jobs):
        ada_issue(j + 3)
        wt = wring[j % NWS]
        bb_, mr_, key_ = ada_bufs[j % 2]
        dma('sp', bb_[0:5, :], bvec[c0:c0 + 512].partition_broadcast(5), writes=[key_], chan=key_)
        p = nextps()
        for kc in range(8):
            op('pe', lambda e, kc=kc, p=p, wt=wt: e.matmul(p[0:5, :], c5T[:, kc, 0:5], wt[:, kc, :], start=(kc == 0), stop=(kc == 7)),
               reads=[c5T, wt], writes=[p])
        op('dve', lambda e, p=p, bb_=bb_, mr_=mr_: e.tensor_tensor(mr_[0:5, :], p[0:5, :], bb_[0:5, :], ALU.add), reads=[p, key_], writes=[key_])
        dma('sp', dst[:, c0:c0 + 512], mr_[0:5, :], reads=[key_], writes=[('dram', dst.tensor.name)], chan=key_)

    def derive(dst3, scr, base, g_pre, g_post):
        rd = [('dram', scr.tensor.name)]
        wr = [('dram', 'modd')]
        P5 = 1 + NS
        dma('sp', tmpf[0:P5, :], scr[:, base + D:base + 2 * D], reads=rd, writes=[tmpf], chan=tmpf)
        dma('sp', M0[0:P5, :], g_pre.partition_broadcast(P5), writes=[M0], chan=M0)
        op('dve', lambda e: e.scalar_tensor_tensor(tmpf[0:P5, :], tmpf[0:P5, :], 1.0, M0[0:P5, :], ALU.add, ALU.mult), reads=[tmpf, M0], writes=[tmpf])
        dma('sp', dst3[0], tmpf[0:P5, :], reads=[tmpf], writes=wr, chan=tmpf)
        dma('sp', dst3[1], scr[:, base:base + D], reads=rd, writes=wr, chan='modd_copy')
        if g_post is not None:
            dma('sp', tmpf[0:P5, :], scr[:, base + 2 * D:base + 3 * D], reads=rd, writes=[tmpf], chan=tmpf)
            dma('sp', M0[0:P5, :], g_post.partition_broadcast(P5), writes=[M0], chan=M0)
            op('dve', lambda e: e.tensor_tensor(tmpf[0:P5, :], tmpf[0:P5, :], M0[0:P5, :], ALU.mult), reads=[tmpf, M0], writes=[tmpf])
            dma('sp', dst3[2], tmpf[0:P5, :], reads=[tmpf], writes=wr, chan=tmpf)

    moddm = dscr("moddm", [2, 2, 3, 1 + NS, D])
    for l_ in range(2):
        derive(moddm[l_, 0], modscr[l_], 0, g_norm[l_, 0], g_norm[l_, 1])
        derive(moddm[l_, 1], modscr[l_], 3 * D, g_norm[l_, 2], g_norm[l_, 3])
    derive(modd[2], modkvscr, 0, g_kv, None)
    emit_conversions()

    def _modsrc(layer, phase):
        if phase == 'kv':
            return modd[2]
        return moddm[layer, 0 if phase == 'mix' else 1]

    def _mrow(src2, sample):
        if sample:
            return src2[1:1 + NS, :], NS
        return src2[0, :].partition_broadcast(128), 128

    def load_gs(layer, phase, sample):
        src = _modsrc(layer, phase)
        for i_, t_ in ((0, M0), (1, M1)):
            a, P = _mrow(src[i_], sample)
            dma('sp', t_[0:P, :], a, reads=[('dram', 'modd')], writes=[t_], chan=t_)

    def load_gg(layer, phase, sample):
        a, P = _mrow(_modsrc(layer, phase)[2], sample)
        dma('sp', M2[0:P, :], a, reads=[('dram', 'modd')], writes=[M2], chan=M2)

    def rstd_of(src, P, col):
        op('act', lambda e: e.activation(junk[0:P, :], src, AF.Square, accum_out=colA[0:P, col:col + 1]),
           reads=[src.tensor.name], writes=[junk, ('colA', col)])
        op('act', lambda e: e.activation(colA[0:P, col + 1:col + 2], colA[0:P, col:col + 1], AF.Ln, bias=EPSC[0:P, :], scale=1.0 / D),
           reads=[('colA', col)], writes=[('colA', col + 1)])
        op('act', lambda e: e.activation(colA[0:P, col + 1:col + 2], colA[0:P, col + 1:col + 2], AF.Exp, scale=-0.5),
           reads=[('colA', col + 1)], writes=[('colA', col + 1)])
        return colA[0:P, col + 1:col + 2], ('colA', col + 1)

    EPSC = sb([128, 1], F32, "epsc")
    op('dve', lambda e: e.memset(EPSC[:], EPS), writes=[EPSC])

    def norm_mod_T(xb, P, gm, sh, c0):
        rs, rk = rstd_of(xb[0:P, :], P, 0)
        op('dve', lambda e: e.scalar_tensor_tensor(tmpf[0:P, :], xb[0:P, :], rs, mods[gm][0:P, :], ALU.mult, ALU.mult),
           reads=[xb.name, rk, mods[gm]], writes=[tmpf])
        op('dve', lambda e: e.tensor_tensor(hn[0:P, :], tmpf[0:P, :], mods[sh][0:P, :], ALU.add),
           reads=[tmpf, mods[sh]], writes=[hn])
        for kc in range(8):
            op('pe', lambda e, kc=kc: e.transpose(pT[:, kc * 128:kc * 128 + P], hn[0:P, kc * 128:(kc + 1) * 128], ident_b[0:P, 0:P]),
               reads=[hn, ident_b], writes=[pT])
        op('act', lambda e: e.activation(hT[:, :, c0:c0 + P], pT[:].rearrange("p (a b) -> p a b", b=128)[:, :, 0:P], AF.Copy),
           reads=[pT], writes=[hT])

    def post_norm_res(pm_, xb, P, gg):
        rs, rk = rstd_of(pm_[0:P, :], P, 2)
        op('dve', lambda e: e.scalar_tensor_tensor(tmpf[0:P, :], pm_[0:P, :], rs, mods[gg][0:P, :], ALU.mult, ALU.mult),
           reads=[pm_.name, rk, mods[gg]], writes=[tmpf])
        op('dve', lambda e: e.tensor_tensor(xb[0:P, :], xb[0:P, :], tmpf[0:P, :], ALU.add),
           reads=[tmpf, xb.name], writes=[xb.name])

    def proj_tok(lhsT3, blocks, wmat, c0s, pm_list, half_of):
        for c0 in c0s:
            wt = wslab_raw(wk(wmat, c0))
            hf = half_of(c0)
            for b, (col0, P) in enumerate(blocks):
                for kc in range(8):
                    op('pe', lambda e, kc=kc, b=b, col0=col0, P=P, wt=wt, hf=hf: e.matmul(
                        pm_list[b][0:P, hf * 512:(hf + 1) * 512], lhsT3[:, kc, col0:col0 + P], wt[:, kc, :], start=(kc == 0), stop=(kc == 7)),
                        reads=[lhsT3.name, wt], writes=[pm_list[b].name])

    def mlp(layer, blocks, N, after_norm=None):
        for b, (col0, P) in enumerate(blocks):
            norm_mod_T(xblk[b], P, "gm2", "sh2", col0)
        if after_norm:
            after_norm()
        for s in range(8):
            wt = wslab('up%d' % layer, s)
            for half in range(2):
                p = nextps()
                for q in range(2):
                    cc = half * 2 + q
                    for kc in range(8):
                        op('pe', lambda e, kc=kc, cc=cc, q=q, p=p, wt=wt: e.matmul(
                            p[:, q * NT:q * NT + N], wt[:, kc, cc * 128:(cc + 1) * 128], hT[:, kc, 0:N], start=(kc == 0), stop=(kc == 7)),
                            reads=[hT, wt], writes=[p])
                fc = s * 4 + half * 2
                pin = p[:].rearrange("p (a b) -> p a b", b=NT)[:, :, 0:N]
                op('act', lambda e, pin=pin: e.activation(relut_v[:, :, 0:N], pin, AF.Relu), reads=[p], writes=[tmpf])
                op('dve', lambda e, fc=fc, pin=pin: e.tensor_tensor(hid[:, fc:fc + 2, 0:N], relut_v[:, :, 0:N], pin, ALU.mult),
                   reads=[p, tmpf], writes=[hid])
        wd = w_down[layer].rearrange("(fg kc p) n -> fg p kc n", p=128, kc=8)
        for half in range(2):
            for fg in range(4):
                wt = wslab('down%d' % layer, half * 4 + fg)
                for b, (col0, P) in enumerate(blocks):
                    for kc in range(8):
                        op('pe', lambda e, kc=kc, b=b, col0=col0, P=P, wt=wt, fg=fg, half=half: e.matmul(
                            pmix[b][0:P, half * 512:(half + 1) * 512], hid[:, fg * 8 + kc, col0:col0 + P], wt[:, kc, :],
                            start=(fg == 0 and kc == 0), stop=(fg == 3 and kc == 7)),
                            reads=[hid, wt], writes=[pmix[b].name])
        for b, (col0, P) in enumerate(blocks):
            post_norm_res(pmix[b], xblk[b], P, "gg2")

    def w_in_proj(N, blocks, sample, parts=('qk', 'v', 'o', 'g')):
        qstep = 4 if sample else 1
        for s in (range(4) if 'qk' in parts else ()):
            wt = wslab('w_in', s)
            for half in range(2):
                p = nextps()
                for q in range(2):
                    cc = half * 2 + q
                    for kc in range(8):
                        op('pe', lambda e, kc=kc, cc=cc, q=q, p=p, wt=wt: e.matmul(
                            p[:, q * NT:q * NT + N], wt[:, kc, cc * 128:(cc + 1) * 128], hT[:, kc, 0:N], start=(kc == 0), stop=(kc == 7)),
                            reads=[hT, wt], writes=[p])
                fc = s * 4 + half * 2
                pin = p[:].rearrange("p (a b) -> p a b", b=NT)[:, :, 0:N]
                if sample:
                    dst = qkpre[:, fc:fc + 2, 3:3 + 4 * N:4]
                else:
                    dst = qkpre[:, fc:fc + 2, 3:3 + N]
                op('act', lambda e, dst=dst, pin=pin: e.activation(dst, pin, AF.Copy), reads=[p], writes=[qkpre])
        for s in (range(2) if 'v' in parts else ()):
            wt = wslab('w_in', 4 + s)
            for b, (col0, P) in enumerate(blocks):
                p = nextps()
                for kc in range(8):
                    op('pe', lambda e, kc=kc, col0=col0, P=P, p=p, wt=wt: e.matmul(
                        p[0:P, :], hT[:, kc, col0:col0 + P], wt[:, kc, :], start=(kc == 0), stop=(kc == 7)),
                        reads=[hT, wt], writes=[p])
                op('act', lambda e, b=b, P=P, p=p, s=s: e.activation(
                    vext[b][0:P, 2 * s:2 * s + 2, 0:256], p[0:P, :].rearrange("p (a b) -> p a b", b=256), AF.Copy),
                    reads=[p], writes=[vext[b]])
        for s in (range(2) if 'o' in parts else ()):
            wt = wslab('w_in', 6 + s)
            for half in range(2):
                p = nextps()
                for q in range(2):
                    cc = half * 2 + q
                    for kc in range(8):
                        op('pe', lambda e, kc=kc, cc=cc, q=q, p=p, wt=wt: e.matmul(
                            p[:, q * NT:q * NT + N], wt[:, kc, cc * 128:(cc + 1) * 128], hT[:, kc, 0:N], start=(kc == 0), stop=(kc == 7)),
                            reads=[hT, wt], writes=[p])
                fc = s * 4 + half * 2
                pin = p[:].rearrange("p (a b) -> p a b", b=NT)[:, :, 0:N]
                op('act', lambda e, fc=fc, pin=pin: e.activation(sigT[:, fc:fc + 2, 0:N], pin, AF.Sigmoid), reads=[p], writes=[sigT])
        if 'g' not in parts:
            return
        for gi, nm in ((0, "li"), (1, "fp")):
            for kc in range(8):
                op('pe', lambda e, kc=kc, gi=gi: e.matmul(pS[0:4, gi * NT:gi * NT + N], Wg[:, kc, gi * 4:gi * 4 + 4], hT[:, kc, 0:N],
                                                          start=(kc == 0), stop=(kc == 7)), reads=[Wg, hT], writes=[pS])
        op('dve', lambda e: e.tensor_scalar(R["li"][:, 0:N], pS[0:4, 0:N], bg[:, 0:1], None, ALU.add), reads=[pS, bg], writes=[R["li"]])
        op('dve', lambda e: e.tensor_scalar(R["fp"][:, 0:N], pS[0:4, NT:NT + N], bg[:, 1:2], -1.0, ALU.add, ALU.mult), reads=[pS, bg], writes=[R["fp"]])
        op('act', lambda e: e.activation(R["t"][:, 0:N], R["fp"][:, 0:N], AF.Exp), reads=[R["fp"]], writes=[R["t"]])
        op('act', lambda e: e.activation(R["lf"][:, 0:N], R["t"][:, 0:N], AF.Ln, bias=ONE4[:, :], scale=1.0), reads=[R["t"]], writes=[R["lf"]])
        op('dve', lambda e: e.tensor_scalar(R["lf"][:, 0:N], R["lf"][:, 0:N], -1.0, None, ALU.mult), reads=[R["lf"]], writes=[R["lf"]])

    ONE4 = sb([4, 1], F32, "one4")
    op('dve', lambda e: e.memset(ONE4[:], 1.0), writes=[ONE4])

    def conv_silu(N, sample):
        step = 4 if sample else 1
        for fc in range(16):
            if fc % 2 == 0:
                p = nextps()
            q = fc % 2
            for j in range(4):
                if sample:
                    rhs = qkpre[:, fc, j:j + 4 * N:4]
                else:
                    rhs = qkpre[:, fc, j:j + N]
                op('pe', lambda e, j=j, fc=fc, q=q, p=p, rhs=rhs: e.matmul(p[:, q * NT:q * NT + N], diagW[:, j, fc, :], rhs, start=(j == 0), stop=(j == 3)),
                   reads=[diagW, qkpre], writes=[p])
            op('act', lambda e, fc=fc, q=q, p=p: e.activation(qkT[:, fc, 0:N], p[:, q * NT:q * NT + N], AF.Silu, bias=wcT[:, fc, 4:5], scale=1.0),
               reads=[p, wcT], writes=[qkT])

    def gate_rows_prompt(N):
        op('dve', lambda e: e.tensor_tensor_scan(R["B"][:, 0:N], R["one"][:, 0:N], R["lf"][:, 0:N], rcol["Bprev"][:, 0:1], ALU.mult, ALU.add),
           reads=[R["one"], R["lf"], rcol["Bprev"]], writes=[R["B"]])
        op('dve', lambda e: e.tensor_tensor(R["a"][:, 0:N], R["li"][:, 0:N], R["B"][:, 0:N], ALU.subtract), reads=[R["li"], R["B"]], writes=[R["a"]])
        op('dve', lambda e: e.tensor_tensor_scan(R["g"][:, 0:N], R["a"][:, 0:N], R["a"][:, 0:N], rcol["gprev"][:, 0:1], ALU.max, ALU.max),
           reads=[R["a"], rcol["gprev"]], writes=[R["g"]])

    def mlstm_chunk(b, c0, first_of_tile):
        if first_of_tile:
            gp = rcol["gprev"][:, 0:1]
            gpk = rcol["gprev"]
        else:
            gp = R["g"][:, c0 - 1:c0]
            gpk = R["g"]
        op('dve', lambda e: e.tensor_scalar(rcol["nb"][:, 0:1], gp, -1.0, -LN16, ALU.mult, ALU.add), reads=[gpk], writes=[rcol["nb"]])
        op('dve', lambda e: e.tensor_scalar(rcol["nb"][:, 1:2], gp, -1.0, None, ALU.mult), reads=[gpk], writes=[rcol["nb"]])
        op('act', lambda e: e.activation(R["wk"][:, 0:128], R["a"][:, c0:c0 + 128], AF.Exp, bias=rcol["nb"][:, 0:1], scale=1.0),
           reads=[R["a"], rcol["nb"]], writes=[R["wk"]])
        op('act', lambda e: e.activation(R["e"][:, 0:128], R["B"][:, c0:c0 + 128], AF.Exp, bias=rcol["nb"][:, 1:2], scale=-1.0),
           reads=[R["B"], rcol["nb"]], writes=[R["e"]])
        op('dve', lambda e: e.tensor_tensor(rcol["dec"][:, 0:1], gp, R["g"][:, c0 + 127:c0 + 128], ALU.subtract), reads=[gpk, R["g"]], writes=[rcol["dec"]])
        op('act', lambda e: e.activation(rcol["dec"][:, 0:1], rcol["dec"][:, 0:1], AF.Exp), reads=[rcol["dec"]], writes=[rcol["dec"]])
        op('pe', lambda e: e.transpose(pS[:, 0:4], R["wk"][:, 0:128], ident_f[0:4, 0:4]), reads=[R["wk"], ident_f], writes=[pS])
        op('pe', lambda e: e.transpose(pS[:, 4:8], R["e"][:, 0:128], ident_f[0:4, 0:4]), reads=[R["e"], ident_f], writes=[pS])
        op('dve', lambda e: e.tensor_scalar(rcol["t"][:, 0:4], ident_f[0:4, 0:4], rcol["dec"][:, 0:1], None, ALU.mult), reads=[ident_f, rcol["dec"]], writes=[rcol["t"]])
        op('pe', lambda e: e.matmul(pS[:, 8:12], ones_f[0:4, :], rcol["t"][:, 0:4], start=True, stop=True), reads=[ones_f, rcol["t"]], writes=[pS])
        op('act', lambda e: e.activation(cols[:, 0:12], pS[:, 0:12], AF.Copy), reads=[pS], writes=[cols])
        fq = lambda h, kc: 2 * h + kc
        fk = lambda h, kc: 8 + 2 * h + kc
        pst = nextps()
        for h in range(4):
            for kc in range(2):
                op('pe', lambda e, kc=kc, h=h: e.matmul(pst[:, h * 128:(h + 1) * 128], qkT[:, fk(h, kc), c0:c0 + 128], qkT[:, fq(h, kc), c0:c0 + 128], start=(kc == 0), stop=(kc == 1)),
                   reads=[qkT], writes=[pst])
        for h in range(4):
            op('dve', lambda e, h=h: e.scalar_tensor_tensor(PpTv[h], pst[:, h * 128:(h + 1) * 128], cols[:, h:h + 1], maskB[:], ALU.mult, ALU.mult),
               reads=[pst, cols, maskB], writes=[PpTt[h]])
        for h in range(4):
            for kc in range(2):
                op('pe', lambda e, kc=kc, h=h: e.transpose(pT[:, h * 256 + kc * 128:h * 256 + (kc + 1) * 128], qkT[:, fk(h, kc), c0:c0 + 128], ident_b[:]), reads=[qkT, ident_b], writes=[pT])
        for h in range(4):
            op('act', lambda e, h=h: e.activation(Kpv[h], pT[:, h * 256:(h + 1) * 256], AF.Identity, scale=cols[:, h:h + 1]), reads=[pT, cols], writes=[Kpt[h]])
        for h in range(4):
            op('pool', lambda e, h=h: e.tensor_scalar(Cst[:, h, :, :], Cst[:, h, :, :], cols[:, 8 + h:9 + h], 0.0, ALU.mult, ALU.add), reads=[Cst, cols], writes=[Cst])
        nb_ = [(pmix[0], 0), (pmix[0], 512), (pmix[1], 0), (pmix[1], 512)]
        for h in range(4):
            pm_, cb = nb_[h]
            for kc in range(2):
                op('pe', lambda e, kc=kc, h=h, pm_=pm_, cb=cb: e.matmul(pm_[:, cb:cb + 257], qkT[:, fq(h, kc), c0:c0 + 128], Cbf[:, h, kc, :], start=(kc == 0), stop=False),
                   reads=[qkT, Cbf], writes=[pm_.name])
            op('pe', lambda e, h=h, pm_=pm_, cb=cb: e.matmul(pm_[:, cb:cb + 257], PpTv[h], vext[b][:, h, :], start=False, stop=True), reads=[PpTt[h], vext[b]], writes=[pm_.name])
        for h in range(4):
            pm_, cb = nb_[h]
            op('act', lambda e, h=h, pm_=pm_, cb=cb: e.activation(cols[:, 12 + h:13 + h], pm_[:, cb + 256:cb + 257], AF.Abs), reads=[pm_.name], writes=[('c12', h)])
        op('dve', lambda e: e.tensor_tensor(cols[:, 12:16], cols[:, 12:16], cols[:, 4:8], ALU.max), reads=[('c12', 0), ('c12', 1), ('c12', 2), ('c12', 3), cols], writes=[('c12', 0), ('c12', 1), ('c12', 2), ('c12', 3)])
        op('dve', lambda e: e.reciprocal(cols[:, 12:16], cols[:, 12:16]), reads=[('c12', 0), ('c12', 1), ('c12', 2), ('c12', 3)], writes=[('c12', 0), ('c12', 1), ('c12', 2), ('c12', 3)])
        for h in range(4):
            pm_, cb = nb_[h]
            op('act', lambda e, h=h, pm_=pm_, cb=cb: e.activation(hhm[b][:, h * 256:(h + 1) * 256], pm_[:, cb:cb + 256], AF.Identity, scale=cols[:, 12 + h:13 + h]),
               reads=[pm_.name, ('c12', h)], writes=[hhm[b]])
        for h in range(4):
            for kc in range(2):
                pu = nextps()
                op('pe', lambda e, kc=kc, pu=pu, h=h: e.matmul(pu[:, 0:257], Kpv[h][:, kc * 128:(kc + 1) * 128], vext[b][:, h, :], start=True, stop=True),
                   reads=[Kpt[h], vext[b]], writes=[pu])
                op('dve', lambda e, kc=kc, pu=pu, h=h: e.scalar_tensor_tensor(Cst[:, h, kc, :], pu[:, 0:257], cols[:, 8 + h:9 + h], Cst[:, h, kc, :], ALU.mult, ALU.add),
                   reads=[pu, Cst, cols], writes=[Cst])
        for h in range(4):
            op('pool', lambda e, h=h: e.tensor_copy(Cbf[:, h, :, :], Cst[:, h, :, :]), reads=[Cst], writes=[Cbf])

    def gated_T(b, c0, P):
        hsrc = hhm[b]
        for kc in range(8):
            op('pe', lambda e, kc=kc: e.transpose(pT[:, kc * 128:kc * 128 + P], hsrc[0:P, kc * 128:(kc + 1) * 128], ident_b[0:P, 0:P]), reads=[hsrc, ident_b], writes=[pT])
        op('dve', lambda e: e.tensor_tensor(gatedT[:, :, 0:P], pT[:].rearrange("p (a b) -> p a b", b=128)[:, :, 0:P], sigT[:, :, c0:c0 + P], ALU.mult),
           reads=[pT, sigT], writes=[gatedT])

    def wout_block(b, P, wts):
        for half in range(2):
            for kc in range(8):
                op('pe', lambda e, kc=kc, half=half: e.matmul(pmix[b][0:P, half * 512:(half + 1) * 512], gatedT[:, kc, 0:P], wts[half][:, kc, :], start=(kc == 0), stop=(kc == 7)),
                   reads=[gatedT, wts[half]], writes=[pmix[b].name])

    def kv_proj(blocks, tok0, sample, after_norm=None):
        for b, (col0, P) in enumerate(blocks):
            norm_mod_T(xblk[b], P, "gmkv", "shkv", col0)
        if after_norm:
            after_norm()
        for s in range(12):
            wt = wslab('w_kv', s)
            for b, (col0, P) in enumerate(blocks):
                p = nextps()
                for kc in range(8):
                    op('pe', lambda e, kc=kc, col0=col0, P=P, p=p, wt=wt: e.matmul(p[0:P, :], hT[:, kc, col0:col0 + P], wt[:, kc, :], start=(kc == 0), stop=(kc == 7)),
                       reads=[hT, wt], writes=[p])
                kvrot['b'] += 1
                kvbf = kvbfs[kvrot['b'] % 2]
                op('act', lambda e, P=P, p=p, kvbf=kvbf: e.activation(kvbf[0:P, :], p[0:P, :], AF.Copy), reads=[p], writes=[kvbf])
                kvsel, g, hf = s // 6, (s % 6) // 2, s % 2
                kvrot['t'] += 1
                kvtok = kvtoks[kvrot['t'] % 2]
                if sample:
                    dma('sp', kvscr_s[:, s * 512:(s + 1) * 512], kvbf[0:P, :], reads=[kvbf], writes=[('dram', 'kvscr_s')], chan=kvbf)
                    dst = (sko if kvsel == 0 else svo)[g]
                    op('dve', lambda e, P=P, p=p, kvtok=kvtok: e.tensor_copy(kvtok[0:P, :], p[0:P, :]), reads=[p], writes=[kvtok])
                    dma('sp', dst[:, hf * 512:(hf + 1) * 512], kvtok[0:P, :], reads=[kvtok], chan=kvtok)
                else:
                    t0 = tok0 + col0
                    dma('sp', kvscr[t0:t0 + P, s * 512:(s + 1) * 512], kvbf[0:P, :], reads=[kvbf], writes=[('dram', 'kvscr', t0 // 128)], chan=kvbf)
                    r = min(GROUPS[g][0], T)
                    if t0 >= T - r:
                        dst = (pko if kvsel == 0 else pvo)[g]
                        op('dve', lambda e, P=P, p=p, kvtok=kvtok: e.tensor_copy(kvtok[0:P, :], p[0:P, :]), reads=[p], writes=[kvtok])
                        dma('sp', dst[t0 - (T - r):t0 - (T - r) + P, hf * 512:(hf + 1) * 512], kvtok[0:P, :], reads=[kvtok], chan=kvtok)

    SCALE = float(128 ** -0.5)

    def attn_load(st):
        A_ = st['A']
        nq, nh, rd, hq = st['nq'], st['nh'], st['rd'], st.get('hq', 'sp')
        if nh > 0:
            dma(hq, A_["Khist"][128 - nh:128, :], st['khist'], reads=rd, writes=[A_["Khist"]], chan=A_["Khist"])
            dma(hq, A_["Vh"][128 - nh:128, :, 0:128], st['vhist'].rearrange("n (h d) -> n h d", d=128), reads=rd, writes=[A_["Vh"]], chan=A_["Vh"])
        dma('sp', A_["Kcur"][0:nq, :], st['kcur'], reads=rd, writes=[A_["Kcur"]], chan=A_["Kcur"])
        dma('sp', A_["Vc"][0:nq, :, 0:128], st['vcur'].rearrange("n (h d) -> n h d", d=128), reads=rd, writes=[A_["Vc"]], chan=A_["Vc"])

    def attn_compute(st):
        A_ = st['A']
        qT3, nq, nh, odst, wr = st['q3'], st['nq'], st['nh'], st['odst'], st['wr']
        hb = 4 if nq > 64 else 8
        Khist, Kcur, Vh, Vc, KTh, KTc, Eh, Ec, Osb = (A_[k_] for k_ in ("Khist", "Kcur", "Vh", "Vc", "KTh", "KTc", "Eh", "Ec", "Osb"))
        if nh > 0:
            for h in range(8):
                op('pe', lambda e, h=h: e.transpose(pT[:, h * 128:(h + 1) * 128], Khist[:, h * 128:(h + 1) * 128], ident_b[:]), reads=[Khist, ident_b], writes=[pT])
            op('dve', lambda e: e.tensor_copy(KTh[:].rearrange("p a b -> p (a b)"), pT[:]), reads=[pT], writes=[KTh])
        for h in range(8):
            op('pe', lambda e, h=h: e.transpose(pT[:, h * 128:h * 128 + nq], Kcur[0:nq, h * 128:(h + 1) * 128], ident_b[0:nq, 0:nq]), reads=[Kcur, ident_b], writes=[pT])
        op('act', lambda e: e.activation(KTc[:, :, 0:nq], pT[:].rearrange("p (a b) -> p a b", b=128)[:, :, 0:nq], AF.Copy), reads=[pT], writes=[KTc])
        for h0 in range(0, 8, hb):
            if nh > 0:
                p = nextps()
                for h in range(h0, h0 + hb):
                    op('pe', lambda e, h=h, p=p: e.matmul(p[:, (h - h0) * nq:(h - h0 + 1) * nq], KTh[:, h, :], qT3[:, h, :], start=True, stop=True),
                       reads=[KTh, qT3.tensor.name], writes=[p])
                op('act', lambda e, p=p: e.activation(Eh[:, h0:h0 + hb, 0:nq], p[:, 0:hb * nq].rearrange("p (a b) -> p a b", b=nq), AF.Exp, scale=SCALE),
                   reads=[p], writes=[Eh])
                op('dve', lambda e: e.scalar_tensor_tensor(Eh[:, h0:h0 + hb, 0:nq], Eh[:, h0:h0 + hb, 0:nq], vtab[:, nh:nh + 1],
                                                            maskA[:, 0:nq].unsqueeze(1).broadcast_to([128, hb, nq]), ALU.mult, ALU.mult),
                   reads=[Eh, vtab, maskA], writes=[Eh])
            p = nextps()
            for h in range(h0, h0 + hb):
                op('pe', lambda e, h=h, p=p: e.matmul(p[0:nq, (h - h0) * nq:(h - h0 + 1) * nq], KTc[:, h, 0:nq], qT3[:, h, :], start=True, stop=True),
                   reads=[KTc, qT3.tensor.name], writes=[p])
            op('act', lambda e, p=p: e.activation(Ec[0:nq, h0:h0 + hb, 0:nq], p[0:nq, 0:hb * nq].rearrange("p (a b) -> p a b", b=nq), AF.Exp, scale=SCALE),
               reads=[p], writes=[Ec])
            op('dve', lambda e: e.tensor_tensor(Ec[0:nq, h0:h0 + hb, 0:nq], Ec[0:nq, h0:h0 + hb, 0:nq],
                                                maskB[0:nq, 0:nq].unsqueeze(1).broadcast_to([nq, hb, nq]), ALU.mult),
               reads=[Ec, maskB], writes=[Ec])
        for h in range(8):
            pm_ = pmix[0] if h < 6 else pmix[1]
            cb = (512 if 3 <= h < 6 else 0) + (h % 3) * 129
            if nh > 0:
                op('pe', lambda e, h=h, pm_=pm_, cb=cb: e.matmul(pm_[0:nq, cb:cb + 129], Eh[:, h, 0:nq], Vh[:, h, :], start=True, stop=False),
                   reads=[Eh, Vh], writes=[pm_.name])
            op('pe', lambda e, h=h, pm_=pm_, cb=cb: e.matmul(pm_[0:nq, cb:cb + 129], Ec[0:nq, h, 0:nq], Vc[0:nq, h, :], start=(nh == 0), stop=True),
               reads=[Ec, Vc], writes=[pm_.name])
        op('act', lambda e: e.activation(Osb[0:nq, 0:387], pmix[0][0:nq, 0:387], AF.Copy), reads=[pmix[0].name], writes=[Osb])
        op('act', lambda e: e.activation(Osb[0:nq, 387:774], pmix[0][0:nq, 512:899], AF.Copy), reads=[pmix[0].name], writes=[Osb])
        op('act', lambda e: e.activation(Osb[0:nq, 774:1032], pmix[1][0:nq, 0:258], AF.Copy), reads=[pmix[1].name], writes=[Osb])
        dma('sp', odst, Osb[0:nq, :], reads=[Osb], writes=wr, chan=Osb)

    def run_steps(steps, pieces=None):
        pieces = list(pieces or [])
        for st in steps:
            st['A'] = ASET[astep['i'] % 2]
            astep['i'] += 1
        if steps:
            attn_load(steps[0])
        for i_, st in enumerate(steps):
            if i_ + 1 < len(steps):
                attn_load(steps[i_ + 1])
            attn_compute(st)
            if pieces and i_ % 2 == 1:
                pieces.pop(0)()
        while pieces:
            pieces.pop(0)()

    def attn_combine(P, osrc3, rd, b, wts):
        dma('sp', Og[0:P, :, :], osrc3.rearrange("g n c -> n g c"), reads=rd, writes=[Og], chan=Og)
        op('dve', lambda e: e.tensor_tensor(Og[0:P, 0, :], Og[0:P, 0, :], Og[0:P, 1, :], ALU.add), reads=[Og], writes=[Og])
        op('dve', lambda e: e.tensor_tensor(Og[0:P, 0, :], Og[0:P, 0, :], Og[0:P, 2, :], ALU.add), reads=[Og], writes=[Og])
        o3 = Og[0:P, 0, :].rearrange("p (h c) -> p h c", c=129)
        op('dve', lambda e: e.reciprocal(cols[0:P, 0:8], o3[:, :, 128]), reads=[Og], writes=[cols])
        op('dve', lambda e: e.tensor_tensor(hh[0:P, :].rearrange("p (h c) -> p h c", c=128), o3[:, :, 0:128],
                                            cols[0:P, 0:8].unsqueeze(2).broadcast_to([P, 8, 128]), ALU.mult), reads=[Og, cols], writes=[hh])
        for kc in range(8):
            op('pe', lambda e, kc=kc: e.transpose(pT[:, kc * 128:kc * 128 + P], hh[0:P, kc * 128:(kc + 1) * 128], ident_b[0:P, 0:P]), reads=[hh, ident_b], writes=[pT])
        op('act', lambda e: e.activation(gatedT[:, :, 0:P], pT[:].rearrange("p (a b) -> p a b", b=128)[:, :, 0:P], AF.Copy), reads=[pT], writes=[gatedT])
        wout_block(b, P, wts)

    def q_proj(N):
        for s in range(6):
            wt = wslab('w_q', s)
            for half in range(2):
                p = nextps()
                for q in range(2):
                    cc = half * 2 + q
                    for kc in range(8):
                        op('pe', lambda e, kc=kc, cc=cc, q=q, p=p, wt=wt: e.matmul(
                            p[:, q * NT:q * NT + N], wt[:, kc, cc * 128:(cc + 1) * 128], hT[:, kc, 0:N], start=(kc == 0), stop=(kc == 7)),
                            reads=[hT, wt], writes=[p])
                fc = s * 4 + half * 2
                pin = p[:].rearrange("p (a b) -> p a b", b=NT)[:, :, 0:N]
                op('act', lambda e, fc=fc, pin=pin: e.activation(QT[:, fc:fc + 2, 0:N], pin, AF.Copy), reads=[p], writes=[QT])

    blocks_p = [(0, 128), (128, 128)]
    load_state = {'mods': None}

    def ensure_mods(layer, sample):
        if load_state['mods'] != (layer, sample):
            load_mods(layer, sample)
            load_state['mods'] = (layer, sample)

    def front_pieces(ti):
        tok0 = ti * NT
        P_ = []

        def nrm(b):
            def f():
                if b == 0:
                    load_gs(0, 'mix', False)
                dma('sp', xstage[:], xp[tok0 + b * 128:tok0 + (b + 1) * 128, :], writes=[xstage], chan=xstage)
                norm_mod_T(xstage, 128, "gm1", "sh1", b * 128)
                if b == 1 and ti > 0:
                    load_gs(1, 'mlp', False)
            return f
        P_.append(nrm(0))
        P_.append(nrm(1))
        for part in ('qk', 'v', 'o', 'g'):
            P_.append(lambda part=part: w_in_proj(NT, blocks_p, False, parts=(part,)))

        def pconv_piece():
            for s in range(4):
                wt = wslab('w_in', s)
                p = nextps()
                for kc in range(8):
                    op('pe', lambda e, kc=kc, p=p, wt=wt: e.matmul(p[0:3, :], hT[:, kc, NT - 3:NT], wt[:, kc, :], start=(kc == 0), stop=(kc == 7)), reads=[hT, wt], writes=[p])
                kt = kvtoks[s % 2]
                op('act', lambda e, p=p, kt=kt: e.activation(kt[0:3, :], p[0:3, :], AF.Copy), reads=[p], writes=[kt])
                dma('sp', pconv[:, s * 512:(s + 1) * 512], kt[0:3, :], reads=[kt], chan=kt)
        if ti == ntile - 1:
            P_.append(pconv_piece)

        def conv_piece():
            conv_silu(NT, False)
            op('dve', lambda e: e.tensor_copy(qkpre[:, :, 0:3], qkpre[:, :, NT:NT + 3]), reads=[qkpre], writes=[qkpre])
            gate_rows_prompt(NT)
        P_.append(conv_piece)
        P_.append(lambda: mlstm_chunk(0, 0, True))

        def chunk1():
            mlstm_chunk(1, 128, False)
            op('dve', lambda e: e.tensor_copy(rcol["Bprev"][:, 0:1], R["B"][:, NT - 1:NT]), reads=[R["B"]], writes=[rcol["Bprev"]])
            op('dve', lambda e: e.tensor_copy(rcol["gprev"][:, 0:1], R["g"][:, NT - 1:NT]), reads=[R["g"]], writes=[rcol["gprev"]])
        P_.append(chunk1)
        return P_

    def back(ti):
        tok0 = ti * NT
        load_gg(0, 'mix', False)
        for b in range(2):
            dma('sp', xblk[b][:], xp[tok0 + b * 128:tok0 + (b + 1) * 128, :], writes=[xblk[b].name], chan=xblk[b])
        wts = [wslab('w_out', 0), wslab('w_out', 1)]
        for b in range(2):
            gated_T(b, b * 128, 128)
            wout_block(b, 128, wts)
            post_norm_res(pmix[b], xblk[b], 128, "gg1")
        if ti == 0:
            load_gs(0, 'mlp', False)
        load_gg(0, 'mlp', False)
        mlp(0, blocks_p, NT, after_norm=lambda: load_gs(0, 'kv', False))
        kv_proj(blocks_p, tok0, False, after_norm=lambda: load_gs(1, 'mix', False))

    def layer1_tile(ti, pieces):
        tok0 = ti * NT
        load_gg(1, 'mix', False)
        for b in range(2):
            norm_mod_T(xblk[b], 128, "gm1", "sh1", b * 128)
        if not pieces:
            load_gs(1, 'mlp', False)
        q_proj(NT)
        steps = []
        okeys = []
        for g, (win, dil) in enumerate(GROUPS):
            nq = 128 if dil == 1 else NT // dil
            nsteps = NT // (nq * dil)
            for r in range(dil):
                for st in range(nsteps):
                    tq0 = tok0 + st * 128 * (1 if dil == 1 else 0) + r
                    j0 = (tq0 - r) // dil
                    nh = min(128, j0)
                    kb, vb = g * D, 3 * D + g * D
                    rows_c = slice(tq0, tq0 + (nq - 1) * dil + 1, dil)
                    kh = vh_ = None
                    if nh > 0:
                        th0 = tq0 - nh * dil
                        rows_h = slice(th0, th0 + (nh - 1) * dil + 1, dil)
                        kh = kvscr[rows_h, kb:kb + D]
                        vh_ = kvscr[rows_h, vb:vb + D]
                    if dil == 1:
                        q3 = QT[:, g * 8:(g + 1) * 8, st * 128:st * 128 + 128]
                    else:
                        q3 = QT[:, g * 8:(g + 1) * 8, r:NT:dil]
                    lo_blk = max(0, (tq0 - 128 * dil)) // 128
                    rd = [('dram', 'kvscr', bb) for bb in range(lo_blk, (tok0 + NT) // 128)]
                    key = ('dram', 'oscr', ti, len(steps))
                    okeys.append(key)
                    steps.append(dict(q3=q3, nq=nq, nh=nh, khist=kh, vhist=vh_, kcur=kvscr[rows_c, kb:kb + D], vcur=kvscr[rows_c, vb:vb + D],
                                      odst=oscr[g, rows_c, :], rd=rd, wr=[key]))
        run_steps(steps, pieces)
        wts = [wslab('w_o', 0), wslab('w_o', 1)]
        for b in range(2):
            t0 = tok0 + b * 128
            attn_combine(128, oscr[:, t0:t0 + 128, :], okeys, b, wts)
            post_norm_res(pmix[b], xblk[b], 128, "gg1")
        load_gg(1, 'mlp', False)
        mlp(1, blocks_p, NT, after_norm=(lambda: load_gs(0, 'mlp', False)) if ti < ntile - 1 else None)
        for b in range(2):
            t0 = tok0 + b * 128
            dma('sp', yp[t0:t0 + 128, :], xblk[b][:], reads=[xblk[b].name], chan=xblk[b])


    colsS = sb([4, 64], F32, "colsS")
    acols = sb([128, 32], F32, "acols")
    rhs32 = sb([4, 32], F32, "rhs32")
    Ktok = Khist

    def sample_tile():
        blocks_s = [(0, NS)]
        P = NS
        load_gs(0, 'mix', True)
        load_gg(0, 'mix', True)
        dma('sp', xblk[0][0:P, :], xs, writes=[xblk[0]], chan=xblk[0])
        norm_mod_T(xblk[0], P, "gm1", "sh1", 0)
        dma('sp', big[0:12, 0:2048], stconv.rearrange("i j c -> (i j) c"), writes=[big], chan=big)
        for fc in range(16):
            op('pe', lambda e, fc=fc: e.transpose(pS[:, fc * 12:fc * 12 + 12], big[0:12, fc * 128:(fc + 1) * 128], ident_f[0:12, 0:12]),
               reads=[big, ident_f], writes=[pS])
        op('act', lambda e: e.activation(qkpre[:, :, 0:16].rearrange("p f (i j) -> p f i j", j=4)[:, :, :, 0:3],
                                         pS[:, 0:192].rearrange("p (f i j) -> p f i j", i=4, j=3), AF.Copy), reads=[pS], writes=[qkpre])
        w_in_proj(P, blocks_s, True)
        dma('sp', sconvo[:, 0:2, :], stconv[:, 1:3, :], chan='sconv_copy')
        for s_ in range(4):
            wt = wslab('w_in', s_)
            p = nextps()
            for kc in range(8):
                op('pe', lambda e, kc=kc, p=p, wt=wt: e.matmul(p[0:P, :], hT[:, kc, 0:P], wt[:, kc, :], start=(kc == 0), stop=(kc == 7)), reads=[hT, wt], writes=[p])
            op('act', lambda e, p=p: e.activation(kvtok[0:P, :], p[0:P, :], AF.Copy), reads=[p], writes=[kvtok])
            dma('sp', sconvo[:, 2, s_ * 512:(s_ + 1) * 512], kvtok[0:P, :], reads=[kvtok], chan=kvtok)
        conv_silu(P, True)
        with nc.allow_non_contiguous_dma(reason="tiny"):
            dma('sp', R["B"][:, 0:4], stm.rearrange("i h -> h i"), writes=[R["B"]], chan=R["B"])
        op('dve', lambda e: e.tensor_tensor(R["a"][:, 0:4], R["lf"][:, 0:4], R["B"][:, 0:4], ALU.add), reads=[R["lf"], R["B"]], writes=[R["a"]])
        op('dve', lambda e: e.tensor_tensor(R["g"][:, 0:4], R["a"][:, 0:4], R["li"][:, 0:4], ALU.max), reads=[R["a"], R["li"]], writes=[R["g"]])
        with nc.allow_non_contiguous_dma(reason="tiny"):
            dma('sp', smo.rearrange("i h -> h i"), R["g"][:, 0:4], reads=[R["g"]], chan=R["g"])
        op('dve', lambda e: e.tensor_tensor(R["wk"][:, 0:4], R["a"][:, 0:4], R["g"][:, 0:4], ALU.subtract), reads=[R["a"], R["g"]], writes=[R["wk"]])
        op('act', lambda e: e.activation(R["wk"][:, 0:4], R["wk"][:, 0:4], AF.Exp), reads=[R["wk"]], writes=[R["wk"]])
        op('dve', lambda e: e.tensor_tensor(R["e"][:, 0:4], R["li"][:, 0:4], R["g"][:, 0:4], ALU.subtract), reads=[R["li"], R["g"]], writes=[R["e"]])
        op('act', lambda e: e.activation(R["e"][:, 0:4], R["e"][:, 0:4], AF.Exp), reads=[R["e"]], writes=[R["e"]])
        op('act', lambda e: e.activation(R["t"][:, 0:4], R["g"][:, 0:4], AF.Exp, scale=-1.0), reads=[R["g"]], writes=[R["t"]])
        for off, src in ((0, "wk"), (16, "t")):
            op('dve', lambda e, off=off, src=src: e.tensor_tensor(
                rhs32[:, off:off + 16].rearrange("p (h i) -> p h i", i=4),
                R[src][:, 0:4].unsqueeze(1).broadcast_to([4, 4, 4]),
                ident_f[0:4, 0:4].unsqueeze(2).broadcast_to([4, 4, 4]), ALU.mult), reads=[R[src], ident_f], writes=[rhs32])
        op('pe', lambda e: e.matmul(pS[:, 0:32], ones_f[0:4, :], rhs32[:, :], start=True, stop=True), reads=[ones_f, rhs32], writes=[pS])
        op('act', lambda e: e.activation(acols[:], pS[:, 0:32], AF.Copy), reads=[pS], writes=[acols])
        op('pe', lambda e: e.transpose(pS[0:4, 0:4], R["e"][:, 0:4], ident_f[0:4, 0:4]), reads=[R["e"], ident_f], writes=[pS])
        op('act', lambda e: e.activation(colsS[:, 0:4], pS[0:4, 0:4], AF.Copy, scale=1.0 / 16.0), reads=[pS], writes=[colsS])
        op('dve', lambda e: e.tensor_tensor(
            colsS[:, 16:32].rearrange("p (i h) -> p i h", h=4),
            colsS[:, 0:4].unsqueeze(1).broadcast_to([4, 4, 4]),
            ident_f[0:4, 0:4].unsqueeze(2).broadcast_to([4, 4, 4]), ALU.mult), reads=[colsS, ident_f], writes=[('colsS2',)])
        for j in range(8):
            op('pe', lambda e, j=j: e.transpose(pT[0:P, j * 128:(j + 1) * 128], qkT[:, 8 + j, 0:P], ident_b[:]), reads=[qkT, ident_b], writes=[pT])
        op('act', lambda e: e.activation(Ktok[0:P, :], pT[0:P, :], AF.Copy), reads=[pT], writes=[Ktok])
        for i in range(NS):
            dma('sp', Cst[:, :, :, 0:256], stC[i].rearrange("h (kc p) v -> p h kc v", p=128), writes=[Cst], chan=Cst)
            with nc.allow_non_contiguous_dma(reason="tiny"):
                dma('sp', Cst[:, :, :, 256], stn[i].rearrange("h (kc p) -> p h kc", p=128), writes=[Cst], chan=Cst)
            for h in range(4):
                op('dve', lambda e, h=h, i=i: e.tensor_scalar(Kp[0:P, :], Ktok[0:P, h * 256:(h + 1) * 256], colsS[:, 16 + i * 4 + h:17 + i * 4 + h], None, ALU.mult),
                   reads=[Ktok, ('colsS2',)], writes=[Kp])
                for kc in range(2):
                    pu = nextps()
                    op('pe', lambda e, kc=kc, pu=pu, h=h: e.matmul(pu[:, 0:257], Kp[0:P, kc * 128:(kc + 1) * 128], vext[0][0:P, h, :], start=True, stop=True),
                       reads=[Kp, vext[0]], writes=[pu])
                    op('dve', lambda e, kc=kc, pu=pu, h=h, i=i: e.scalar_tensor_tensor(Cst[:, h, kc, :], Cst[:, h, kc, :], acols[:, h * 4 + i:h * 4 + i + 1], pu[:, 0:257], ALU.mult, ALU.add),
                       reads=[pu, Cst, acols], writes=[Cst])
                    op('dve', lambda e, kc=kc, h=h: e.tensor_copy(Cbf[:, h, kc, :], Cst[:, h, kc, :]), reads=[Cst], writes=[Cbf])
                pn_ = nextps()
                for kc in range(2):
                    op('pe', lambda e, kc=kc, pn_=pn_, h=h, i=i: e.matmul(pn_[0:1, 0:257], qkT[:, 2 * h + kc, i:i + 1], Cbf[:, h, kc, :], start=(kc == 0), stop=(kc == 1)),
                       reads=[qkT, Cbf], writes=[pn_])
                op('act', lambda e, pn_=pn_: e.activation(cols[0:1, 12:13], pn_[0:1, 256:257], AF.Abs), reads=[pn_], writes=[('c12', 0)])
                op('dve', lambda e, h=h, i=i: e.tensor_tensor(cols[0:1, 12:13], cols[0:1, 12:13], acols[0:1, 16 + h * 4 + i:17 + h * 4 + i], ALU.max),
                   reads=[('c12', 0), acols], writes=[('c12', 0)])
                op('dve', lambda e: e.reciprocal(cols[0:1, 12:13], cols[0:1, 12:13]), reads=[('c12', 0)], writes=[('c12', 0)])
                op('act', lambda e, pn_=pn_, h=h: e.activation(hh[0:1, h * 256:(h + 1) * 256], pn_[0:1, 0:256], AF.Identity, scale=cols[0:1, 12:13]),
                   reads=[pn_, ('c12', 0)], writes=[hh])
            for h in range(4):
                for kc in range(2):
                    dma('sp', sCo[i, h, kc * 128:(kc + 1) * 128, :], Cst[:, h, kc, 0:256], reads=[Cst], chan=Cst)
            with nc.allow_non_contiguous_dma(reason="tiny"):
                dma('sp', sno[i].rearrange("h (kc p) -> p h kc", p=128), Cst[:, :, :, 256], reads=[Cst], chan=Cst)
            for kc in range(8):
                op('pe', lambda e, kc=kc, i=i: e.transpose(pT[:, kc * 128 + 2 * i:kc * 128 + 2 * i + 1], hh[0:1, kc * 128:(kc + 1) * 128], ident_b[0:1, 0:1]), reads=[hh, ident_b], writes=[pT])
        op('dve', lambda e: e.tensor_tensor(gatedT[:, :, 0:P], pT[:].rearrange("p (a b) -> p a b", b=128)[:, :, 0:2 * P:2], sigT[:, :, 0:P], ALU.mult),
           reads=[pT, sigT], writes=[gatedT])
        wts = [wslab('w_out', 0), wslab('w_out', 1)]
        wout_block(0, P, wts)
        post_norm_res(pmix[0], xblk[0], P, "gg1")
        load_gs(0, 'mlp', True)
        load_gg(0, 'mlp', True)
        mlp(0, blocks_s, P, after_norm=lambda: load_gs(0, 'kv', True))
        kv_proj(blocks_s, 0, True, after_norm=lambda: load_gs(1, 'mix', True))
        load_gg(1, 'mix', True)
        norm_mod_T(xblk[0], P, "gm1", "sh1", 0)
        load_gs(1, 'mlp', True)
        q_proj(P)
        steps = []
        for g, (win, dil) in enumerate(GROUPS):
            kb, vb = g * D, 3 * D + g * D
            for i in range(NS):
                steps.append(dict(q3=QT[:, g * 8:(g + 1) * 8, i:i + 1], nq=1, nh=128,
                                  khist=cache[('k', g)][i, 0:win:dil, :], vhist=cache[('v', g)][i, 0:win:dil, :],
                                  kcur=kvscr_s[i:i + 1, kb:kb + D], vcur=kvscr_s[i:i + 1, vb:vb + D],
                                  odst=oscr_s[g, i:i + 1, :], rd=[('dram', 'kvscr_s')], wr=[('dram', 'oscr_s', g, i)], hq='pool'))
        run_steps(steps)
        wts = [wslab('w_o', 0), wslab('w_o', 1)]
        attn_combine(P, oscr_s[:, 0:P, :], [('dram', 'oscr_s', g_, i_) for g_ in range(3) for i_ in range(NS)], 0, wts)
        post_norm_res(pmix[0], xblk[0], P, "gg1")
        load_gg(1, 'mlp', True)
        mlp(1, blocks_s, P)
        dma('sp', ys, xblk[0][0:P, :], reads=[xblk[0]], chan=xblk[0])

    import os as _os
    _nt = int(_os.environ.get("KDBG_TILES", ntile))
    nt_ = min(ntile, _nt)
    if nt_ > 0:
        for f_ in front_pieces(0):
            f_()
        back(0)
    for ti in range(nt_):
        layer1_tile(ti, front_pieces(ti + 1) if ti + 1 < nt_ else None)
        if ti + 1 < nt_:
            back(ti + 1)

    for h in range(4):
        for kc in range(2):
            dma('sp', pC[h, kc * 128:(kc + 1) * 128, :], Cst[:, h, kc, 0:256], reads=[Cst], chan=Cst)
    with nc.allow_non_contiguous_dma(reason="tiny"):
        dma('sp', pn.rearrange("h (kc p) -> p h kc", p=128), Cst[:, :, :, 256], reads=[Cst], chan=Cst)
    op('dve', lambda e: e.tensor_tensor(rcol["m"][:, 0:1], rcol["Bprev"][:, 0:1], rcol["gprev"][:, 0:1], ALU.add), reads=[rcol["Bprev"], rcol["gprev"]], writes=[rcol["m"]])
    dma('sp', pm, rcol["m"][:, 0:1], reads=[rcol["m"]], chan=rcol["m"])

    if do_samples and not _os.environ.get("KDBG_NOSAMP"):
        sample_tile()

    S.finish()
    if dry:
        return wreq
    return nc


def _consts():
    ik = np.arange(128)[:, None]
    iq = np.arange(128)[None, :]
    maskA = (ik >= iq).astype(np.float32)
    maskB = (ik <= iq).astype(np.float32)
    vtab = (np.arange(128)[:, None] >= (128 - np.arange(129)[None, :])).astype(np.float32)
    return np.eye(128, dtype=np.float32), maskA, maskB, vtab


def make_in_maps(inp, ncores, T):
    ident, maskA, maskB, vtab = _consts()
    maps = []
    f = lambda a: np.ascontiguousarray(np.asarray(a, dtype=np.float32))
    nb = inp['x_prompt'].shape[0]
    for c in range(ncores):
        b = c % nb
        sl = slice(NS * c, NS * c + NS)
        m = {
            "xp": f(inp['x_prompt'][b]), "xs": f(inp['x_sample'][sl, 0]),
            "c5": f(np.concatenate([inp['c_prompt'][b:b + 1], inp['c_sample'][sl]], 0)),
            "stC": f(inp['state_C'][0, sl]), "stn": f(inp['state_n'][0, sl]), "stm": f(inp['state_m'][0, sl]),
            "stconv": f(inp['state_conv'][0, sl]),
            "w_ada": f(inp['w_ada']), "b_ada": f(inp['b_ada']), "g_norm": f(inp['g_norm']),
            "w_mlp_up": f(inp['w_mlp_up']), "w_mlp_down": f(inp['w_mlp_down']), "w_a_in": f(inp['w_a_in'][0]),
            "b_a_gate": f(inp['b_a_gate'][0]),
            "wcb": f(np.concatenate([inp['w_a_conv'][0], inp['b_a_conv'][0][None]], 0)),
            "w_a_out": f(inp['w_a_out'][0]), "g_kv": f(inp['g_kv']), "w_ada_kv": f(inp['w_ada_kv']),
            "b_ada_kv": f(inp['b_ada_kv']), "w_kv": f(inp['w_kv']), "w_b_q": f(inp['w_b_q'][0]), "w_b_o": f(inp['w_b_o'][0]),
            "ident": ident, "maskA": maskA, "maskB": maskB, "vtab": vtab,
        }
        caches = ((inp['cache_k_g0'], inp['cache_v_g0']), (inp['cache_k_g1'], inp['cache_v_g1']),
                  (inp['cache_k_g2'], inp['cache_v_g2']))
        for g in range(3):
            m["ck%d" % g] = f(caches[g][0][sl]).reshape(NS, -1, D)
            m["cv%d" % g] = f(caches[g][1][sl]).reshape(NS, -1, D)
        maps.append(m)
    return maps


def assemble(results, nb, nsb, T):
    R0 = results
    ncores = len(R0)
    y_prompt = np.stack([R0[b]["yp"] for b in range(nb)])
    y_sample = np.concatenate([R0[c]["ys"] for c in range(ncores)])[:, None, :]
    p_C = np.stack([R0[b]["pC"] for b in range(nb)])[None]
    p_n = np.stack([R0[b]["pn"] for b in range(nb)])[None]
    p_m = np.stack([R0[b]["pm"][:, 0] for b in range(nb)])[None]
    p_conv = np.stack([R0[b]["pconv"] for b in range(nb)])[None]
    s_C = np.concatenate([R0[c]["sC"] for c in range(ncores)])[None]
    s_n = np.concatenate([R0[c]["sn"] for c in range(ncores)])[None]
    s_m = np.concatenate([R0[c]["sm"] for c in range(ncores)])[None]
    s_conv = np.concatenate([R0[c]["sconv"] for c in range(ncores)])[None]
    outs = [y_prompt, y_sample, p_C, p_n, p_m, p_conv, s_C, s_n, s_m, s_conv]
    for g in range(3):
        for kv in ("pk", "pv"):
            a = np.stack([R0[b]["%s%d" % (kv, g)] for b in range(nb)])
            outs.append(a.reshape(nb, a.shape[1], 8, 128))
    for g in range(3):
        for kv in ("sk", "sv"):
            a = np.concatenate([R0[c]["%s%d" % (kv, g)] for c in range(ncores)])
            outs.append(a.reshape(a.shape[0], 1, 8, 128))
    return tuple(np.ascontiguousarray(o, dtype=np.float32) for o in outs)


def kernel(**inputs):
    inp = {k: np.asarray(v) for k, v in inputs.items()}
    T = inp['x_prompt'].shape[1]
    nb = inp['x_prompt'].shape[0]
    ncores = 8
    nc = build(T)
    maps = make_in_maps(inp, ncores, T)
    res = run_bass_kernel_spmd(nc, maps, core_ids=list(range(ncores)))
    return assemble(res.results, nb, inp['x_sample'].shape[0], T)
```

```python
import numpy as np
import concourse.bass as bass
import concourse.mybir as mybir
from concourse.bass_utils import run_bass_kernel_spmd

F32 = mybir.dt.float32
BF16 = mybir.dt.bfloat16
AF = mybir.ActivationFunctionType
ALU = mybir.AluOpType

D = 1024
KC = 8
NT = 256
DFF = 4096
EPS = 1e-6
NS = 4
GROUPS = ((128, 1), (512, 4), (2048, 16))
LN16 = float(np.log(16.0))


class Sch:
    def __init__(s, nc, dry=False):
        s.nc = nc
        s.dry = dry
        s.E = {'pe': nc.tensor, 'act': nc.scalar, 'dve': nc.vector, 'pool': nc.gpsimd, 'sp': nc.sync}
        s.sem = {e: nc.alloc_semaphore(name='sem_' + e) for e in s.E}
        s.cnt = {e: 0 for e in s.E}
        s.waited = {e: {} for e in s.E}
        s.lastw = {}
        s.readers = {}
        s.dsem = {}
        s.excl = set()

    @staticmethod
    def _k(r):
        if isinstance(r, (str, tuple)):
            return r
        if hasattr(r, 'tensor'):
            return r.tensor.name
        return r.name

    def _split(s, reads, writes):
        reads = [s._k(r) for r in reads]
        writes = [s._k(w) for w in writes]
        ex = [r for r in reads if r in s.excl and r not in writes]
        return [r for r in reads if r not in ex], writes + ex

    def _tokens(s, reads, writes):
        reads, writes = s._split(reads, writes)
        toks = []
        for r in reads:
            t = s.lastw.get(r)
            if t:
                toks.append(t + (True,))
        for w in writes:
            t = s.lastw.get(w)
            if t:
                toks.append(t + (False,))
            toks.extend(t_ + (False,) for t_ in s.readers.get(w, ()))
        return toks

    def _wait(s, e, toks):
        need = {}
        for (key, sem, val, is_raw) in toks:
            if key == e and e == 'pe':
                continue
            if key == e and e in ('act', 'dve') and not is_raw:
                continue
            if s.waited[e].get(key, 0) >= val:
                continue
            if need.get(key, (None, 0))[1] < val:
                need[key] = (sem, val)
        for key, (sem, val) in need.items():
            s.E[e].wait_ge(sem, val)
            s.waited[e][key] = val

    def _commit(s, tok, reads, writes):
        reads, writes = s._split(reads, writes)
        for w in writes:
            s.lastw[w] = tok
            s.readers[w] = []
        for r in reads:
            if r not in writes:
                lst = s.readers.setdefault(r, [])
                lst[:] = [t for t in lst if t[0] != tok[0]]
                lst.append(tok)

    def op(s, e, fn, reads=(), writes=()):
        if s.dry:
            return
        s._wait(e, s._tokens(reads, writes))
        inst = fn(s.E[e])
        s.cnt[e] += 1
        inst.then_inc(s.sem[e], 1)
        s._commit((e, s.sem[e], s.cnt[e]), reads, writes)

    def dma(s, q, out, in_, reads=(), writes=(), chan=None, **kw):
        if s.dry:
            return
        if q == 'sp' and type(out.tensor).__name__.startswith('DRam'):
            q = 'pool'
        s._wait(q, s._tokens(reads, writes))
        inst = s.E[q].dma_start(out, in_, **kw)
        chan = s._k(chan)
        if chan not in s.dsem:
            s.dsem[chan] = [s.nc.alloc_semaphore(name='dsem_%d' % len(s.dsem)), 0]
        ds = s.dsem[chan]
        ds[1] += 16
        inst.then_inc(ds[0], 16)
        s._commit((('dma', chan), ds[0], ds[1]), reads, writes)

    def finish(s):
        if s.dry:
            return
        for chan, (sem, val) in s.dsem.items():
            if val:
                s.E['sp'].wait_ge(sem, val)
        for e in ('pe', 'act', 'dve', 'pool'):
            if s.cnt[e]:
                s.E['sp'].wait_ge(s.sem[e], s.cnt[e])


def build(T, do_samples=True, wseq=None):
    assert T % NT == 0
    if wseq is None:
        wseq = build(T, do_samples, wseq=[])
    dry = (len(wseq) == 0)
    nc = bass.Bass("TRN2", target_bir_lowering=False)
    S = Sch(nc, dry=dry)
    wreq = []
    ntile = T // NT

    def din(name, shape):
        return nc.dram_tensor(name, list(shape), F32, kind="ExternalInput").ap()

    def dout(name, shape):
        return nc.dram_tensor(name, list(shape), F32, kind="ExternalOutput").ap()

    def dscr(name, shape, dt=F32):
        return nc.dram_tensor(name, list(shape), dt, kind="Internal").ap()

    xp = din("xp", [T, D])
    xs = din("xs", [NS, D])
    c5 = din("c5", [1 + NS, D])
    stC = din("stC", [NS, 4, 256, 256])
    stn = din("stn", [NS, 4, 256])
    stm = din("stm", [NS, 4])
    stconv = din("stconv", [NS, 3, 2048])
    cache = {}
    for g, (win, dil) in enumerate(GROUPS):
        cache[('k', g)] = din("ck%d" % g, [NS, win, D])
        cache[('v', g)] = din("cv%d" % g, [NS, win, D])
    w_ada = din("w_ada", [2, D, 6 * D])
    b_ada = din("b_ada", [2, 6 * D])
    g_norm = din("g_norm", [2, 4, D])
    w_up = din("w_mlp_up", [2, D, DFF])
    w_down = din("w_mlp_down", [2, DFF, D])
    w_in = din("w_a_in", [D, 4104])
    b_gate = din("b_a_gate", [8])
    wcb = din("wcb", [5, 2048])
    w_out = din("w_a_out", [D, D])
    g_kv = din("g_kv", [D])
    w_ada_kv = din("w_ada_kv", [D, 2 * D])
    b_ada_kv = din("b_ada_kv", [2 * D])
    w_kv = din("w_kv", [D, 6 * D])
    w_q = din("w_b_q", [D, 3 * D])
    w_o = din("w_b_o", [D, D])
    identd = din("ident", [128, 128])
    maskAd = din("maskA", [128, 128])
    maskBd = din("maskB", [128, 128])
    vtabd = din("vtab", [128, 129])

    yp = dout("yp", [T, D])
    ys = dout("ys", [NS, D])
    pC = dout("pC", [4, 256, 256])
    pn = dout("pn", [4, 256])
    pm = dout("pm", [4, 1])
    pconv = dout("pconv", [3, 2048])
    sCo = dout("sC", [NS, 4, 256, 256])
    sno = dout("sn", [NS, 4, 256])
    smo = dout("sm", [NS, 4])
    sconvo = dout("sconv", [NS, 3, 2048])
    pko, pvo, sko, svo = [], [], [], []
    for g, (win, dil) in enumerate(GROUPS):
        r = min(win, T)
        pko.append(dout("pk%d" % g, [r, D]))
        pvo.append(dout("pv%d" % g, [r, D]))
        sko.append(dout("sk%d" % g, [NS, D]))
        svo.append(dout("sv%d" % g, [NS, D]))

    modscr = dscr("modscr", [2, 1 + NS, 6 * D])
    modd = dscr("modd", [3, 3, 1 + NS, D])
    modkvscr = dscr("modkvscr", [1 + NS, 2 * D])
    kvscr = dscr("kvscr", [T, 6 * D], BF16)
    kvscr_s = dscr("kvscr_s", [NS, 6 * D], BF16)
    oscr = dscr("oscr", [3, T, 8 * 129])
    oscr_s = dscr("oscr_s", [3, NS, 8 * 129])

    _n = [0]

    def sb(shape, dt=F32, name=None):
        _n[0] += 1
        return nc.alloc_sbuf_tensor(name or ("t%d" % _n[0]), list(shape), dt)

    def ps(shape, dt=F32, name=None):
        _n[0] += 1
        t = nc.alloc_psum_tensor(name or ("p%d" % _n[0]), list(shape), dt)
        S.excl.add(t.name)
        return t

    pmix = [ps([128, 1024]), ps([128, 1024])]
    psAB = [ps([128, 512]), ps([128, 512])]
    pT = ps([128, 1024], BF16)
    pS = ps([128, 512])
    _ab = [0]

    def nextps():
        _ab[0] ^= 1
        return psAB[_ab[0]]

    ident_f = sb([128, 128])
    ident_b = sb([128, 128], BF16)
    maskA = sb([128, 128], BF16)
    maskB = sb([128, 128], BF16)
    vtab = sb([128, 129])
    ones_f = sb([128, 128])
    ones_b = sb([128, 128], BF16)
    NWS = 4
    wring = [sb([128, 8, 512], BF16, "wring%d" % i) for i in range(NWS)]
    xblk = [sb([128, D], F32, "xblk%d" % i) for i in range(2)]
    M0 = sb([128, D], F32, "modM0")
    M1 = sb([128, D], F32, "modM1")
    M2 = sb([128, D], F32, "modM2")
    mods = {"gm1": M0, "sh1": M1, "gg1": M2, "gm2": M0, "sh2": M1, "gg2": M2, "gmkv": M0, "shkv": M1}
    modtmp = M2
    hT = sb([128, 8, NT], BF16, "hT")

    tmpf = sb([128, D], F32, "tmpf")
    relut_v = tmpf[:, 0:2 * NT].rearrange("p (a b) -> p a b", b=NT)
    hn = sb([128, D], BF16, "hn")
    junk = hn
    colA = sb([128, 8], F32, "colA")
    qkpre = sb([128, 16, 3 + NT + 16], BF16, "qkpre")
    qkT = sb([128, 16, NT], BF16, "qkT")
    vext = [sb([128, 4, 257], BF16, "vext%d" % i) for i in range(2)]
    sigT = sb([128, 8, NT], BF16, "sigT")
    Cst = sb([128, 4, 2, 257], F32, "Cst")
    Cbf = sb([128, 4, 2, 257], BF16, "Cbf")
    Ctmp = None
    Kpt = [sb([128, 256], BF16, "Kp%d" % h_) for h_ in range(4)]
    PpTt = [sb([128, 128], BF16, "PpT%d" % h_) for h_ in range(4)]
    Kpv = [t_[:] for t_ in Kpt]
    PpTv = [t_[:] for t_ in PpTt]
    Kp = Kpt[0]
    hh = sb([128, D], BF16, "hh")
    hhm = [sb([128, D], BF16, "hhm%d" % i) for i in range(2)]
    xstage = sb([128, D], F32, "xstage")
    gatedT = sb([128, 8, 128], BF16, "gatedT")
    big = sb([128, 4096], F32, "big")
    hid = big[:].bitcast(BF16).rearrange("p (a b) -> p a b", b=NT)
    relut = None
    Wg = sb([128, 8, 8], BF16, "Wg")
    diagW = sb([128, 4, 16, 128], BF16, "diagW")
    wcT = sb([128, 16, 5], F32, "wcT")
    bg = sb([4, 2], F32, "bg")
    R = {k: sb([4, NT], F32, "row_" + k) for k in ("li", "fp", "lf", "B", "a", "g", "one")}
    R["wk"] = sb([4, 128], F32, "row_wk")
    R["e"] = sb([4, 128], F32, "row_e")
    R["t"] = R["fp"]
    rcol = {k: sb([4, 4], F32, "rcol_" + k) for k in ("Bprev", "gprev", "nb", "dec", "t", "m")}
    cols = sb([128, 16], F32, "cols")
    kvtoks = [sb([128, 512], F32, "kvtok%d" % i) for i in range(2)]
    kvbfs = [sb([128, 512], BF16, "kvbf%d" % i) for i in range(2)]
    kvtok = kvtoks[0]
    kvrot = {'t': 0, 'b': 0}
    QT = hid[:, 0:24, :]
    ASET = []
    for i_ in range(2):
        ASET.append(dict(
            Khist=sb([128, D], BF16, "Khist%d" % i_), Kcur=sb([128, D], BF16, "Kcur%d" % i_),
            Vh=sb([128, 8, 129], BF16, "Vh%d" % i_), Vc=sb([128, 8, 129], BF16, "Vc%d" % i_),
            KTh=sb([128, 8, 128], BF16, "KTh%d" % i_), KTc=sb([128, 8, 128], BF16, "KTc%d" % i_),
            Eh=sb([128, 8, 128], BF16, "Eh%d" % i_), Ec=sb([128, 8, 128], BF16, "Ec%d" % i_),
            Osb=sb([128, 8 * 129], F32, "Osb%d" % i_)))
    Khist = ASET[0]["Khist"]
    astep = {'i': 0}
    Og = big[:, 0:3096].rearrange("p (a b) -> p a b", b=1032)
    c5sb = modtmp
    wc5 = big[:, 0:2048]
    c5T = sb([128, 8, 8], BF16, "c5T")
    badab = M1[0:8, 0:512]
    modrow = M1[0:8, 512:1024]

    op = S.op
    dma = S.dma

    dma('sp', ident_f[:], identd, writes=[ident_f], chan=ident_f)
    dma('pool', ident_b[:], identd, writes=[ident_b], chan=ident_b)
    dma('pool', maskA[:], maskAd, writes=[maskA], chan=maskA)
    dma('pool', maskB[:], maskBd, writes=[maskB], chan=maskB)
    dma('sp', vtab[:], vtabd, writes=[vtab], chan=vtab)
    op('dve', lambda e: e.memset(ones_f[:], 1.0), writes=[ones_f])
    op('dve', lambda e: e.memset(ones_b[:], 1.0), writes=[ones_b])
    op('dve', lambda e: e.memset(qkpre[:], 0.0), writes=[qkpre])
    op('dve', lambda e: e.memset(Cst[:], 0.0), writes=[Cst])
    op('dve', lambda e: e.memset(Cbf[:], 0.0), writes=[Cbf])
    op('dve', lambda e: e.memset(R["one"][:], 1.0), writes=[R["one"]])
    for i in range(2):
        op('dve', lambda e, i=i: e.memset(vext[i][:], 1.0), writes=[vext[i]])
    for A_ in ASET:
        op('dve', lambda e, A_=A_: e.memset(A_["Vh"][:], 1.0), writes=[A_["Vh"]])
        op('dve', lambda e, A_=A_: e.memset(A_["Vc"][:], 1.0), writes=[A_["Vc"]])
        op('dve', lambda e, A_=A_: e.memset(A_["Khist"][:], 0.0), writes=[A_["Khist"]])
        op('dve', lambda e, A_=A_: e.memset(A_["Kcur"][:], 0.0), writes=[A_["Kcur"]])
    op('dve', lambda e: e.memset(rcol["Bprev"][:], 0.0), writes=[rcol["Bprev"]])
    op('dve', lambda e: e.memset(rcol["gprev"][:], 0.0), writes=[rcol["gprev"]])
    op('dve', lambda e: e.memset(c5T[:], 0.0), writes=[c5T])
    with nc.allow_non_contiguous_dma(reason="tiny"):
        dma('pool', Wg[:], w_in.rearrange("(kc p) n -> p kc n", p=128)[:, :, 4096:4104], writes=[Wg], chan=Wg)
        dma('sp', bg[:], b_gate.rearrange("(two h) -> h two", two=2), writes=[bg], chan=bg)
    dma('sp', wc5[0:5, 0:2048], wcb, writes=[wc5], chan=wc5)
    for fc in range(16):
        op('pe', lambda e, fc=fc: e.transpose(pS[:, fc * 5:fc * 5 + 5], wc5[0:5, fc * 128:(fc + 1) * 128], ident_f[0:5, 0:5]),
           reads=[wc5, ident_f], writes=[pS])
    op('act', lambda e: e.activation(wcT[:].rearrange("p a b -> p (a b)"), pS[:, 0:80], AF.Copy), reads=[pS], writes=[wcT])
    for j in range(4):
        for fc in range(16):
            op('dve', lambda e, j=j, fc=fc: e.tensor_scalar(diagW[:, j, fc, :], ident_f[:], wcT[:, fc, j:j + 1], None, ALU.mult),
               reads=[ident_f, wcT], writes=[diagW])

    wstate = {'i': 0}

    def wk(w2, c0, width=512):
        return w2.rearrange("(kc p) n -> p kc n", p=128)[:, :, c0:c0 + width]

    def wslab_raw(src3):
        i = wstate['i'] % NWS
        wstate['i'] += 1
        t = wring[i]
        w = src3.shape[2]
        dma('pool', t[:, :, 0:w], src3, writes=[t], chan=t)
        return t

    WSRC = {}
    WSRC['w_in'] = [wk(w_in, s_ * 512) for s_ in range(8)]
    WSRC['w_out'] = [wk(w_out, s_ * 512) for s_ in range(2)]
    for l_ in range(2):
        WSRC['up%d' % l_] = [wk(w_up[l_], s_ * 512) for s_ in range(8)]
        wd_ = w_down[l_].rearrange("(fg kc p) n -> fg p kc n", p=128, kc=8)
        WSRC['down%d' % l_] = [wd_[fg][:, :, half * 512:(half + 1) * 512] for half in range(2) for fg in range(4)]
    WSRC['w_kv'] = [wk(w_kv, s_ * 512) for s_ in range(12)]
    WSRC['w_q'] = [wk(w_q, s_ * 512) for s_ in range(6)]
    WSRC['w_o'] = [wk(w_o, s_ * 512) for s_ in range(2)]
    WSC = {}
    for name_ in ('w_in', 'w_out', 'up0', 'down0', 'w_kv', 'w_q', 'w_o', 'up1', 'down1'):
        WSC[name_] = dscr("wsc_" + name_, [len(WSRC[name_]), 128, 8, 512], BF16)

    def emit_conversions():
        for name_ in ('w_in', 'w_out', 'up0', 'down0', 'w_kv', 'w_q', 'w_o', 'up1', 'down1'):
            for s_, src in enumerate(WSRC[name_]):
                kcv = wstate['i']
                wstate['i'] += 1
                dma('pool', WSC[name_][s_], src, writes=[('dram', 'wsc', name_, s_), ('cvt', kcv % 3)], chan=('cvt', kcv % 3))

    AHEAD = 2
    wissued = {'n': 0}

    def wslab(name, s_):
        k = len(wreq)
        wreq.append((name, s_))
        if dry:
            return wring[0]
        assert wseq[k] == (name, s_), (k, wseq[k], name, s_)
        while wissued['n'] < min(len(wseq), k + 1 + AHEAD):
            j = wissued['n']
            nm, sj = wseq[j]
            tj = wring[j % NWS]
            dma('sp', tj[:], WSC[nm][sj], reads=[('dram', 'wsc', nm, sj)], writes=[tj], chan=tj)
            wissued['n'] += 1
        return wring[k % NWS]

    dma('sp', c5sb[0:5, :], c5, writes=[c5sb], chan=c5sb)
    op('act', lambda e: e.activation(tmpf[0:5, :], c5sb[0:5, :], AF.Sigmoid), reads=[c5sb], writes=[tmpf])
    op('dve', lambda e: e.tensor_tensor(tmpf[0:5, :], tmpf[0:5, :], c5sb[0:5, :], ALU.mult), reads=[tmpf, c5sb], writes=[tmpf])
    for kc in range(8):
        op('pe', lambda e, kc=kc: e.transpose(pS[:, kc * 8:kc * 8 + 5], tmpf[0:5, kc * 128:(kc + 1) * 128], ident_f[0:5, 0:5]),
           reads=[tmpf, ident_f], writes=[pS])
    op('act', lambda e: e.activation(c5T[:, :, 0:5], pS[:, 0:64].rearrange("p (a b) -> p a b", b=8)[:, :, 0:5], AF.Copy),
       reads=[pS], writes=[c5T])

    ada_jobs = []
    for (wmat, bvec, ncols, dst) in ((w_ada[0], b_ada[0], 6 * D, modscr[0]), (w_ada[1], b_ada[1], 6 * D, modscr[1]),
                                     (w_ada_kv, b_ada_kv, 2 * D, modkvscr)):
        for c0 in range(0, ncols, 512):
            ada_jobs.append((wmat, bvec, c0, dst))
    ada_bufs = [(M1[0:8, 0:512], M1[0:8, 512:1024], M1), (M2[0:8, 0:512], M2[0:8, 512:1024], M2)]
    ada_issued = {'n': 0}

    def ada_issue(upto):
        while ada_issued['n'] < min(len(ada_jobs), upto):
            j = ada_issued['n']
            wmat, bvec, c0, dst = ada_jobs[j]
            tj = wring[j % NWS]
            dma('pool', tj[:], wk(wmat, c0), writes=[tj], chan=tj)
            ada_issued['n'] += 1

    for j, (wmat, bvec, c0, dst) in enumerate(ada_jobs):
        ada_issue(j + 3)
        wt = wring[j % NWS]
        bb_, mr_, key_ = ada_bufs[j % 2]
        dma('sp', bb_[0:5, :], bvec[c0:c0 + 512].partition_broadcast(5), writes=[key_], chan=key_)
        p = nextps()
        for kc in range(8):
            op('pe', lambda e, kc=kc, p=p, wt=wt: e.matmul(p[0:5, :], c5T[:, kc, 0:5], wt[:, kc, :], start=(kc == 0), stop=(kc == 7)),
               reads=[c5T, wt], writes=[p])
        op('dve', lambda e, p=p, bb_=bb_, mr_=mr_: e.tensor_tensor(mr_[0:5, :], p[0:5, :], bb_[0:5, :], ALU.add), reads=[p, key_], writes=[key_])
        dma('sp', dst[:, c0:c0 + 512], mr_[0:5, :], reads=[key_], writes=[('dram', dst.tensor.name)], chan=key_)

    def derive(dst3, scr, base, g_pre, g_post):
        rd = [('dram', scr.tensor.name)]
        wr = [('dram', 'modd')]
        P5 = 1 + NS
        dma('sp', tmpf[0:P5, :], scr[:, base + D:base + 2 * D], reads=rd, writes=[tmpf], chan=tmpf)
        dma('sp', M0[0:P5, :], g_pre.partition_broadcast(P5), writes=[M0], chan=M0)
        op('dve', lambda e: e.scalar_tensor_tensor(tmpf[0:P5, :], tmpf[0:P5, :], 1.0, M0[0:P5, :], ALU.add, ALU.mult), reads=[tmpf, M0], writes=[tmpf])
        dma('sp', dst3[0], tmpf[0:P5, :], reads=[tmpf], writes=wr, chan=tmpf)
        dma('sp', dst3[1], scr[:, base:base + D], reads=rd, writes=wr, chan='modd_copy')
        if g_post is not None:
            dma('sp', tmpf[0:P5, :], scr[:, base + 2 * D:base + 3 * D], reads=rd, writes=[tmpf], chan=tmpf)
            dma('sp', M0[0:P5, :], g_post.partition_broadcast(P5), writes=[M0], chan=M0)
            op('dve', lambda e: e.tensor_tensor(tmpf[0:P5, :], tmpf[0:P5, :], M0[0:P5, :], ALU.mult), reads=[tmpf, M0], writes=[tmpf])
            dma('sp', dst3[2], tmpf[0:P5, :], reads=[tmpf], writes=wr, chan=tmpf)

    moddm = dscr("moddm", [2, 2, 3, 1 + NS, D])
    for l_ in range(2):
        derive(moddm[l_, 0], modscr[l_], 0, g_norm[l_, 0], g_norm[l_, 1])
        derive(moddm[l_, 1], modscr[l_], 3 * D, g_norm[l_, 2], g_norm[l_, 3])
    derive(modd[2], modkvscr, 0, g_kv, None)
    emit_conversions()

    def _modsrc(layer, phase):
        if phase == 'kv':
            return modd[2]
        return moddm[layer, 0 if phase == 'mix' else 1]

    def _mrow(src2, sample):
        if sample:
            return src2[1:1 + NS, :], NS
        return src2[0, :].partition_broadcast(128), 128

    def load_gs(layer, phase, sample):
        src = _modsrc(layer, phase)
        for i_, t_ in ((0, M0), (1, M1)):
            a, P = _mrow(src[i_], sample)
            dma('sp', t_[0:P, :], a, reads=[('dram', 'modd')], writes=[t_], chan=t_)

    def load_gg(layer, phase, sample):
        a, P = _mrow(_modsrc(layer, phase)[2], sample)
        dma('sp', M2[0:P, :], a, reads=[('dram', 'modd')], writes=[M2], chan=M2)

    def rstd_of(src, P, col):
        op('act', lambda e: e.activation(junk[0:P, :], src, AF.Square, accum_out=colA[0:P, col:col + 1]),
           reads=[src.tensor.name], writes=[junk, ('colA', col)])
        op('act', lambda e: e.activation(colA[0:P, col + 1:col + 2], colA[0:P, col:col + 1], AF.Ln, bias=EPSC[0:P, :], scale=1.0 / D),
           reads=[('colA', col)], writes=[('colA', col + 1)])
        op('act', lambda e: e.activation(colA[0:P, col + 1:col + 2], colA[0:P, col + 1:col + 2], AF.Exp, scale=-0.5),
           reads=[('colA', col + 1)], writes=[('colA', col + 1)])
        return colA[0:P, col + 1:col + 2], ('colA', col + 1)

    EPSC = sb([128, 1], F32, "epsc")
    op('dve', lambda e: e.memset(EPSC[:], EPS), writes=[EPSC])

    def norm_mod_T(xb, P, gm, sh, c0):
        rs, rk = rstd_of(xb[0:P, :], P, 0)
        op('dve', lambda e: e.scalar_tensor_tensor(tmpf[0:P, :], xb[0:P, :], rs, mods[gm][0:P, :], ALU.mult, ALU.mult),
           reads=[xb.name, rk, mods[gm]], writes=[tmpf])
        op('dve', lambda e: e.tensor_tensor(hn[0:P, :], tmpf[0:P, :], mods[sh][0:P, :], ALU.add),
           reads=[tmpf, mods[sh]], writes=[hn])
        for kc in range(8):
            op('pe', lambda e, kc=kc: e.transpose(pT[:, kc * 128:kc * 128 + P], hn[0:P, kc * 128:(kc + 1) * 128], ident_b[0:P, 0:P]),
               reads=[hn, ident_b], writes=[pT])
        op('act', lambda e: e.activation(hT[:, :, c0:c0 + P], pT[:].rearrange("p (a b) -> p a b", b=128)[:, :, 0:P], AF.Copy),
           reads=[pT], writes=[hT])

    def post_norm_res(pm_, xb, P, gg):
        rs, rk = rstd_of(pm_[0:P, :], P, 2)
        op('dve', lambda e: e.scalar_tensor_tensor(tmpf[0:P, :], pm_[0:P, :], rs, mods[gg][0:P, :], ALU.mult, ALU.mult),
           reads=[pm_.name, rk, mods[gg]], writes=[tmpf])
        op('dve', lambda e: e.tensor_tensor(xb[0:P, :], xb[0:P, :], tmpf[0:P, :], ALU.add),
           reads=[tmpf, xb.name], writes=[xb.name])

    def proj_tok(lhsT3, blocks, wmat, c0s, pm_list, half_of):
        for c0 in c0s:
            wt = wslab_raw(wk(wmat, c0))
            hf = half_of(c0)
            for b, (col0, P) in enumerate(blocks):
                for kc in range(8):
                    op('pe', lambda e, kc=kc, b=b, col0=col0, P=P, wt=wt, hf=hf: e.matmul(
                        pm_list[b][0:P, hf * 512:(hf + 1) * 512], lhsT3[:, kc, col0:col0 + P], wt[:, kc, :], start=(kc == 0), stop=(kc == 7)),
                        reads=[lhsT3.name, wt], writes=[pm_list[b].name])

    def mlp(layer, blocks, N, after_norm=None):
        for b, (col0, P) in enumerate(blocks):
            norm_mod_T(xblk[b], P, "gm2", "sh2", col0)
        if after_norm:
            after_norm()
        for s in range(8):
            wt = wslab('up%d' % layer, s)
            for half in range(2):
                p = nextps()
                for q in range(2):
                    cc = half * 2 + q
                    for kc in range(8):
                        op('pe', lambda e, kc=kc, cc=cc, q=q, p=p, wt=wt: e.matmul(
                            p[:, q * NT:q * NT + N], wt[:, kc, cc * 128:(cc + 1) * 128], hT[:, kc, 0:N], start=(kc == 0), stop=(kc == 7)),
                            reads=[hT, wt], writes=[p])
                fc = s * 4 + half * 2
                pin = p[:].rearrange("p (a b) -> p a b", b=NT)[:, :, 0:N]
                op('act', lambda e, pin=pin: e.activation(relut_v[:, :, 0:N], pin, AF.Relu), reads=[p], writes=[tmpf])
                op('dve', lambda e, fc=fc, pin=pin: e.tensor_tensor(hid[:, fc:fc + 2, 0:N], relut_v[:, :, 0:N], pin, ALU.mult),
                   reads=[p, tmpf], writes=[hid])
        wd = w_down[layer].rearrange("(fg kc p) n -> fg p kc n", p=128, kc=8)
        for half in range(2):
            for fg in range(4):
                wt = wslab('down%d' % layer, half * 4 + fg)
                for b, (col0, P) in enumerate(blocks):
                    for kc in range(8):
                        op('pe', lambda e, kc=kc, b=b, col0=col0, P=P, wt=wt, fg=fg, half=half: e.matmul(
                            pmix[b][0:P, half * 512:(half + 1) * 512], hid[:, fg * 8 + kc, col0:col0 + P], wt[:, kc, :],
                            start=(fg == 0 and kc == 0), stop=(fg == 3 and kc == 7)),
                            reads=[hid, wt], writes=[pmix[b].name])
        for b, (col0, P) in enumerate(blocks):
            post_norm_res(pmix[b], xblk[b], P, "gg2")

    def w_in_proj(N, blocks, sample, parts=('qk', 'v', 'o', 'g')):
        qstep = 4 if sample else 1
        for s in (range(4) if 'qk' in parts else ()):
            wt = wslab('w_in', s)
            for half in range(2):
                p = nextps()
                for q in range(2):
                    cc = half * 2 + q
                    for kc in range(8):
                        op('pe', lambda e, kc=kc, cc=cc, q=q, p=p, wt=wt: e.matmul(
                            p[:, q * NT:q * NT + N], wt[:, kc, cc * 128:(cc + 1) * 128], hT[:, kc, 0:N], start=(kc == 0), stop=(kc == 7)),
                            reads=[hT, wt], writes=[p])
                fc = s * 4 + half * 2
                pin = p[:].rearrange("p (a b) -> p a b", b=NT)[:, :, 0:N]
                if sample:
                    dst = qkpre[:, fc:fc + 2, 3:3 + 4 * N:4]
                else:
                    dst = qkpre[:, fc:fc + 2, 3:3 + N]
                op('act', lambda e, dst=dst, pin=pin: e.activation(dst, pin, AF.Copy), reads=[p], writes=[qkpre])
        for s in (range(2) if 'v' in parts else ()):
            wt = wslab('w_in', 4 + s)
            for b, (col0, P) in enumerate(blocks):
                p = nextps()
                for kc in range(8):
                    op('pe', lambda e, kc=kc, col0=col0, P=P, p=p, wt=wt: e.matmul(
                        p[0:P, :], hT[:, kc, col0:col0 + P], wt[:, kc, :], start=(kc == 0), stop=(kc == 7)),
                        reads=[hT, wt], writes=[p])
                op('act', lambda e, b=b, P=P, p=p, s=s: e.activation(
                    vext[b][0:P, 2 * s:2 * s + 2, 0:256], p[0:P, :].rearrange("p (a b) -> p a b", b=256), AF.Copy),
                    reads=[p], writes=[vext[b]])
        for s in (range(2) if 'o' in parts else ()):
            wt = wslab('w_in', 6 + s)
            for half in range(2):
                p = nextps()
                for q in range(2):
                    cc = half * 2 + q
                    for kc in range(8):
                        op('pe', lambda e, kc=kc, cc=cc, q=q, p=p, wt=wt: e.matmul(
                            p[:, q * NT:q * NT + N], wt[:, kc, cc * 128:(cc + 1) * 128], hT[:, kc, 0:N], start=(kc == 0), stop=(kc == 7)),
                            reads=[hT, wt], writes=[p])
                fc = s * 4 + half * 2
                pin = p[:].rearrange("p (a b) -> p a b", b=NT)[:, :, 0:N]
                op('act', lambda e, fc=fc, pin=pin: e.activation(sigT[:, fc:fc + 2, 0:N], pin, AF.Sigmoid), reads=[p], writes=[sigT])
        if 'g' not in parts:
            return
        for gi, nm in ((0, "li"), (1, "fp")):
            for kc in range(8):
                op('pe', lambda e, kc=kc, gi=gi: e.matmul(pS[0:4, gi * NT:gi * NT + N], Wg[:, kc, gi * 4:gi * 4 + 4], hT[:, kc, 0:N],
                                                          start=(kc == 0), stop=(kc == 7)), reads=[Wg, hT], writes=[pS])
        op('dve', lambda e: e.tensor_scalar(R["li"][:, 0:N], pS[0:4, 0:N], bg[:, 0:1], None, ALU.add), reads=[pS, bg], writes=[R["li"]])
        op('dve', lambda e: e.tensor_scalar(R["fp"][:, 0:N], pS[0:4, NT:NT + N], bg[:, 1:2], -1.0, ALU.add, ALU.mult), reads=[pS, bg], writes=[R["fp"]])
        op('act', lambda e: e.activation(R["t"][:, 0:N], R["fp"][:, 0:N], AF.Exp), reads=[R["fp"]], writes=[R["t"]])
        op('act', lambda e: e.activation(R["lf"][:, 0:N], R["t"][:, 0:N], AF.Ln, bias=ONE4[:, :], scale=1.0), reads=[R["t"]], writes=[R["lf"]])
        op('dve', lambda e: e.tensor_scalar(R["lf"][:, 0:N], R["lf"][:, 0:N], -1.0, None, ALU.mult), reads=[R["lf"]], writes=[R["lf"]])

    ONE4 = sb([4, 1], F32, "one4")
    op('dve', lambda e: e.memset(ONE4[:], 1.0), writes=[ONE4])

    def conv_silu(N, sample):
        step = 4 if sample else 1
        for fc in range(16):
            if fc % 2 == 0:
                p = nextps()
            q = fc % 2
            for j in range(4):
                if sample:
                    rhs = qkpre[:, fc, j:j + 4 * N:4]
                else:
                    rhs = qkpre[:, fc, j:j + N]
                op('pe', lambda e, j=j, fc=fc, q=q, p=p, rhs=rhs: e.matmul(p[:, q * NT:q * NT + N], diagW[:, j, fc, :], rhs, start=(j == 0), stop=(j == 3)),
                   reads=[diagW, qkpre], writes=[p])
            op('act', lambda e, fc=fc, q=q, p=p: e.activation(qkT[:, fc, 0:N], p[:, q * NT:q * NT + N], AF.Silu, bias=wcT[:, fc, 4:5], scale=1.0),
               reads=[p, wcT], writes=[qkT])

    def gate_rows_prompt(N):
        op('dve', lambda e: e.tensor_tensor_scan(R["B"][:, 0:N], R["one"][:, 0:N], R["lf"][:, 0:N], rcol["Bprev"][:, 0:1], ALU.mult, ALU.add),
           reads=[R["one"], R["lf"], rcol["Bprev"]], writes=[R["B"]])
        op('dve', lambda e: e.tensor_tensor(R["a"][:, 0:N], R["li"][:, 0:N], R["B"][:, 0:N], ALU.subtract), reads=[R["li"], R["B"]], writes=[R["a"]])
        op('dve', lambda e: e.tensor_tensor_scan(R["g"][:, 0:N], R["a"][:, 0:N], R["a"][:, 0:N], rcol["gprev"][:, 0:1], ALU.max, ALU.max),
           reads=[R["a"], rcol["gprev"]], writes=[R["g"]])

    def mlstm_chunk(b, c0, first_of_tile):
        if first_of_tile:
            gp = rcol["gprev"][:, 0:1]
            gpk = rcol["gprev"]
        else:
            gp = R["g"][:, c0 - 1:c0]
            gpk = R["g"]
        op('dve', lambda e: e.tensor_scalar(rcol["nb"][:, 0:1], gp, -1.0, -LN16, ALU.mult, ALU.add), reads=[gpk], writes=[rcol["nb"]])
        op('dve', lambda e: e.tensor_scalar(rcol["nb"][:, 1:2], gp, -1.0, None, ALU.mult), reads=[gpk], writes=[rcol["nb"]])
        op('act', lambda e: e.activation(R["wk"][:, 0:128], R["a"][:, c0:c0 + 128], AF.Exp, bias=rcol["nb"][:, 0:1], scale=1.0),
           reads=[R["a"], rcol["nb"]], writes=[R["wk"]])
        op('act', lambda e: e.activation(R["e"][:, 0:128], R["B"][:, c0:c0 + 128], AF.Exp, bias=rcol["nb"][:, 1:2], scale=-1.0),
           reads=[R["B"], rcol["nb"]], writes=[R["e"]])
        op('dve', lambda e: e.tensor_tensor(rcol["dec"][:, 0:1], gp, R["g"][:, c0 + 127:c0 + 128], ALU.subtract), reads=[gpk, R["g"]], writes=[rcol["dec"]])
        op('act', lambda e: e.activation(rcol["dec"][:, 0:1], rcol["dec"][:, 0:1], AF.Exp), reads=[rcol["dec"]], writes=[rcol["dec"]])
        op('pe', lambda e: e.transpose(pS[:, 0:4], R["wk"][:, 0:128], ident_f[0:4, 0:4]), reads=[R["wk"], ident_f], writes=[pS])
        op('pe', lambda e: e.transpose(pS[:, 4:8], R["e"][:, 0:128], ident_f[0:4, 0:4]), reads=[R["e"], ident_f], writes=[pS])
        op('dve', lambda e: e.tensor_scalar(rcol["t"][:, 0:4], ident_f[0:4, 0:4], rcol["dec"][:, 0:1], None, ALU.mult), reads=[ident_f, rcol["dec"]], writes=[rcol["t"]])
        op('pe', lambda e: e.matmul(pS[:, 8:12], ones_f[0:4, :], rcol["t"][:, 0:4], start=True, stop=True), reads=[ones_f, rcol["t"]], writes=[pS])
        op('act', lambda e: e.activation(cols[:, 0:12], pS[:, 0:12], AF.Copy), reads=[pS], writes=[cols])
        fq = lambda h, kc: 2 * h + kc
        fk = lambda h, kc: 8 + 2 * h + kc
        pst = nextps()
        for h in range(4):
            for kc in range(2):
                op('pe', lambda e, kc=kc, h=h: e.matmul(pst[:, h * 128:(h + 1) * 128], qkT[:, fk(h, kc), c0:c0 + 128], qkT[:, fq(h, kc), c0:c0 + 128], start=(kc == 0), stop=(kc == 1)),
                   reads=[qkT], writes=[pst])
        for h in range(4):
            op('dve', lambda e, h=h: e.scalar_tensor_tensor(PpTv[h], pst[:, h * 128:(h + 1) * 128], cols[:, h:h + 1], maskB[:], ALU.mult, ALU.mult),
               reads=[pst, cols, maskB], writes=[PpTt[h]])
        for h in range(4):
            for kc in range(2):
                op('pe', lambda e, kc=kc, h=h: e.transpose(pT[:, h * 256 + kc * 128:h * 256 + (kc + 1) * 128], qkT[:, fk(h, kc), c0:c0 + 128], ident_b[:]), reads=[qkT, ident_b], writes=[pT])
        for h in range(4):
            op('act', lambda e, h=h: e.activation(Kpv[h], pT[:, h * 256:(h + 1) * 256], AF.Identity, scale=cols[:, h:h + 1]), reads=[pT, cols], writes=[Kpt[h]])
        for h in range(4):
            op('pool', lambda e, h=h: e.tensor_scalar(Cst[:, h, :, :], Cst[:, h, :, :], cols[:, 8 + h:9 + h], 0.0, ALU.mult, ALU.add), reads=[Cst, cols], writes=[Cst])
        nb_ = [(pmix[0], 0), (pmix[0], 512), (pmix[1], 0), (pmix[1], 512)]
        for h in range(4):
            pm_, cb = nb_[h]
            for kc in range(2):
                op('pe', lambda e, kc=kc, h=h, pm_=pm_, cb=cb: e.matmul(pm_[:, cb:cb + 257], qkT[:, fq(h, kc), c0:c0 + 128], Cbf[:, h, kc, :], start=(kc == 0), stop=False),
                   reads=[qkT, Cbf], writes=[pm_.name])
            op('pe', lambda e, h=h, pm_=pm_, cb=cb: e.matmul(pm_[:, cb:cb + 257], PpTv[h], vext[b][:, h, :], start=False, stop=True), reads=[PpTt[h], vext[b]], writes=[pm_.name])
        for h in range(4):
            pm_, cb = nb_[h]
            op('act', lambda e, h=h, pm_=pm_, cb=cb: e.activation(cols[:, 12 + h:13 + h], pm_[:, cb + 256:cb + 257], AF.Abs), reads=[pm_.name], writes=[('c12', h)])
        op('dve', lambda e: e.tensor_tensor(cols[:, 12:16], cols[:, 12:16], cols[:, 4:8], ALU.max), reads=[('c12', 0), ('c12', 1), ('c12', 2), ('c12', 3), cols], writes=[('c12', 0), ('c12', 1), ('c12', 2), ('c12', 3)])
        op('dve', lambda e: e.reciprocal(cols[:, 12:16], cols[:, 12:16]), reads=[('c12', 0), ('c12', 1), ('c12', 2), ('c12', 3)], writes=[('c12', 0), ('c12', 1), ('c12', 2), ('c12', 3)])
        for h in range(4):
            pm_, cb = nb_[h]
            op('act', lambda e, h=h, pm_=pm_, cb=cb: e.activation(hhm[b][:, h * 256:(h + 1) * 256], pm_[:, cb:cb + 256], AF.Identity, scale=cols[:, 12 + h:13 + h]),
               reads=[pm_.name, ('c12', h)], writes=[hhm[b]])
        for h in range(4):
            for kc in range(2):
                pu = nextps()
                op('pe', lambda e, kc=kc, pu=pu, h=h: e.matmul(pu[:, 0:257], Kpv[h][:, kc * 128:(kc + 1) * 128], vext[b][:, h, :], start=True, stop=True),
                   reads=[Kpt[h], vext[b]], writes=[pu])
                op('dve', lambda e, kc=kc, pu=pu, h=h: e.scalar_tensor_tensor(Cst[:, h, kc, :], pu[:, 0:257], cols[:, 8 + h:9 + h], Cst[:, h, kc, :], ALU.mult, ALU.add),
                   reads=[pu, Cst, cols], writes=[Cst])
        for h in range(4):
            op('pool', lambda e, h=h: e.tensor_copy(Cbf[:, h, :, :], Cst[:, h, :, :]), reads=[Cst], writes=[Cbf])

    def gated_T(b, c0, P):
        hsrc = hhm[b]
        for kc in range(8):
            op('pe', lambda e, kc=kc: e.transpose(pT[:, kc * 128:kc * 128 + P], hsrc[0:P, kc * 128:(kc + 1) * 128], ident_b[0:P, 0:P]), reads=[hsrc, ident_b], writes=[pT])
        op('dve', lambda e: e.tensor_tensor(gatedT[:, :, 0:P], pT[:].rearrange("p (a b) -> p a b", b=128)[:, :, 0:P], sigT[:, :, c0:c0 + P], ALU.mult),
           reads=[pT, sigT], writes=[gatedT])

    def wout_block(b, P, wts):
        for half in range(2):
            for kc in range(8):
                op('pe', lambda e, kc=kc, half=half: e.matmul(pmix[b][0:P, half * 512:(half + 1) * 512], gatedT[:, kc, 0:P], wts[half][:, kc, :], start=(kc == 0), stop=(kc == 7)),
                   reads=[gatedT, wts[half]], writes=[pmix[b].name])

    def kv_proj(blocks, tok0, sample, after_norm=None):
        for b, (col0, P) in enumerate(blocks):
            norm_mod_T(xblk[b], P, "gmkv", "shkv", col0)
        if after_norm:
            after_norm()
        for s in range(12):
            wt = wslab('w_kv', s)
            for b, (col0, P) in enumerate(blocks):
                p = nextps()
                for kc in range(8):
                    op('pe', lambda e, kc=kc, col0=col0, P=P, p=p, wt=wt: e.matmul(p[0:P, :], hT[:, kc, col0:col0 + P], wt[:, kc, :], start=(kc == 0), stop=(kc == 7)),
                       reads=[hT, wt], writes=[p])
                kvrot['b'] += 1
                kvbf = kvbfs[kvrot['b'] % 2]
                op('act', lambda e, P=P, p=p, kvbf=kvbf: e.activation(kvbf[0:P, :], p[0:P, :], AF.Copy), reads=[p], writes=[kvbf])
                kvsel, g, hf = s // 6, (s % 6) // 2, s % 2
                kvrot['t'] += 1
                kvtok = kvtoks[kvrot['t'] % 2]
                if sample:
                    dma('sp', kvscr_s[:, s * 512:(s + 1) * 512], kvbf[0:P, :], reads=[kvbf], writes=[('dram', 'kvscr_s')], chan=kvbf)
                    dst = (sko if kvsel == 0 else svo)[g]
                    op('dve', lambda e, P=P, p=p, kvtok=kvtok: e.tensor_copy(kvtok[0:P, :], p[0:P, :]), reads=[p], writes=[kvtok])
                    dma('sp', dst[:, hf * 512:(hf + 1) * 512], kvtok[0:P, :], reads=[kvtok], chan=kvtok)
                else:
                    t0 = tok0 + col0
                    dma('sp', kvscr[t0:t0 + P, s * 512:(s + 1) * 512], kvbf[0:P, :], reads=[kvbf], writes=[('dram', 'kvscr', t0 // 128)], chan=kvbf)
                    r = min(GROUPS[g][0], T)
                    if t0 >= T - r:
                        dst = (pko if kvsel == 0 else pvo)[g]
                        op('dve', lambda e, P=P, p=p, kvtok=kvtok: e.tensor_copy(kvtok[0:P, :], p[0:P, :]), reads=[p], writes=[kvtok])
                        dma('sp', dst[t0 - (T - r):t0 - (T - r) + P, hf * 512:(hf + 1) * 512], kvtok[0:P, :], reads=[kvtok], chan=kvtok)

    SCALE = float(128 ** -0.5)

    def attn_load(st):
        A_ = st['A']
        nq, nh, rd, hq = st['nq'], st['nh'], st['rd'], st.get('hq', 'sp')
        if nh > 0:
            dma(hq, A_["Khist"][128 - nh:128, :], st['khist'], reads=rd, writes=[A_["Khist"]], chan=A_["Khist"])
            dma(hq, A_["Vh"][128 - nh:128, :, 0:128], st['vhist'].rearrange("n (h d) -> n h d", d=128), reads=rd, writes=[A_["Vh"]], chan=A_["Vh"])
        dma('sp', A_["Kcur"][0:nq, :], st['kcur'], reads=rd, writes=[A_["Kcur"]], chan=A_["Kcur"])
        dma('sp', A_["Vc"][0:nq, :, 0:128], st['vcur'].rearrange("n (h d) -> n h d", d=128), reads=rd, writes=[A_["Vc"]], chan=A_["Vc"])

    def attn_compute(st):
        A_ = st['A']
        qT3, nq, nh, odst, wr = st['q3'], st['nq'], st['nh'], st['odst'], st['wr']
        hb = 4 if nq > 64 else 8
        Khist, Kcur, Vh, Vc, KTh, KTc, Eh, Ec, Osb = (A_[k_] for k_ in ("Khist", "Kcur", "Vh", "Vc", "KTh", "KTc", "Eh", "Ec", "Osb"))
        if nh > 0:
            for h in range(8):
                op('pe', lambda e, h=h: e.transpose(pT[:, h * 128:(h + 1) * 128], Khist[:, h * 128:(h + 1) * 128], ident_b[:]), reads=[Khist, ident_b], writes=[pT])
            op('dve', lambda e: e.tensor_copy(KTh[:].rearrange("p a b -> p (a b)"), pT[:]), reads=[pT], writes=[KTh])
        pSb = pS[:].bitcast(BF16)
        for h in range(8):
            op('pe', lambda e, h=h: e.transpose(pSb[:, h * 128:h * 128 + nq], Kcur[0:nq, h * 128:(h + 1) * 128], ident_b[0:nq, 0:nq]), reads=[Kcur, ident_b], writes=[pS])
        op('act', lambda e: e.activation(KTc[:, :, 0:nq], pSb.rearrange("p (a b) -> p a b", b=128)[:, :, 0:nq], AF.Copy), reads=[pS], writes=[KTc])
        for h0 in range(0, 8, hb):
            if nh > 0:
                p = nextps()
                for h in range(h0, h0 + hb):
                    op('pe', lambda e, h=h, p=p: e.matmul(p[:, (h - h0) * nq:(h - h0 + 1) * nq], KTh[:, h, :], qT3[:, h, :], start=True, stop=True),
                       reads=[KTh, qT3.tensor.name], writes=[p])
                op('act', lambda e, p=p: e.activation(Eh[:, h0:h0 + hb, 0:nq], p[:, 0:hb * nq].rearrange("p (a b) -> p a b", b=nq), AF.Exp, scale=SCALE),
                   reads=[p], writes=[Eh])
                op('dve', lambda e: e.scalar_tensor_tensor(Eh[:, h0:h0 + hb, 0:nq], Eh[:, h0:h0 + hb, 0:nq], vtab[:, nh:nh + 1],
                                                            maskA[:, 0:nq].unsqueeze(1).broadcast_to([128, hb, nq]), ALU.mult, ALU.mult),
                   reads=[Eh, vtab, maskA], writes=[Eh])
            p = nextps()
            for h in range(h0, h0 + hb):
                op('pe', lambda e, h=h, p=p: e.matmul(p[0:nq, (h - h0) * nq:(h - h0 + 1) * nq], KTc[:, h, 0:nq], qT3[:, h, :], start=True, stop=True),
                   reads=[KTc, qT3.tensor.name], writes=[p])
            op('act', lambda e, p=p: e.activation(Ec[0:nq, h0:h0 + hb, 0:nq], p[0:nq, 0:hb * nq].rearrange("p (a b) -> p a b", b=nq), AF.Exp, scale=SCALE),
               reads=[p], writes=[Ec])
            op('dve', lambda e: e.tensor_tensor(Ec[0:nq, h0:h0 + hb, 0:nq], Ec[0:nq, h0:h0 + hb, 0:nq],
                                                maskB[0:nq, 0:nq].unsqueeze(1).broadcast_to([nq, hb, nq]), ALU.mult),
               reads=[Ec, maskB], writes=[Ec])
        for h in range(8):
            pm_ = pmix[0] if h < 6 else pmix[1]
            cb = (512 if 3 <= h < 6 else 0) + (h % 3) * 129
            if nh > 0:
                op('pe', lambda e, h=h, pm_=pm_, cb=cb: e.matmul(pm_[0:nq, cb:cb + 129], Eh[:, h, 0:nq], Vh[:, h, :], start=True, stop=False),
                   reads=[Eh, Vh], writes=[pm_.name])
            op('pe', lambda e, h=h, pm_=pm_, cb=cb: e.matmul(pm_[0:nq, cb:cb + 129], Ec[0:nq, h, 0:nq], Vc[0:nq, h, :], start=(nh == 0), stop=True),
               reads=[Ec, Vc], writes=[pm_.name])
        op('act', lambda e: e.activation(Osb[0:nq, 0:387], pmix[0][0:nq, 0:387], AF.Copy), reads=[pmix[0].name], writes=[Osb])
        op('act', lambda e: e.activation(Osb[0:nq, 387:774], pmix[0][0:nq, 512:899], AF.Copy), reads=[pmix[0].name], writes=[Osb])
        op('act', lambda e: e.activation(Osb[0:nq, 774:1032], pmix[1][0:nq, 0:258], AF.Copy), reads=[pmix[1].name], writes=[Osb])
        dma('sp', odst, Osb[0:nq, :], reads=[Osb], writes=wr, chan=Osb)

    def run_steps(steps, pieces=None):
        pieces = list(pieces or [])
        for st in steps:
            st['A'] = ASET[astep['i'] % 2]
            astep['i'] += 1
        if steps:
            attn_load(steps[0])
        for i_, st in enumerate(steps):
            if i_ + 1 < len(steps):
                attn_load(steps[i_ + 1])
            attn_compute(st)
            if pieces and i_ % 2 == 1:
                pieces.pop(0)()
        while pieces:
            pieces.pop(0)()

    def attn_combine(P, osrc3, rd, b, wts):
        dma('sp', Og[0:P, :, :], osrc3.rearrange("g n c -> n g c"), reads=rd, writes=[Og], chan=Og)
        op('dve', lambda e: e.tensor_tensor(Og[0:P, 0, :], Og[0:P, 0, :], Og[0:P, 1, :], ALU.add), reads=[Og], writes=[Og])
        op('dve', lambda e: e.tensor_tensor(Og[0:P, 0, :], Og[0:P, 0, :], Og[0:P, 2, :], ALU.add), reads=[Og], writes=[Og])
        o3 = Og[0:P, 0, :].rearrange("p (h c) -> p h c", c=129)
        op('dve', lambda e: e.reciprocal(cols[0:P, 0:8], o3[:, :, 128]), reads=[Og], writes=[cols])
        op('dve', lambda e: e.tensor_tensor(hh[0:P, :].rearrange("p (h c) -> p h c", c=128), o3[:, :, 0:128],
                                            cols[0:P, 0:8].unsqueeze(2).broadcast_to([P, 8, 128]), ALU.mult), reads=[Og, cols], writes=[hh])
        for kc in range(8):
            op('pe', lambda e, kc=kc: e.transpose(pT[:, kc * 128:kc * 128 + P], hh[0:P, kc * 128:(kc + 1) * 128], ident_b[0:P, 0:P]), reads=[hh, ident_b], writes=[pT])
        op('act', lambda e: e.activation(gatedT[:, :, 0:P], pT[:].rearrange("p (a b) -> p a b", b=128)[:, :, 0:P], AF.Copy), reads=[pT], writes=[gatedT])
        wout_block(b, P, wts)

    def q_proj(N):
        for s in range(6):
            wt = wslab('w_q', s)
            for half in range(2):
                p = nextps()
                for q in range(2):
                    cc = half * 2 + q
                    for kc in range(8):
                        op('pe', lambda e, kc=kc, cc=cc, q=q, p=p, wt=wt: e.matmul(
                            p[:, q * NT:q * NT + N], wt[:, kc, cc * 128:(cc + 1) * 128], hT[:, kc, 0:N], start=(kc == 0), stop=(kc == 7)),
                            reads=[hT, wt], writes=[p])
                fc = s * 4 + half * 2
                pin = p[:].rearrange("p (a b) -> p a b", b=NT)[:, :, 0:N]
                op('act', lambda e, fc=fc, pin=pin: e.activation(QT[:, fc:fc + 2, 0:N], pin, AF.Copy), reads=[p], writes=[QT])

    blocks_p = [(0, 128), (128, 128)]
    load_state = {'mods': None}

    def ensure_mods(layer, sample):
        if load_state['mods'] != (layer, sample):
            load_mods(layer, sample)
            load_state['mods'] = (layer, sample)

    def front_pieces(ti):
        tok0 = ti * NT
        P_ = []

        def nrm(b):
            def f():
                if b == 0:
                    load_gs(0, 'mix', False)
                dma('sp', xstage[:], xp[tok0 + b * 128:tok0 + (b + 1) * 128, :], writes=[xstage], chan=xstage)
                norm_mod_T(xstage, 128, "gm1", "sh1", b * 128)
                if b == 1 and ti > 0:
                    load_gs(1, 'mlp', False)
            return f
        P_.append(nrm(0))
        P_.append(nrm(1))
        for part in ('qk', 'v', 'o', 'g'):
            P_.append(lambda part=part: w_in_proj(NT, blocks_p, False, parts=(part,)))

        def pconv_piece():
            for s in range(4):
                wt = wslab('w_in', s)
                p = nextps()
                for kc in range(8):
                    op('pe', lambda e, kc=kc, p=p, wt=wt: e.matmul(p[0:3, :], hT[:, kc, NT - 3:NT], wt[:, kc, :], start=(kc == 0), stop=(kc == 7)), reads=[hT, wt], writes=[p])
                kt = kvtoks[s % 2]
                op('act', lambda e, p=p, kt=kt: e.activation(kt[0:3, :], p[0:3, :], AF.Copy), reads=[p], writes=[kt])
                dma('sp', pconv[:, s * 512:(s + 1) * 512], kt[0:3, :], reads=[kt], chan=kt)
        if ti == ntile - 1:
            P_.append(pconv_piece)

        def conv_piece():
            conv_silu(NT, False)
            op('dve', lambda e: e.tensor_copy(qkpre[:, :, 0:3], qkpre[:, :, NT:NT + 3]), reads=[qkpre], writes=[qkpre])
            gate_rows_prompt(NT)
        P_.append(conv_piece)
        P_.append(lambda: mlstm_chunk(0, 0, True))

        def chunk1():
            mlstm_chunk(1, 128, False)
            op('dve', lambda e: e.tensor_copy(rcol["Bprev"][:, 0:1], R["B"][:, NT - 1:NT]), reads=[R["B"]], writes=[rcol["Bprev"]])
            op('dve', lambda e: e.tensor_copy(rcol["gprev"][:, 0:1], R["g"][:, NT - 1:NT]), reads=[R["g"]], writes=[rcol["gprev"]])
        P_.append(chunk1)
        return P_

    def back(ti):
        tok0 = ti * NT
        load_gg(0, 'mix', False)
        for b in range(2):
            dma('sp', xblk[b][:], xp[tok0 + b * 128:tok0 + (b + 1) * 128, :], writes=[xblk[b].name], chan=xblk[b])
        wts = [wslab('w_out', 0), wslab('w_out', 1)]
        for b in range(2):
            gated_T(b, b * 128, 128)
            wout_block(b, 128, wts)
            post_norm_res(pmix[b], xblk[b], 128, "gg1")
        if ti == 0:
            load_gs(0, 'mlp', False)
        load_gg(0, 'mlp', False)
        mlp(0, blocks_p, NT, after_norm=lambda: load_gs(0, 'kv', False))
        kv_proj(blocks_p, tok0, False, after_norm=lambda: load_gs(1, 'mix', False))

    def layer1_tile(ti, pieces):
        tok0 = ti * NT
        load_gg(1, 'mix', False)
        for b in range(2):
            norm_mod_T(xblk[b], 128, "gm1", "sh1", b * 128)
        if not pieces:
            load_gs(1, 'mlp', False)
        q_proj(NT)
        steps = []
        okeys = []
        for g, (win, dil) in enumerate(GROUPS):
            nq = 128 if dil == 1 else NT // dil
            nsteps = NT // (nq * dil)
            for r in range(dil):
                for st in range(nsteps):
                    tq0 = tok0 + st * 128 * (1 if dil == 1 else 0) + r
                    j0 = (tq0 - r) // dil
                    nh = min(128, j0)
                    kb, vb = g * D, 3 * D + g * D
                    rows_c = slice(tq0, tq0 + (nq - 1) * dil + 1, dil)
                    kh = vh_ = None
                    if nh > 0:
                        th0 = tq0 - nh * dil
                        rows_h = slice(th0, th0 + (nh - 1) * dil + 1, dil)
                        kh = kvscr[rows_h, kb:kb + D]
                        vh_ = kvscr[rows_h, vb:vb + D]
                    if dil == 1:
                        q3 = QT[:, g * 8:(g + 1) * 8, st * 128:st * 128 + 128]
                    else:
                        q3 = QT[:, g * 8:(g + 1) * 8, r:NT:dil]
                    lo_blk = max(0, (tq0 - 128 * dil)) // 128
                    rd = [('dram', 'kvscr', bb) for bb in range(lo_blk, (tok0 + NT) // 128)]
                    key = ('dram', 'oscr', ti, len(steps))
                    okeys.append(key)
                    steps.append(dict(q3=q3, nq=nq, nh=nh, khist=kh, vhist=vh_, kcur=kvscr[rows_c, kb:kb + D], vcur=kvscr[rows_c, vb:vb + D],
                                      odst=oscr[g, rows_c, :], rd=rd, wr=[key]))
        run_steps(steps, pieces)
        wts = [wslab('w_o', 0), wslab('w_o', 1)]
        for b in range(2):
            t0 = tok0 + b * 128
            attn_combine(128, oscr[:, t0:t0 + 128, :], okeys, b, wts)
            post_norm_res(pmix[b], xblk[b], 128, "gg1")
        load_gg(1, 'mlp', False)
        mlp(1, blocks_p, NT, after_norm=(lambda: load_gs(0, 'mlp', False)) if ti < ntile - 1 else None)
        for b in range(2):
            t0 = tok0 + b * 128
            dma('sp', yp[t0:t0 + 128, :], xblk[b][:], reads=[xblk[b].name], chan=xblk[b])


    colsS = sb([4, 64], F32, "colsS")
    acols = sb([128, 32], F32, "acols")
    rhs32 = sb([4, 32], F32, "rhs32")
    Ktok = Khist

    def sample_tile():
        blocks_s = [(0, NS)]
        P = NS
        load_gs(0, 'mix', True)
        load_gg(0, 'mix', True)
        dma('sp', xblk[0][0:P, :], xs, writes=[xblk[0]], chan=xblk[0])
        norm_mod_T(xblk[0], P, "gm1", "sh1", 0)
        dma('sp', big[0:12, 0:2048], stconv.rearrange("i j c -> (i j) c"), writes=[big], chan=big)
        for fc in range(16):
            op('pe', lambda e, fc=fc: e.transpose(pS[:, fc * 12:fc * 12 + 12], big[0:12, fc * 128:(fc + 1) * 128], ident_f[0:12, 0:12]),
               reads=[big, ident_f], writes=[pS])
        op('act', lambda e: e.activation(qkpre[:, :, 0:16].rearrange("p f (i j) -> p f i j", j=4)[:, :, :, 0:3],
                                         pS[:, 0:192].rearrange("p (f i j) -> p f i j", i=4, j=3), AF.Copy), reads=[pS], writes=[qkpre])
        w_in_proj(P, blocks_s, True)
        dma('sp', sconvo[:, 0:2, :], stconv[:, 1:3, :], chan='sconv_copy')
        for s_ in range(4):
            wt = wslab('w_in', s_)
            p = nextps()
            for kc in range(8):
                op('pe', lambda e, kc=kc, p=p, wt=wt: e.matmul(p[0:P, :], hT[:, kc, 0:P], wt[:, kc, :], start=(kc == 0), stop=(kc == 7)), reads=[hT, wt], writes=[p])
            op('act', lambda e, p=p: e.activation(kvtok[0:P, :], p[0:P, :], AF.Copy), reads=[p], writes=[kvtok])
            dma('sp', sconvo[:, 2, s_ * 512:(s_ + 1) * 512], kvtok[0:P, :], reads=[kvtok], chan=kvtok)
        conv_silu(P, True)
        with nc.allow_non_contiguous_dma(reason="tiny"):
            dma('sp', R["B"][:, 0:4], stm.rearrange("i h -> h i"), writes=[R["B"]], chan=R["B"])
        op('dve', lambda e: e.tensor_tensor(R["a"][:, 0:4], R["lf"][:, 0:4], R["B"][:, 0:4], ALU.add), reads=[R["lf"], R["B"]], writes=[R["a"]])
        op('dve', lambda e: e.tensor_tensor(R["g"][:, 0:4], R["a"][:, 0:4], R["li"][:, 0:4], ALU.max), reads=[R["a"], R["li"]], writes=[R["g"]])
        with nc.allow_non_contiguous_dma(reason="tiny"):
            dma('sp', smo.rearrange("i h -> h i"), R["g"][:, 0:4], reads=[R["g"]], chan=R["g"])
        op('dve', lambda e: e.tensor_tensor(R["wk"][:, 0:4], R["a"][:, 0:4], R["g"][:, 0:4], ALU.subtract), reads=[R["a"], R["g"]], writes=[R["wk"]])
        op('act', lambda e: e.activation(R["wk"][:, 0:4], R["wk"][:, 0:4], AF.Exp), reads=[R["wk"]], writes=[R["wk"]])
        op('dve', lambda e: e.tensor_tensor(R["e"][:, 0:4], R["li"][:, 0:4], R["g"][:, 0:4], ALU.subtract), reads=[R["li"], R["g"]], writes=[R["e"]])
        op('act', lambda e: e.activation(R["e"][:, 0:4], R["e"][:, 0:4], AF.Exp), reads=[R["e"]], writes=[R["e"]])
        op('act', lambda e: e.activation(R["t"][:, 0:4], R["g"][:, 0:4], AF.Exp, scale=-1.0), reads=[R["g"]], writes=[R["t"]])
        for off, src in ((0, "wk"), (16, "t")):
            op('dve', lambda e, off=off, src=src: e.tensor_tensor(
                rhs32[:, off:off + 16].rearrange("p (h i) -> p h i", i=4),
                R[src][:, 0:4].unsqueeze(1).broadcast_to([4, 4, 4]),
                ident_f[0:4, 0:4].unsqueeze(2).broadcast_to([4, 4, 4]), ALU.mult), reads=[R[src], ident_f], writes=[rhs32])
        op('pe', lambda e: e.matmul(pS[:, 0:32], ones_f[0:4, :], rhs32[:, :], start=True, stop=True), reads=[ones_f, rhs32], writes=[pS])
        op('act', lambda e: e.activation(acols[:], pS[:, 0:32], AF.Copy), reads=[pS], writes=[acols])
        op('pe', lambda e: e.transpose(pS[0:4, 0:4], R["e"][:, 0:4], ident_f[0:4, 0:4]), reads=[R["e"], ident_f], writes=[pS])
        op('act', lambda e: e.activation(colsS[:, 0:4], pS[0:4, 0:4], AF.Copy, scale=1.0 / 16.0), reads=[pS], writes=[colsS])
        op('dve', lambda e: e.tensor_tensor(
            colsS[:, 16:32].rearrange("p (i h) -> p i h", h=4),
            colsS[:, 0:4].unsqueeze(1).broadcast_to([4, 4, 4]),
            ident_f[0:4, 0:4].unsqueeze(2).broadcast_to([4, 4, 4]), ALU.mult), reads=[colsS, ident_f], writes=[('colsS2',)])
        for j in range(8):
            op('pe', lambda e, j=j: e.transpose(pT[0:P, j * 128:(j + 1) * 128], qkT[:, 8 + j, 0:P], ident_b[:]), reads=[qkT, ident_b], writes=[pT])
        op('act', lambda e: e.activation(Ktok[0:P, :], pT[0:P, :], AF.Copy), reads=[pT], writes=[Ktok])
        for i in range(NS):
            dma('sp', Cst[:, :, :, 0:256], stC[i].rearrange("h (kc p) v -> p h kc v", p=128), writes=[Cst], chan=Cst)
            with nc.allow_non_contiguous_dma(reason="tiny"):
                dma('sp', Cst[:, :, :, 256], stn[i].rearrange("h (kc p) -> p h kc", p=128), writes=[Cst], chan=Cst)
            for h in range(4):
                op('dve', lambda e, h=h, i=i: e.tensor_scalar(Kp[0:P, :], Ktok[0:P, h * 256:(h + 1) * 256], colsS[:, 16 + i * 4 + h:17 + i * 4 + h], None, ALU.mult),
                   reads=[Ktok, ('colsS2',)], writes=[Kp])
                for kc in range(2):
                    pu = nextps()
                    op('pe', lambda e, kc=kc, pu=pu, h=h: e.matmul(pu[:, 0:257], Kp[0:P, kc * 128:(kc + 1) * 128], vext[0][0:P, h, :], start=True, stop=True),
                       reads=[Kp, vext[0]], writes=[pu])
                    op('dve', lambda e, kc=kc, pu=pu, h=h, i=i: e.scalar_tensor_tensor(Cst[:, h, kc, :], Cst[:, h, kc, :], acols[:, h * 4 + i:h * 4 + i + 1], pu[:, 0:257], ALU.mult, ALU.add),
                       reads=[pu, Cst, acols], writes=[Cst])
                    op('dve', lambda e, kc=kc, h=h: e.tensor_copy(Cbf[:, h, kc, :], Cst[:, h, kc, :]), reads=[Cst], writes=[Cbf])
                pn_ = nextps()
                for kc in range(2):
                    op('pe', lambda e, kc=kc, pn_=pn_, h=h, i=i: e.matmul(pn_[0:1, 0:257], qkT[:, 2 * h + kc, i:i + 1], Cbf[:, h, kc, :], start=(kc == 0), stop=(kc == 1)),
                       reads=[qkT, Cbf], writes=[pn_])
                op('act', lambda e, pn_=pn_: e.activation(cols[0:1, 12:13], pn_[0:1, 256:257], AF.Abs), reads=[pn_], writes=[('c12', 0)])
                op('dve', lambda e, h=h, i=i: e.tensor_tensor(cols[0:1, 12:13], cols[0:1, 12:13], acols[0:1, 16 + h * 4 + i:17 + h * 4 + i], ALU.max),
                   reads=[('c12', 0), acols], writes=[('c12', 0)])
                op('dve', lambda e: e.reciprocal(cols[0:1, 12:13], cols[0:1, 12:13]), reads=[('c12', 0)], writes=[('c12', 0)])
                op('act', lambda e, pn_=pn_, h=h: e.activation(hh[0:1, h * 256:(h + 1) * 256], pn_[0:1, 0:256], AF.Identity, scale=cols[0:1, 12:13]),
                   reads=[pn_, ('c12', 0)], writes=[hh])
            for h in range(4):
                for kc in range(2):
                    dma('sp', sCo[i, h, kc * 128:(kc + 1) * 128, :], Cst[:, h, kc, 0:256], reads=[Cst], chan=Cst)
            with nc.allow_non_contiguous_dma(reason="tiny"):
                dma('sp', sno[i].rearrange("h (kc p) -> p h kc", p=128), Cst[:, :, :, 256], reads=[Cst], chan=Cst)
            for kc in range(8):
                op('pe', lambda e, kc=kc, i=i: e.transpose(pT[:, kc * 128 + 2 * i:kc * 128 + 2 * i + 1], hh[0:1, kc * 128:(kc + 1) * 128], ident_b[0:1, 0:1]), reads=[hh, ident_b], writes=[pT])
        op('dve', lambda e: e.tensor_tensor(gatedT[:, :, 0:P], pT[:].rearrange("p (a b) -> p a b", b=128)[:, :, 0:2 * P:2], sigT[:, :, 0:P], ALU.mult),
           reads=[pT, sigT], writes=[gatedT])
        wts = [wslab('w_out', 0), wslab('w_out', 1)]
        wout_block(0, P, wts)
        post_norm_res(pmix[0], xblk[0], P, "gg1")
        load_gs(0, 'mlp', True)
        load_gg(0, 'mlp', True)
        mlp(0, blocks_s, P, after_norm=lambda: load_gs(0, 'kv', True))
        kv_proj(blocks_s, 0, True, after_norm=lambda: load_gs(1, 'mix', True))
        load_gg(1, 'mix', True)
        norm_mod_T(xblk[0], P, "gm1", "sh1", 0)
        load_gs(1, 'mlp', True)
        q_proj(P)
        steps = []
        for g, (win, dil) in enumerate(GROUPS):
            kb, vb = g * D, 3 * D + g * D
            for i in range(NS):
                steps.append(dict(q3=QT[:, g * 8:(g + 1) * 8, i:i + 1], nq=1, nh=128,
                                  khist=cache[('k', g)][i, 0:win:dil, :], vhist=cache[('v', g)][i, 0:win:dil, :],
                                  kcur=kvscr_s[i:i + 1, kb:kb + D], vcur=kvscr_s[i:i + 1, vb:vb + D],
                                  odst=oscr_s[g, i:i + 1, :], rd=[('dram', 'kvscr_s')], wr=[('dram', 'oscr_s', g, i)], hq='pool'))
        run_steps(steps)
        wts = [wslab('w_o', 0), wslab('w_o', 1)]
        attn_combine(P, oscr_s[:, 0:P, :], [('dram', 'oscr_s', g_, i_) for g_ in range(3) for i_ in range(NS)], 0, wts)
        post_norm_res(pmix[0], xblk[0], P, "gg1")
        load_gg(1, 'mlp', True)
        mlp(1, blocks_s, P)
        dma('sp', ys, xblk[0][0:P, :], reads=[xblk[0]], chan=xblk[0])

    import os as _os
    _nt = int(_os.environ.get("KDBG_TILES", ntile))
    nt_ = min(ntile, _nt)
    if nt_ > 0:
        for f_ in front_pieces(0):
            f_()
        back(0)
    for ti in range(nt_):
        layer1_tile(ti, front_pieces(ti + 1) if ti + 1 < nt_ else None)
        if ti + 1 < nt_:
            back(ti + 1)

    for h in range(4):
        for kc in range(2):
            dma('sp', pC[h, kc * 128:(kc + 1) * 128, :], Cst[:, h, kc, 0:256], reads=[Cst], chan=Cst)
    with nc.allow_non_contiguous_dma(reason="tiny"):
        dma('sp', pn.rearrange("h (kc p) -> p h kc", p=128), Cst[:, :, :, 256], reads=[Cst], chan=Cst)
    op('dve', lambda e: e.tensor_tensor(rcol["m"][:, 0:1], rcol["Bprev"][:, 0:1], rcol["gprev"][:, 0:1], ALU.add), reads=[rcol["Bprev"], rcol["gprev"]], writes=[rcol["m"]])
    dma('sp', pm, rcol["m"][:, 0:1], reads=[rcol["m"]], chan=rcol["m"])

    if do_samples and not _os.environ.get("KDBG_NOSAMP"):
        sample_tile()

    S.finish()
    if dry:
        return wreq
    return nc


def _consts():
    ik = np.arange(128)[:, None]
    iq = np.arange(128)[None, :]
    maskA = (ik >= iq).astype(np.float32)
    maskB = (ik <= iq).astype(np.float32)
    vtab = (np.arange(128)[:, None] >= (128 - np.arange(129)[None, :])).astype(np.float32)
    return np.eye(128, dtype=np.float32), maskA, maskB, vtab


def make_in_maps(inp, ncores, T):
    ident, maskA, maskB, vtab = _consts()
    maps = []
    f = lambda a: np.ascontiguousarray(np.asarray(a, dtype=np.float32))
    nb = inp['x_prompt'].shape[0]
    for c in range(ncores):
        b = c % nb
        sl = slice(NS * c, NS * c + NS)
        m = {
            "xp": f(inp['x_prompt'][b]), "xs": f(inp['x_sample'][sl, 0]),
            "c5": f(np.concatenate([inp['c_prompt'][b:b + 1], inp['c_sample'][sl]], 0)),
            "stC": f(inp['state_C'][0, sl]), "stn": f(inp['state_n'][0, sl]), "stm": f(inp['state_m'][0, sl]),
            "stconv": f(inp['state_conv'][0, sl]),
            "w_ada": f(inp['w_ada']), "b_ada": f(inp['b_ada']), "g_norm": f(inp['g_norm']),
            "w_mlp_up": f(inp['w_mlp_up']), "w_mlp_down": f(inp['w_mlp_down']), "w_a_in": f(inp['w_a_in'][0]),
            "b_a_gate": f(inp['b_a_gate'][0]),
            "wcb": f(np.concatenate([inp['w_a_conv'][0], inp['b_a_conv'][0][None]], 0)),
            "w_a_out": f(inp['w_a_out'][0]), "g_kv": f(inp['g_kv']), "w_ada_kv": f(inp['w_ada_kv']),
            "b_ada_kv": f(inp['b_ada_kv']), "w_kv": f(inp['w_kv']), "w_b_q": f(inp['w_b_q'][0]), "w_b_o": f(inp['w_b_o'][0]),
            "ident": ident, "maskA": maskA, "maskB": maskB, "vtab": vtab,
        }
        caches = ((inp['cache_k_g0'], inp['cache_v_g0']), (inp['cache_k_g1'], inp['cache_v_g1']),
                  (inp['cache_k_g2'], inp['cache_v_g2']))
        for g in range(3):
            m["ck%d" % g] = f(caches[g][0][sl]).reshape(NS, -1, D)
            m["cv%d" % g] = f(caches[g][1][sl]).reshape(NS, -1, D)
        maps.append(m)
    return maps


def assemble(results, nb, nsb, T):
    R0 = results
    ncores = len(R0)
    y_prompt = np.stack([R0[b]["yp"] for b in range(nb)])
    y_sample = np.concatenate([R0[c]["ys"] for c in range(ncores)])[:, None, :]
    p_C = np.stack([R0[b]["pC"] for b in range(nb)])[None]
    p_n = np.stack([R0[b]["pn"] for b in range(nb)])[None]
    p_m = np.stack([R0[b]["pm"][:, 0] for b in range(nb)])[None]
    p_conv = np.stack([R0[b]["pconv"] for b in range(nb)])[None]
    s_C = np.concatenate([R0[c]["sC"] for c in range(ncores)])[None]
    s_n = np.concatenate([R0[c]["sn"] for c in range(ncores)])[None]
    s_m = np.concatenate([R0[c]["sm"] for c in range(ncores)])[None]
    s_conv = np.concatenate([R0[c]["sconv"] for c in range(ncores)])[None]
    outs = [y_prompt, y_sample, p_C, p_n, p_m, p_conv, s_C, s_n, s_m, s_conv]
    for g in range(3):
        for kv in ("pk", "pv"):
            a = np.stack([R0[b]["%s%d" % (kv, g)] for b in range(nb)])
            outs.append(a.reshape(nb, a.shape[1], 8, 128))
    for g in range(3):
        for kv in ("sk", "sv"):
            a = np.concatenate([R0[c]["%s%d" % (kv, g)] for c in range(ncores)])
            outs.append(a.reshape(a.shape[0], 1, 8, 128))
    return tuple(np.ascontiguousarray(o, dtype=np.float32) for o in outs)


def kernel(**inputs):
    inp = {k: np.asarray(v) for k, v in inputs.items()}
    T = inp['x_prompt'].shape[1]
    nb = inp['x_prompt'].shape[0]
    ncores = 8
    nc = build(T)
    maps = make_in_maps(inp, ncores, T)
    res = run_bass_kernel_spmd(nc, maps, core_ids=list(range(ncores)))
    return assemble(res.results, nb, inp['x_sample'].shape[0], T)
```

```python
import numpy as np
import concourse.bass as bass
import concourse.mybir as mybir
from concourse.bass_utils import run_bass_kernel_spmd

F32 = mybir.dt.float32
BF16 = mybir.dt.bfloat16
AF = mybir.ActivationFunctionType
ALU = mybir.AluOpType

D = 1024
KC = 8
NT = 256
DFF = 4096
EPS = 1e-6
NS = 4
GROUPS = ((128, 1), (512, 4), (2048, 16))
LN16 = float(np.log(16.0))


class Sch:
    def __init__(s, nc, dry=False):
        s.nc = nc
        s.dry = dry
        s.E = {'pe': nc.tensor, 'act': nc.scalar, 'dve': nc.vector, 'pool': nc.gpsimd, 'sp': nc.sync}
        s.sem = {e: nc.alloc_semaphore(name='sem_' + e) for e in s.E}
        s.cnt = {e: 0 for e in s.E}
        s.waited = {e: {} for e in s.E}
        s.lastw = {}
        s.readers = {}
        s.dsem = {}
        s.excl = set()

    @staticmethod
    def _k(r):
        if isinstance(r, (str, tuple)):
            return r
        if hasattr(r, 'tensor'):
            return r.tensor.name
        return r.name

    def _split(s, reads, writes):
        reads = [s._k(r) for r in reads]
        writes = [s._k(w) for w in writes]
        ex = [r for r in reads if r in s.excl and r not in writes]
        return [r for r in reads if r not in ex], writes + ex

    def _tokens(s, reads, writes):
        reads, writes = s._split(reads, writes)
        toks = []
        for r in reads:
            t = s.lastw.get(r)
            if t:
                toks.append(t + (True,))
        for w in writes:
            t = s.lastw.get(w)
            if t:
                toks.append(t + (False,))
            toks.extend(t_ + (False,) for t_ in s.readers.get(w, ()))
        return toks

    def _wait(s, e, toks):
        need = {}
        for (key, sem, val, is_raw) in toks:
            if key == e and e == 'pe':
                continue
            if key == e and e in ('act', 'dve') and not is_raw:
                continue
            if s.waited[e].get(key, 0) >= val:
                continue
            if need.get(key, (None, 0))[1] < val:
                need[key] = (sem, val)
        for key, (sem, val) in need.items():
            s.E[e].wait_ge(sem, val)
            s.waited[e][key] = val

    def _commit(s, tok, reads, writes):
        reads, writes = s._split(reads, writes)
        for w in writes:
            s.lastw[w] = tok
            s.readers[w] = []
        for r in reads:
            if r not in writes:
                lst = s.readers.setdefault(r, [])
                lst[:] = [t for t in lst if t[0] != tok[0]]
                lst.append(tok)

    def op(s, e, fn, reads=(), writes=()):
        if s.dry:
            return
        s._wait(e, s._tokens(reads, writes))
        inst = fn(s.E[e])
        s.cnt[e] += 1
        inst.then_inc(s.sem[e], 1)
        s._commit((e, s.sem[e], s.cnt[e]), reads, writes)

    def dma(s, q, out, in_, reads=(), writes=(), chan=None, **kw):
        if s.dry:
            return
        if q == 'sp' and type(out.tensor).__name__.startswith('DRam'):
            q = 'pool'
        s._wait(q, s._tokens(reads, writes))
        inst = s.E[q].dma_start(out, in_, **kw)
        chan = s._k(chan)
        if chan not in s.dsem:
            s.dsem[chan] = [s.nc.alloc_semaphore(name='dsem_%d' % len(s.dsem)), 0]
        ds = s.dsem[chan]
        ds[1] += 16
        inst.then_inc(ds[0], 16)
        s._commit((('dma', chan), ds[0], ds[1]), reads, writes)

    def finish(s):
        if s.dry:
            return
        for chan, (sem, val) in s.dsem.items():
            if val:
                s.E['sp'].wait_ge(sem, val)
        for e in ('pe', 'act', 'dve', 'pool'):
            if s.cnt[e]:
                s.E['sp'].wait_ge(s.sem[e], s.cnt[e])


def build(T, do_samples=True, wseq=None):
    assert T % NT == 0
    if wseq is None:
        wseq = build(T, do_samples, wseq=[])
    dry = (len(wseq) == 0)
    nc = bass.Bass("TRN2", target_bir_lowering=False)
    S = Sch(nc, dry=dry)
    wreq = []
    ntile = T // NT

    def din(name, shape):
        return nc.dram_tensor(name, list(shape), F32, kind="ExternalInput").ap()

    def dout(name, shape):
        return nc.dram_tensor(name, list(shape), F32, kind="ExternalOutput").ap()

    def dscr(name, shape, dt=F32):
        return nc.dram_tensor(name, list(shape), dt, kind="Internal").ap()

    xp = din("xp", [T, D])
    xs = din("xs", [NS, D])
    c5 = din("c5", [1 + NS, D])
    stC = din("stC", [NS, 4, 256, 256])
    stn = din("stn", [NS, 4, 256])
    stm = din("stm", [NS, 4])
    stconv = din("stconv", [NS, 3, 2048])
    cache = {}
    for g, (win, dil) in enumerate(GROUPS):
        cache[('k', g)] = din("ck%d" % g, [NS, win, D])
        cache[('v', g)] = din("cv%d" % g, [NS, win, D])
    w_ada = din("w_ada", [2, D, 6 * D])
    b_ada = din("b_ada", [2, 6 * D])
    g_norm = din("g_norm", [2, 4, D])
    w_up = din("w_mlp_up", [2, D, DFF])
    w_down = din("w_mlp_down", [2, DFF, D])
    w_in = din("w_a_in", [D, 4104])
    b_gate = din("b_a_gate", [8])
    wcb = din("wcb", [5, 2048])
    w_out = din("w_a_out", [D, D])
    g_kv = din("g_kv", [D])
    w_ada_kv = din("w_ada_kv", [D, 2 * D])
    b_ada_kv = din("b_ada_kv", [2 * D])
    w_kv = din("w_kv", [D, 6 * D])
    w_q = din("w_b_q", [D, 3 * D])
    w_o = din("w_b_o", [D, D])
    identd = din("ident", [128, 128])
    maskAd = din("maskA", [128, 128])
    maskBd = din("maskB", [128, 128])
    vtabd = din("vtab", [128, 129])

    yp = dout("yp", [T, D])
    ys = dout("ys", [NS, D])
    pC = dout("pC", [4, 256, 256])
    pn = dout("pn", [4, 256])
    pm = dout("pm", [4, 1])
    pconv = dout("pconv", [3, 2048])
    sCo = dout("sC", [NS, 4, 256, 256])
    sno = dout("sn", [NS, 4, 256])
    smo = dout("sm", [NS, 4])
    sconvo = dout("sconv", [NS, 3, 2048])
    pko, pvo, sko, svo = [], [], [], []
    for g, (win, dil) in enumerate(GROUPS):
        r = min(win, T)
        pko.append(dout("pk%d" % g, [r, D]))
        pvo.append(dout("pv%d" % g, [r, D]))
        sko.append(dout("sk%d" % g, [NS, D]))
        svo.append(dout("sv%d" % g, [NS, D]))

    modscr = dscr("modscr", [2, 1 + NS, 6 * D])
    modd = dscr("modd", [3, 3, 1 + NS, D])
    modkvscr = dscr("modkvscr", [1 + NS, 2 * D])
    kvscr = dscr("kvscr", [T, 6 * D], BF16)
    kvscr_s = dscr("kvscr_s", [NS, 6 * D], BF16)
    oscr = dscr("oscr", [3, T, 8 * 129])
    oscr_s = dscr("oscr_s", [3, NS, 8 * 129])

    _n = [0]

    def sb(shape, dt=F32, name=None):
        _n[0] += 1
        return nc.alloc_sbuf_tensor(name or ("t%d" % _n[0]), list(shape), dt)

    def ps(shape, dt=F32, name=None):
        _n[0] += 1
        t = nc.alloc_psum_tensor(name or ("p%d" % _n[0]), list(shape), dt)
        S.excl.add(t.name)
        return t

    pmix = [ps([128, 1024]), ps([128, 1024])]
    psAB = [ps([128, 512]), ps([128, 512])]
    pT = ps([128, 1024], BF16)
    pS = ps([128, 512])
    _ab = [0]

    def nextps():
        _ab[0] ^= 1
        return psAB[_ab[0]]

    ident_f = sb([128, 128])
    ident_b = sb([128, 128], BF16)
    maskA = sb([128, 128], BF16)
    maskB = sb([128, 128], BF16)
    vtab = sb([128, 129])
    ones_f = sb([128, 128])
    ones_b = sb([128, 128], BF16)
    NWS = 4
    wring = [sb([128, 8, 512], BF16, "wring%d" % i) for i in range(NWS)]
    xblk = [sb([128, D], F32, "xblk%d" % i) for i in range(2)]
    M0 = sb([128, D], F32, "modM0")
    M1 = sb([128, D], F32, "modM1")
    M2 = sb([128, D], F32, "modM2")
    mods = {"gm1": M0, "sh1": M1, "gg1": M2, "gm2": M0, "sh2": M1, "gg2": M2, "gmkv": M0, "shkv": M1}
    modtmp = M2
    hT = sb([128, 8, NT], BF16, "hT")

    tmpf = sb([128, D], F32, "tmpf")
    relut_v = tmpf[:, 0:2 * NT].rearrange("p (a b) -> p a b", b=NT)
    hn = sb([128, D], BF16, "hn")
    junk = hn
    colA = sb([128, 8], F32, "colA")
    qkpre = sb([128, 16, 3 + NT + 16], BF16, "qkpre")
    qkT = sb([128, 16, NT], BF16, "qkT")
    vext = [sb([128, 4, 257], BF16, "vext%d" % i) for i in range(2)]
    sigT = sb([128, 8, NT], BF16, "sigT")
    Cst = sb([128, 4, 2, 257], F32, "Cst")
    Cbf = sb([128, 4, 2, 257], BF16, "Cbf")
    Ctmp = None
    Kpt = [sb([128, 256], BF16, "Kp%d" % h_) for h_ in range(4)]
    PpTt = [sb([128, 128], BF16, "PpT%d" % h_) for h_ in range(4)]
    Kpv = [t_[:] for t_ in Kpt]
    PpTv = [t_[:] for t_ in PpTt]
    Kp = Kpt[0]
    hh = sb([128, D], BF16, "hh")
    hhm = [sb([128, D], BF16, "hhm%d" % i) for i in range(2)]
    xstage = sb([128, D], F32, "xstage")
    gatedT = sb([128, 8, 128], BF16, "gatedT")
    big = sb([128, 4096], F32, "big")
    hid = big[:].bitcast(BF16).rearrange("p (a b) -> p a b", b=NT)
    relut = None
    Wg = sb([128, 8, 8], BF16, "Wg")
    diagW = sb([128, 4, 16, 128], BF16, "diagW")
    wcT = sb([128, 16, 5], F32, "wcT")
    bg = sb([4, 2], F32, "bg")
    R = {k: sb([4, NT], F32, "row_" + k) for k in ("li", "fp", "lf", "B", "a", "g", "one")}
    R["wk"] = sb([4, 128], F32, "row_wk")
    R["e"] = sb([4, 128], F32, "row_e")
    R["t"] = R["fp"]
    rcol = {k: sb([4, 4], F32, "rcol_" + k) for k in ("Bprev", "gprev", "nb", "dec", "t", "m")}
    cols = sb([128, 16], F32, "cols")
    kvtoks = [sb([128, 512], F32, "kvtok%d" % i) for i in range(2)]
    kvbfs = [sb([128, 512], BF16, "kvbf%d" % i) for i in range(2)]
    kvtok = kvtoks[0]
    kvrot = {'t': 0, 'b': 0}
    QT = hid[:, 0:24, :]
    ASET = []
    for i_ in range(2):
        ASET.append(dict(
            Khist=sb([128, D], BF16, "Khist%d" % i_), Kcur=sb([128, D], BF16, "Kcur%d" % i_),
            Vh=sb([128, 8, 129], BF16, "Vh%d" % i_), Vc=sb([128, 8, 129], BF16, "Vc%d" % i_),
            KTh=sb([128, 8, 128], BF16, "KTh%d" % i_), KTc=sb([128, 8, 128], BF16, "KTc%d" % i_),
            Eh=sb([128, 8, 128], BF16, "Eh%d" % i_), Ec=sb([128, 8, 128], BF16, "Ec%d" % i_),
            Osb=sb([128, 8 * 129], F32, "Osb%d" % i_)))
    Khist = ASET[0]["Khist"]
    astep = {'i': 0}
    Og = big[:, 0:3096].rearrange("p (a b) -> p a b", b=1032)
    c5sb = modtmp
    wc5 = big[:, 0:2048]
    c5T = sb([128, 8, 8], BF16, "c5T")
    badab = M1[0:8, 0:512]
    modrow = M1[0:8, 512:1024]

    op = S.op
    dma = S.dma

    dma('sp', ident_f[:], identd, writes=[ident_f], chan=ident_f)
    dma('pool', ident_b[:], identd, writes=[ident_b], chan=ident_b)
    dma('pool', maskA[:], maskAd, writes=[maskA], chan=maskA)
    dma('pool', maskB[:], maskBd, writes=[maskB], chan=maskB)
    dma('sp', vtab[:], vtabd, writes=[vtab], chan=vtab)
    op('dve', lambda e: e.memset(ones_f[:], 1.0), writes=[ones_f])
    op('dve', lambda e: e.memset(ones_b[:], 1.0), writes=[ones_b])
    op('dve', lambda e: e.memset(qkpre[:], 0.0), writes=[qkpre])
    op('dve', lambda e: e.memset(Cst[:], 0.0), writes=[Cst])
    op('dve', lambda e: e.memset(Cbf[:], 0.0), writes=[Cbf])
    op('dve', lambda e: e.memset(R["one"][:], 1.0), writes=[R["one"]])
    for i in range(2):
        op('dve', lambda e, i=i: e.memset(vext[i][:], 1.0), writes=[vext[i]])
    for A_ in ASET:
        op('dve', lambda e, A_=A_: e.memset(A_["Vh"][:], 1.0), writes=[A_["Vh"]])
        op('dve', lambda e, A_=A_: e.memset(A_["Vc"][:], 1.0), writes=[A_["Vc"]])
        op('dve', lambda e, A_=A_: e.memset(A_["Khist"][:], 0.0), writes=[A_["Khist"]])
        op('dve', lambda e, A_=A_: e.memset(A_["Kcur"][:], 0.0), writes=[A_["Kcur"]])
    op('dve', lambda e: e.memset(rcol["Bprev"][:], 0.0), writes=[rcol["Bprev"]])
    op('dve', lambda e: e.memset(rcol["gprev"][:], 0.0), writes=[rcol["gprev"]])
    op('dve', lambda e: e.memset(c5T[:], 0.0), writes=[c5T])
    with nc.allow_non_contiguous_dma(reason="tiny"):
        dma('pool', Wg[:], w_in.rearrange("(kc p) n -> p kc n", p=128)[:, :, 4096:4104], writes=[Wg], chan=Wg)
        dma('sp', bg[:], b_gate.rearrange("(two h) -> h two", two=2), writes=[bg], chan=bg)
    dma('sp', wc5[0:5, 0:2048], wcb, writes=[wc5], chan=wc5)
    for fc in range(16):
        op('pe', lambda e, fc=fc: e.transpose(pS[:, fc * 5:fc * 5 + 5], wc5[0:5, fc * 128:(fc + 1) * 128], ident_f[0:5, 0:5]),
           reads=[wc5, ident_f], writes=[pS])
    op('act', lambda e: e.activation(wcT[:].rearrange("p a b -> p (a b)"), pS[:, 0:80], AF.Copy), reads=[pS], writes=[wcT])
    for j in range(4):
        for fc in range(16):
            op('dve', lambda e, j=j, fc=fc: e.tensor_scalar(diagW[:, j, fc, :], ident_f[:], wcT[:, fc, j:j + 1], None, ALU.mult),
               reads=[ident_f, wcT], writes=[diagW])

    wstate = {'i': 0}

    def wk(w2, c0, width=512):
        return w2.rearrange("(kc p) n -> p kc n", p=128)[:, :, c0:c0 + width]

    def wslab_raw(src3):
        i = wstate['i'] % NWS
        wstate['i'] += 1
        t = wring[i]
        w = src3.shape[2]
        dma('pool', t[:, :, 0:w], src3, writes=[t], chan=t)
        return t

    WSRC = {}
    WSRC['w_in'] = [wk(w_in, s_ * 512) for s_ in range(8)]
    WSRC['w_out'] = [wk(w_out, s_ * 512) for s_ in range(2)]
    for l_ in range(2):
        WSRC['up%d' % l_] = [wk(w_up[l_], s_ * 512) for s_ in range(8)]
        wd_ = w_down[l_].rearrange("(fg kc p) n -> fg p kc n", p=128, kc=8)
        WSRC['down%d' % l_] = [wd_[fg][:, :, half * 512:(half + 1) * 512] for half in range(2) for fg in range(4)]
    WSRC['w_kv'] = [wk(w_kv, s_ * 512) for s_ in range(12)]
    WSRC['w_q'] = [wk(w_q, s_ * 512) for s_ in range(6)]
    WSRC['w_o'] = [wk(w_o, s_ * 512) for s_ in range(2)]
    WSC = {}
    for name_ in ('w_in', 'w_out', 'up0', 'down0', 'w_kv', 'w_q', 'w_o', 'up1', 'down1'):
        WSC[name_] = dscr("wsc_" + name_, [len(WSRC[name_]), 128, 8, 512], BF16)

    def emit_conversions():
        for name_ in ('w_in', 'w_out', 'up0', 'down0', 'w_kv', 'w_q', 'w_o', 'up1', 'down1'):
            for s_, src in enumerate(WSRC[name_]):
                kcv = wstate['i']
                wstate['i'] += 1
                dma('pool', WSC[name_][s_], src, writes=[('dram', 'wsc', name_, s_), ('cvt', kcv % 3)], chan=('cvt', kcv % 3))

    AHEAD = 2
    wissued = {'n': 0}

    def wslab(name, s_):
        k = len(wreq)
        wreq.append((name, s_))
        if dry:
            return wring[0]
        assert wseq[k] == (name, s_), (k, wseq[k], name, s_)
        while wissued['n'] < min(len(wseq), k + 1 + AHEAD):
            j = wissued['n']
            nm, sj = wseq[j]
            tj = wring[j % NWS]
            dma('sp', tj[:], WSC[nm][sj], reads=[('dram', 'wsc', nm, sj)], writes=[tj], chan=tj)
            wissued['n'] += 1
        return wring[k % NWS]

    dma('sp', c5sb[0:5, :], c5, writes=[c5sb], chan=c5sb)
    op('act', lambda e: e.activation(tmpf[0:5, :], c5sb[0:5, :], AF.Sigmoid), reads=[c5sb], writes=[tmpf])
    op('dve', lambda e: e.tensor_tensor(tmpf[0:5, :], tmpf[0:5, :], c5sb[0:5, :], ALU.mult), reads=[tmpf, c5sb], writes=[tmpf])
    for kc in range(8):
        op('pe', lambda e, kc=kc: e.transpose(pS[:, kc * 8:kc * 8 + 5], tmpf[0:5, kc * 128:(kc + 1) * 128], ident_f[0:5, 0:5]),
           reads=[tmpf, ident_f], writes=[pS])
    op('act', lambda e: e.activation(c5T[:, :, 0:5], pS[:, 0:64].rearrange("p (a b) -> p a b", b=8)[:, :, 0:5], AF.Copy),
       reads=[pS], writes=[c5T])

    ada_jobs = []
    for (wmat, bvec, ncols, dst) in ((w_ada[0], b_ada[0], 6 * D, modscr[0]), (w_ada[1], b_ada[1], 6 * D, modscr[1]),
                                     (w_ada_kv, b_ada_kv, 2 * D, modkvscr)):
        for c0 in range(0, ncols, 512):
            ada_jobs.append((wmat, bvec, c0, dst))
    ada_bufs = [(M1[0:8, 0:512], M1[0:8, 512:1024], M1), (M2[0:8, 0:512], M2[0:8, 512:1024], M2)]
    ada_issued = {'n': 0}

    def ada_issue(upto):
        while ada_issued['n'] < min(len(ada_jobs), upto):
            j = ada_issued['n']
            wmat, bvec, c0, dst = ada_jobs[j]
            tj = wring[j % NWS]
            dma('pool', tj[:], wk(wmat, c0), writes=[tj], chan=tj)
            ada_issued['n'] += 1

    for j, (wmat, bvec, c0, dst) in enumerate(ada_jobs):
        ada_issue(j + 3)
        wt = wring[j % NWS]
        bb_, mr_, key_ = ada_bufs[j % 2]
        dma('sp', bb_[0:5, :], bvec[c0:c0 + 512].partition_broadcast(5), writes=[key_], chan=key_)
        p = nextps()
        for kc in range(8):
            op('pe', lambda e, kc=kc, p=p, wt=wt: e.matmul(p[0:5, :], c5T[:, kc, 0:5], wt[:, kc, :], start=(kc == 0), stop=(kc == 7)),
               reads=[c5T, wt], writes=[p])
        op('dve', lambda e, p=p, bb_=bb_, mr_=mr_: e.tensor_tensor(mr_[0:5, :], p[0:5, :], bb_[0:5, :], ALU.add), reads=[p, key_], writes=[key_])
        dma('sp', dst[:, c0:c0 + 512], mr_[0:5, :], reads=[key_], writes=[('dram', dst.tensor.name)], chan=key_)

    def derive(dst3, scr, base, g_pre, g_post):
        rd = [('dram', scr.tensor.name)]
        wr = [('dram', 'modd')]
        P5 = 1 + NS
        dma('sp', tmpf[0:P5, :], scr[:, base + D:base + 2 * D], reads=rd, writes=[tmpf], chan=tmpf)
        dma('sp', M0[0:P5, :], g_pre.partition_broadcast(P5), writes=[M0], chan=M0)
        op('dve', lambda e: e.scalar_tensor_tensor(tmpf[0:P5, :], tmpf[0:P5, :], 1.0, M0[0:P5, :], ALU.add, ALU.mult), reads=[tmpf, M0], writes=[tmpf])
        dma('sp', dst3[0], tmpf[0:P5, :], reads=[tmpf], writes=wr, chan=tmpf)
        dma('sp', dst3[1], scr[:, base:base + D], reads=rd, writes=wr, chan='modd_copy')
        if g_post is not None:
            dma('sp', tmpf[0:P5, :], scr[:, base + 2 * D:base + 3 * D], reads=rd, writes=[tmpf], chan=tmpf)
            dma('sp', M0[0:P5, :], g_post.partition_broadcast(P5), writes=[M0], chan=M0)
            op('dve', lambda e: e.tensor_tensor(tmpf[0:P5, :], tmpf[0:P5, :], M0[0:P5, :], ALU.mult), reads=[tmpf, M0], writes=[tmpf])
            dma('sp', dst3[2], tmpf[0:P5, :], reads=[tmpf], writes=wr, chan=tmpf)

    moddm = dscr("moddm", [2, 2, 3, 1 + NS, D])
    for l_ in range(2):
        derive(moddm[l_, 0], modscr[l_], 0, g_norm[l_, 0], g_norm[l_, 1])
        derive(moddm[l_, 1], modscr[l_], 3 * D, g_norm[l_, 2], g_norm[l_, 3])
    derive(modd[2], modkvscr, 0, g_kv, None)
    emit_conversions()

    def _modsrc(layer, phase):
        if phase == 'kv':
            return modd[2]
        return moddm[layer, 0 if phase == 'mix' else 1]

    def _mrow(src2, sample):
        if sample:
            return src2[1:1 + NS, :], NS
        return src2[0, :].partition_broadcast(128), 128

    def load_gs(layer, phase, sample):
        src = _modsrc(layer, phase)
        for i_, t_ in ((0, M0), (1, M1)):
            a, P = _mrow(src[i_], sample)
            dma('sp', t_[0:P, :], a, reads=[('dram', 'modd')], writes=[t_], chan=t_)

    def load_gg(layer, phase, sample):
        a, P = _mrow(_modsrc(layer, phase)[2], sample)
        dma('sp', M2[0:P, :], a, reads=[('dram', 'modd')], writes=[M2], chan=M2)

    def rstd_of(src, P, col):
        op('act', lambda e: e.activation(junk[0:P, :], src, AF.Square, accum_out=colA[0:P, col:col + 1]),
           reads=[src.tensor.name], writes=[junk, ('colA', col)])
        op('act', lambda e: e.activation(colA[0:P, col + 1:col + 2], colA[0:P, col:col + 1], AF.Ln, bias=EPSC[0:P, :], scale=1.0 / D),
           reads=[('colA', col)], writes=[('colA', col + 1)])
        op('act', lambda e: e.activation(colA[0:P, col + 1:col + 2], colA[0:P, col + 1:col + 2], AF.Exp, scale=-0.5),
           reads=[('colA', col + 1)], writes=[('colA', col + 1)])
        return colA[0:P, col + 1:col + 2], ('colA', col + 1)

    EPSC = sb([128, 1], F32, "epsc")
    op('dve', lambda e: e.memset(EPSC[:], EPS), writes=[EPSC])

    def norm_mod_T(xb, P, gm, sh, c0):
        rs, rk = rstd_of(xb[0:P, :], P, 0)
        op('dve', lambda e: e.scalar_tensor_tensor(tmpf[0:P, :], xb[0:P, :], rs, mods[gm][0:P, :], ALU.mult, ALU.mult),
           reads=[xb.name, rk, mods[gm]], writes=[tmpf])
        op('dve', lambda e: e.tensor_tensor(hn[0:P, :], tmpf[0:P, :], mods[sh][0:P, :], ALU.add),
           reads=[tmpf, mods[sh]], writes=[hn])
        for kc in range(8):
            op('pe', lambda e, kc=kc: e.transpose(pT[:, kc * 128:kc * 128 + P], hn[0:P, kc * 128:(kc + 1) * 128], ident_b[0:P, 0:P]),
               reads=[hn, ident_b], writes=[pT])
        op('act', lambda e: e.activation(hT[:, :, c0:c0 + P], pT[:].rearrange("p (a b) -> p a b", b=128)[:, :, 0:P], AF.Copy),
           reads=[pT], writes=[hT])

    def post_norm_res(pm_, xb, P, gg):
        rs, rk = rstd_of(pm_[0:P, :], P, 2)
        op('dve', lambda e: e.scalar_tensor_tensor(tmpf[0:P, :], pm_[0:P, :], rs, mods[gg][0:P, :], ALU.mult, ALU.mult),
           reads=[pm_.name, rk, mods[gg]], writes=[tmpf])
        op('dve', lambda e: e.tensor_tensor(xb[0:P, :], xb[0:P, :], tmpf[0:P, :], ALU.add),
           reads=[tmpf, xb.name], writes=[xb.name])

    def proj_tok(lhsT3, blocks, wmat, c0s, pm_list, half_of):
        for c0 in c0s:
            wt = wslab_raw(wk(wmat, c0))
            hf = half_of(c0)
            for b, (col0, P) in enumerate(blocks):
                for kc in range(8):
                    op('pe', lambda e, kc=kc, b=b, col0=col0, P=P, wt=wt, hf=hf: e.matmul(
                        pm_list[b][0:P, hf * 512:(hf + 1) * 512], lhsT3[:, kc, col0:col0 + P], wt[:, kc, :], start=(kc == 0), stop=(kc == 7)),
                        reads=[lhsT3.name, wt], writes=[pm_list[b].name])

    def mlp(layer, blocks, N, after_norm=None):
        for b, (col0, P) in enumerate(blocks):
            norm_mod_T(xblk[b], P, "gm2", "sh2", col0)
        if after_norm:
            after_norm()
        for s in range(8):
            wt = wslab('up%d' % layer, s)
            for half in range(2):
                p = nextps()
                for q in range(2):
                    cc = half * 2 + q
                    for kc in range(8):
                        op('pe', lambda e, kc=kc, cc=cc, q=q, p=p, wt=wt: e.matmul(
                            p[:, q * NT:q * NT + N], wt[:, kc, cc * 128:(cc + 1) * 128], hT[:, kc, 0:N], start=(kc == 0), stop=(kc == 7)),
                            reads=[hT, wt], writes=[p])
                fc = s * 4 + half * 2
                pin = p[:].rearrange("p (a b) -> p a b", b=NT)[:, :, 0:N]
                op('act', lambda e, pin=pin: e.activation(relut_v[:, :, 0:N], pin, AF.Relu), reads=[p], writes=[tmpf])
                op('dve', lambda e, fc=fc, pin=pin: e.tensor_tensor(hid[:, fc:fc + 2, 0:N], relut_v[:, :, 0:N], pin, ALU.mult),
                   reads=[p, tmpf], writes=[hid])
        wd = w_down[layer].rearrange("(fg kc p) n -> fg p kc n", p=128, kc=8)
        for half in range(2):
            for fg in range(4):
                wt = wslab('down%d' % layer, half * 4 + fg)
                for b, (col0, P) in enumerate(blocks):
                    for kc in range(8):
                        op('pe', lambda e, kc=kc, b=b, col0=col0, P=P, wt=wt, fg=fg, half=half: e.matmul(
                            pmix[b][0:P, half * 512:(half + 1) * 512], hid[:, fg * 8 + kc, col0:col0 + P], wt[:, kc, :],
                            start=(fg == 0 and kc == 0), stop=(fg == 3 and kc == 7)),
                            reads=[hid, wt], writes=[pmix[b].name])
        for b, (col0, P) in enumerate(blocks):
            post_norm_res(pmix[b], xblk[b], P, "gg2")

    def w_in_proj(N, blocks, sample, parts=('qk', 'v', 'o', 'g')):
        qstep = 4 if sample else 1
        for s in (range(4) if 'qk' in parts else ()):
            wt = wslab('w_in', s)
            for half in range(2):
                p = nextps()
                for q in range(2):
                    cc = half * 2 + q
                    for kc in range(8):
                        op('pe', lambda e, kc=kc, cc=cc, q=q, p=p, wt=wt: e.matmul(
                            p[:, q * NT:q * NT + N], wt[:, kc, cc * 128:(cc + 1) * 128], hT[:, kc, 0:N], start=(kc == 0), stop=(kc == 7)),
                            reads=[hT, wt], writes=[p])
                fc = s * 4 + half * 2
                pin = p[:].rearrange("p (a b) -> p a b", b=NT)[:, :, 0:N]
                if sample:
                    dst = qkpre[:, fc:fc + 2, 3:3 + 4 * N:4]
                else:
                    dst = qkpre[:, fc:fc + 2, 3:3 + N]
                op('act', lambda e, dst=dst, pin=pin: e.activation(dst, pin, AF.Copy), reads=[p], writes=[qkpre])
        for s in (range(2) if 'v' in parts else ()):
            wt = wslab('w_in', 4 + s)
            for b, (col0, P) in enumerate(blocks):
                p = nextps()
                for kc in range(8):
                    op('pe', lambda e, kc=kc, col0=col0, P=P, p=p, wt=wt: e.matmul(
                        p[0:P, :], hT[:, kc, col0:col0 + P], wt[:, kc, :], start=(kc == 0), stop=(kc == 7)),
                        reads=[hT, wt], writes=[p])
                op('act', lambda e, b=b, P=P, p=p, s=s: e.activation(
                    vext[b][0:P, 2 * s:2 * s + 2, 0:256], p[0:P, :].rearrange("p (a b) -> p a b", b=256), AF.Copy),
                    reads=[p], writes=[vext[b]])
        for s in (range(2) if 'o' in parts else ()):
            wt = wslab('w_in', 6 + s)
            for half in range(2):
                p = nextps()
                for q in range(2):
                    cc = half * 2 + q
                    for kc in range(8):
                        op('pe', lambda e, kc=kc, cc=cc, q=q, p=p, wt=wt: e.matmul(
                            p[:, q * NT:q * NT + N], wt[:, kc, cc * 128:(cc + 1) * 128], hT[:, kc, 0:N], start=(kc == 0), stop=(kc == 7)),
                            reads=[hT, wt], writes=[p])
                fc = s * 4 + half * 2
                pin = p[:].rearrange("p (a b) -> p a b", b=NT)[:, :, 0:N]
                op('act', lambda e, fc=fc, pin=pin: e.activation(sigT[:, fc:fc + 2, 0:N], pin, AF.Sigmoid), reads=[p], writes=[sigT])
        if 'g' not in parts:
            return
        for gi, nm in ((0, "li"), (1, "fp")):
            for kc in range(8):
                op('pe', lambda e, kc=kc, gi=gi: e.matmul(pS[0:4, gi * NT:gi * NT + N], Wg[:, kc, gi * 4:gi * 4 + 4], hT[:, kc, 0:N],
                                                          start=(kc == 0), stop=(kc == 7)), reads=[Wg, hT], writes=[pS])
        op('dve', lambda e: e.tensor_scalar(R["li"][:, 0:N], pS[0:4, 0:N], bg[:, 0:1], None, ALU.add), reads=[pS, bg], writes=[R["li"]])
        op('dve', lambda e: e.tensor_scalar(R["fp"][:, 0:N], pS[0:4, NT:NT + N], bg[:, 1:2], -1.0, ALU.add, ALU.mult), reads=[pS, bg], writes=[R["fp"]])
        op('act', lambda e: e.activation(R["t"][:, 0:N], R["fp"][:, 0:N], AF.Exp), reads=[R["fp"]], writes=[R["t"]])
        op('act', lambda e: e.activation(R["lf"][:, 0:N], R["t"][:, 0:N], AF.Ln, bias=ONE4[:, :], scale=1.0), reads=[R["t"]], writes=[R["lf"]])
        op('dve', lambda e: e.tensor_scalar(R["lf"][:, 0:N], R["lf"][:, 0:N], -1.0, None, ALU.mult), reads=[R["lf"]], writes=[R["lf"]])

    ONE4 = sb([4, 1], F32, "one4")
    op('dve', lambda e: e.memset(ONE4[:], 1.0), writes=[ONE4])

    def conv_silu(N, sample):
        step = 4 if sample else 1
        for fc in range(16):
            if fc % 2 == 0:
                p = nextps()
            q = fc % 2
            for j in range(4):
                if sample:
                    rhs = qkpre[:, fc, j:j + 4 * N:4]
                else:
                    rhs = qkpre[:, fc, j:j + N]
                op('pe', lambda e, j=j, fc=fc, q=q, p=p, rhs=rhs: e.matmul(p[:, q * NT:q * NT + N], diagW[:, j, fc, :], rhs, start=(j == 0), stop=(j == 3)),
                   reads=[diagW, qkpre], writes=[p])
            op('act', lambda e, fc=fc, q=q, p=p: e.activation(qkT[:, fc, 0:N], p[:, q * NT:q * NT + N], AF.Silu, bias=wcT[:, fc, 4:5], scale=1.0),
               reads=[p, wcT], writes=[qkT])

    def gate_rows_prompt(N):
        op('dve', lambda e: e.tensor_tensor_scan(R["B"][:, 0:N], R["one"][:, 0:N], R["lf"][:, 0:N], rcol["Bprev"][:, 0:1], ALU.mult, ALU.add),
           reads=[R["one"], R["lf"], rcol["Bprev"]], writes=[R["B"]])
        op('dve', lambda e: e.tensor_tensor(R["a"][:, 0:N], R["li"][:, 0:N], R["B"][:, 0:N], ALU.subtract), reads=[R["li"], R["B"]], writes=[R["a"]])
        op('dve', lambda e: e.tensor_tensor_scan(R["g"][:, 0:N], R["a"][:, 0:N], R["a"][:, 0:N], rcol["gprev"][:, 0:1], ALU.max, ALU.max),
           reads=[R["a"], rcol["gprev"]], writes=[R["g"]])

    def mlstm_chunk(b, c0, first_of_tile):
        if first_of_tile:
            gp = rcol["gprev"][:, 0:1]
            gpk = rcol["gprev"]
        else:
            gp = R["g"][:, c0 - 1:c0]
            gpk = R["g"]
        op('dve', lambda e: e.tensor_scalar(rcol["nb"][:, 0:1], gp, -1.0, -LN16, ALU.mult, ALU.add), reads=[gpk], writes=[rcol["nb"]])
        op('dve', lambda e: e.tensor_scalar(rcol["nb"][:, 1:2], gp, -1.0, None, ALU.mult), reads=[gpk], writes=[rcol["nb"]])
        op('act', lambda e: e.activation(R["wk"][:, 0:128], R["a"][:, c0:c0 + 128], AF.Exp, bias=rcol["nb"][:, 0:1], scale=1.0),
           reads=[R["a"], rcol["nb"]], writes=[R["wk"]])
        op('act', lambda e: e.activation(R["e"][:, 0:128], R["B"][:, c0:c0 + 128], AF.Exp, bias=rcol["nb"][:, 1:2], scale=-1.0),
           reads=[R["B"], rcol["nb"]], writes=[R["e"]])
        op('dve', lambda e: e.tensor_tensor(rcol["dec"][:, 0:1], gp, R["g"][:, c0 + 127:c0 + 128], ALU.subtract), reads=[gpk, R["g"]], writes=[rcol["dec"]])
        op('act', lambda e: e.activation(rcol["dec"][:, 0:1], rcol["dec"][:, 0:1], AF.Exp), reads=[rcol["dec"]], writes=[rcol["dec"]])
        op('pe', lambda e: e.transpose(pS[:, 0:4], R["wk"][:, 0:128], ident_f[0:4, 0:4]), reads=[R["wk"], ident_f], writes=[pS])
        op('pe', lambda e: e.transpose(pS[:, 4:8], R["e"][:, 0:128], ident_f[0:4, 0:4]), reads=[R["e"], ident_f], writes=[pS])
        op('dve', lambda e: e.tensor_scalar(rcol["t"][:, 0:4], ident_f[0:4, 0:4], rcol["dec"][:, 0:1], None, ALU.mult), reads=[ident_f, rcol["dec"]], writes=[rcol["t"]])
        op('pe', lambda e: e.matmul(pS[:, 8:12], ones_f[0:4, :], rcol["t"][:, 0:4], start=True, stop=True), reads=[ones_f, rcol["t"]], writes=[pS])
        op('act', lambda e: e.activation(cols[:, 0:12], pS[:, 0:12], AF.Copy), reads=[pS], writes=[cols])
        fq = lambda h, kc: 2 * h + kc
        fk = lambda h, kc: 8 + 2 * h + kc
        pst = nextps()
        for h in range(4):
            for kc in range(2):
                op('pe', lambda e, kc=kc, h=h: e.matmul(pst[:, h * 128:(h + 1) * 128], qkT[:, fk(h, kc), c0:c0 + 128], qkT[:, fq(h, kc), c0:c0 + 128], start=(kc == 0), stop=(kc == 1)),
                   reads=[qkT], writes=[pst])
        for h in range(4):
            op('dve', lambda e, h=h: e.scalar_tensor_tensor(PpTv[h], pst[:, h * 128:(h + 1) * 128], cols[:, h:h + 1], maskB[:], ALU.mult, ALU.mult),
               reads=[pst, cols, maskB], writes=[PpTt[h]])
        for h in range(4):
            for kc in range(2):
                op('pe', lambda e, kc=kc, h=h: e.transpose(pT[:, h * 256 + kc * 128:h * 256 + (kc + 1) * 128], qkT[:, fk(h, kc), c0:c0 + 128], ident_b[:]), reads=[qkT, ident_b], writes=[pT])
        for h in range(4):
            op('act', lambda e, h=h: e.activation(Kpv[h], pT[:, h * 256:(h + 1) * 256], AF.Identity, scale=cols[:, h:h + 1]), reads=[pT, cols], writes=[Kpt[h]])
        for h in range(4):
            op('pool', lambda e, h=h: e.tensor_scalar(Cst[:, h, :, :], Cst[:, h, :, :], cols[:, 8 + h:9 + h], 0.0, ALU.mult, ALU.add), reads=[Cst, cols], writes=[Cst])
        nb_ = [(pmix[0], 0), (pmix[0], 512), (pmix[1], 0), (pmix[1], 512)]
        for h in range(4):
            pm_, cb = nb_[h]
            for kc in range(2):
                op('pe', lambda e, kc=kc, h=h, pm_=pm_, cb=cb: e.matmul(pm_[:, cb:cb + 257], qkT[:, fq(h, kc), c0:c0 + 128], Cbf[:, h, kc, :], start=(kc == 0), stop=False),
                   reads=[qkT, Cbf], writes=[pm_.name])
            op('pe', lambda e, h=h, pm_=pm_, cb=cb: e.matmul(pm_[:, cb:cb + 257], PpTv[h], vext[b][:, h, :], start=False, stop=True), reads=[PpTt[h], vext[b]], writes=[pm_.name])
        for h in range(4):
            pm_, cb = nb_[h]
            op('act', lambda e, h=h, pm_=pm_, cb=cb: e.activation(cols[:, 12 + h:13 + h], pm_[:, cb + 256:cb + 257], AF.Abs), reads=[pm_.name], writes=[('c12', h)])
        op('dve', lambda e: e.tensor_tensor(cols[:, 12:16], cols[:, 12:16], cols[:, 4:8], ALU.max), reads=[('c12', 0), ('c12', 1), ('c12', 2), ('c12', 3), cols], writes=[('c12', 0), ('c12', 1), ('c12', 2), ('c12', 3)])
        op('dve', lambda e: e.reciprocal(cols[:, 12:16], cols[:, 12:16]), reads=[('c12', 0), ('c12', 1), ('c12', 2), ('c12', 3)], writes=[('c12', 0), ('c12', 1), ('c12', 2), ('c12', 3)])
        for h in range(4):
            pm_, cb = nb_[h]
            op('act', lambda e, h=h, pm_=pm_, cb=cb: e.activation(hhm[b][:, h * 256:(h + 1) * 256], pm_[:, cb:cb + 256], AF.Identity, scale=cols[:, 12 + h:13 + h]),
               reads=[pm_.name, ('c12', h)], writes=[hhm[b]])
        for h in range(4):
            for kc in range(2):
                pu = nextps()
                op('pe', lambda e, kc=kc, pu=pu, h=h: e.matmul(pu[:, 0:257], Kpv[h][:, kc * 128:(kc + 1) * 128], vext[b][:, h, :], start=True, stop=True),
                   reads=[Kpt[h], vext[b]], writes=[pu])
                op('dve', lambda e, kc=kc, pu=pu, h=h: e.scalar_tensor_tensor(Cst[:, h, kc, :], pu[:, 0:257], cols[:, 8 + h:9 + h], Cst[:, h, kc, :], ALU.mult, ALU.add),
                   reads=[pu, Cst, cols], writes=[Cst])
        for h in range(4):
            op('pool', lambda e, h=h: e.tensor_copy(Cbf[:, h, :, :], Cst[:, h, :, :]), reads=[Cst], writes=[Cbf])

    def gated_T(b, c0, P):
        hsrc = hhm[b]
        for kc in range(8):
            op('pe', lambda e, kc=kc: e.transpose(pT[:, kc * 128:kc * 128 + P], hsrc[0:P, kc * 128:(kc + 1) * 128], ident_b[0:P, 0:P]), reads=[hsrc, ident_b], writes=[pT])
        op('dve', lambda e: e.tensor_tensor(gatedT[:, :, 0:P], pT[:].rearrange("p (a b) -> p a b", b=128)[:, :, 0:P], sigT[:, :, c0:c0 + P], ALU.mult),
           reads=[pT, sigT], writes=[gatedT])

    def wout_block(b, P, wts):
        for half in range(2):
            for kc in range(8):
                op('pe', lambda e, kc=kc, half=half: e.matmul(pmix[b][0:P, half * 512:(half + 1) * 512], gatedT[:, kc, 0:P], wts[half][:, kc, :], start=(kc == 0), stop=(kc == 7)),
                   reads=[gatedT, wts[half]], writes=[pmix[b].name])

    def kv_proj(blocks, tok0, sample, after_norm=None):
        for b, (col0, P) in enumerate(blocks):
            norm_mod_T(xblk[b], P, "gmkv", "shkv", col0)
        if after_norm:
            after_norm()
        for s in range(12):
            wt = wslab('w_kv', s)
            for b, (col0, P) in enumerate(blocks):
                p = nextps()
                for kc in range(8):
                    op('pe', lambda e, kc=kc, col0=col0, P=P, p=p, wt=wt: e.matmul(p[0:P, :], hT[:, kc, col0:col0 + P], wt[:, kc, :], start=(kc == 0), stop=(kc == 7)),
                       reads=[hT, wt], writes=[p])
                kvrot['b'] += 1
                kvbf = kvbfs[kvrot['b'] % 2]
                op('act', lambda e, P=P, p=p, kvbf=kvbf: e.activation(kvbf[0:P, :], p[0:P, :], AF.Copy), reads=[p], writes=[kvbf])
                kvsel, g, hf = s // 6, (s % 6) // 2, s % 2
                kvrot['t'] += 1
                kvtok = kvtoks[kvrot['t'] % 2]
                if sample:
                    dma('sp', kvscr_s[:, s * 512:(s + 1) * 512], kvbf[0:P, :], reads=[kvbf], writes=[('dram', 'kvscr_s')], chan=kvbf)
                    dst = (sko if kvsel == 0 else svo)[g]
                    op('dve', lambda e, P=P, p=p, kvtok=kvtok: e.tensor_copy(kvtok[0:P, :], p[0:P, :]), reads=[p], writes=[kvtok])
                    dma('sp', dst[:, hf * 512:(hf + 1) * 512], kvtok[0:P, :], reads=[kvtok], chan=kvtok)
                else:
                    t0 = tok0 + col0
                    dma('sp', kvscr[t0:t0 + P, s * 512:(s + 1) * 512], kvbf[0:P, :], reads=[kvbf], writes=[('dram', 'kvscr', t0 // 128)], chan=kvbf)
                    r = min(GROUPS[g][0], T)
                    if t0 >= T - r:
                        dst = (pko if kvsel == 0 else pvo)[g]
                        op('dve', lambda e, P=P, p=p, kvtok=kvtok: e.tensor_copy(kvtok[0:P, :], p[0:P, :]), reads=[p], writes=[kvtok])
                        dma('sp', dst[t0 - (T - r):t0 - (T - r) + P, hf * 512:(hf + 1) * 512], kvtok[0:P, :], reads=[kvtok], chan=kvtok)

    SCALE = float(128 ** -0.5)

    def attn_load(st):
        A_ = st['A']
        nq, nh, rd, hq = st['nq'], st['nh'], st['rd'], st.get('hq', 'sp')
        if nh > 0:
            dma(hq, A_["Khist"][128 - nh:128, :], st['khist'], reads=rd, writes=[A_["Khist"]], chan=A_["Khist"])
            dma(hq, A_["Vh"][128 - nh:128, :, 0:128], st['vhist'].rearrange("n (h d) -> n h d", d=128), reads=rd, writes=[A_["Vh"]], chan=A_["Vh"])
        dma('sp', A_["Kcur"][0:nq, :], st['kcur'], reads=rd, writes=[A_["Kcur"]], chan=A_["Kcur"])
        dma('sp', A_["Vc"][0:nq, :, 0:128], st['vcur'].rearrange("n (h d) -> n h d", d=128), reads=rd, writes=[A_["Vc"]], chan=A_["Vc"])

    def attn_compute(st):
        A_ = st['A']
        qT3, nq, nh, odst, wr = st['q3'], st['nq'], st['nh'], st['odst'], st['wr']
        hb = 4 if nq > 64 else 8
        Khist, Kcur, Vh, Vc, KTh, KTc, Eh, Ec, Osb = (A_[k_] for k_ in ("Khist", "Kcur", "Vh", "Vc", "KTh", "KTc", "Eh", "Ec", "Osb"))
        if nh > 0:
            for h in range(8):
                op('pe', lambda e, h=h: e.transpose(pT[:, h * 128:(h + 1) * 128], Khist[:, h * 128:(h + 1) * 128], ident_b[:]), reads=[Khist, ident_b], writes=[pT])
            op('dve', lambda e: e.tensor_copy(KTh[:].rearrange("p a b -> p (a b)"), pT[:]), reads=[pT], writes=[KTh])
        pSb = pS[:].bitcast(BF16)
        for h in range(8):
            op('pe', lambda e, h=h: e.transpose(pSb[:, h * 128:h * 128 + nq], Kcur[0:nq, h * 128:(h + 1) * 128], ident_b[0:nq, 0:nq]), reads=[Kcur, ident_b], writes=[pS])
        op('act', lambda e: e.activation(KTc[:, :, 0:nq], pSb.rearrange("p (a b) -> p a b", b=128)[:, :, 0:nq], AF.Copy), reads=[pS], writes=[KTc])
        for h0 in range(0, 8, hb):
            if nh > 0:
                p = nextps()
                for h in range(h0, h0 + hb):
                    op('pe', lambda e, h=h, p=p: e.matmul(p[:, (h - h0) * nq:(h - h0 + 1) * nq], KTh[:, h, :], qT3[:, h, :], start=True, stop=True),
                       reads=[KTh, qT3.tensor.name], writes=[p])
                op('act', lambda e, p=p: e.activation(Eh[:, h0:h0 + hb, 0:nq], p[:, 0:hb * nq].rearrange("p (a b) -> p a b", b=nq), AF.Exp, scale=SCALE),
                   reads=[p], writes=[Eh])
                op('dve', lambda e: e.scalar_tensor_tensor(Eh[:, h0:h0 + hb, 0:nq], Eh[:, h0:h0 + hb, 0:nq], vtab[:, nh:nh + 1],
                                                            maskA[:, 0:nq].unsqueeze(1).broadcast_to([128, hb, nq]), ALU.mult, ALU.mult),
                   reads=[Eh, vtab, maskA], writes=[Eh])
            p = nextps()
            for h in range(h0, h0 + hb):
                op('pe', lambda e, h=h, p=p: e.matmul(p[0:nq, (h - h0) * nq:(h - h0 + 1) * nq], KTc[:, h, 0:nq], qT3[:, h, :], start=True, stop=True),
                   reads=[KTc, qT3.tensor.name], writes=[p])
            op('act', lambda e, p=p: e.activation(Ec[0:nq, h0:h0 + hb, 0:nq], p[0:nq, 0:hb * nq].rearrange("p (a b) -> p a b", b=nq), AF.Exp, scale=SCALE),
               reads=[p], writes=[Ec])
            op('dve', lambda e: e.tensor_tensor(Ec[0:nq, h0:h0 + hb, 0:nq], Ec[0:nq, h0:h0 + hb, 0:nq],
                                                maskB[0:nq, 0:nq].unsqueeze(1).broadcast_to([nq, hb, nq]), ALU.mult),
               reads=[Ec, maskB], writes=[Ec])
        for h in range(8):
            pm_ = pmix[0] if h < 6 else pmix[1]
            cb = (512 if 3 <= h < 6 else 0) + (h % 3) * 129
            if nh > 0:
                op('pe', lambda e, h=h, pm_=pm_, cb=cb: e.matmul(pm_[0:nq, cb:cb + 129], Eh[:, h, 0:nq], Vh[:, h, :], start=True, stop=False),
                   reads=[Eh, Vh], writes=[pm_.name])
            op('pe', lambda e, h=h, pm_=pm_, cb=cb: e.matmul(pm_[0:nq, cb:cb + 129], Ec[0:nq, h, 0:nq], Vc[0:nq, h, :], start=(nh == 0), stop=True),
               reads=[Ec, Vc], writes=[pm_.name])
        op('act', lambda e: e.activation(Osb[0:nq, 0:387], pmix[0][0:nq, 0:387], AF.Copy), reads=[pmix[0].name], writes=[Osb])
        op('act', lambda e: e.activation(Osb[0:nq, 387:774], pmix[0][0:nq, 512:899], AF.Copy), reads=[pmix[0].name], writes=[Osb])
        op('act', lambda e: e.activation(Osb[0:nq, 774:1032], pmix[1][0:nq, 0:258], AF.Copy), reads=[pmix[1].name], writes=[Osb])
        dma('sp', odst, Osb[0:nq, :], reads=[Osb], writes=wr, chan=Osb)

    def run_steps(steps, pieces=None):
        pieces = list(pieces or [])
        for st in steps:
            st['A'] = ASET[astep['i'] % 2]
            astep['i'] += 1
        if steps:
            attn_load(steps[0])
        for i_, st in enumerate(steps):
            if i_ + 1 < len(steps):
                attn_load(steps[i_ + 1])
            attn_compute(st)
            if pieces:
                pieces.pop(0)()
        while pieces:
            pieces.pop(0)()

    def attn_combine(P, osrc3, rd, b, wts):
        dma('sp', Og[0:P, :, :], osrc3.rearrange("g n c -> n g c"), reads=rd, writes=[Og], chan=Og)
        op('dve', lambda e: e.tensor_tensor(Og[0:P, 0, :], Og[0:P, 0, :], Og[0:P, 1, :], ALU.add), reads=[Og], writes=[Og])
        op('dve', lambda e: e.tensor_tensor(Og[0:P, 0, :], Og[0:P, 0, :], Og[0:P, 2, :], ALU.add), reads=[Og], writes=[Og])
        o3 = Og[0:P, 0, :].rearrange("p (h c) -> p h c", c=129)
        op('dve', lambda e: e.reciprocal(cols[0:P, 0:8], o3[:, :, 128]), reads=[Og], writes=[cols])
        op('dve', lambda e: e.tensor_tensor(hh[0:P, :].rearrange("p (h c) -> p h c", c=128), o3[:, :, 0:128],
                                            cols[0:P, 0:8].unsqueeze(2).broadcast_to([P, 8, 128]), ALU.mult), reads=[Og, cols], writes=[hh])
        for kc in range(8):
            op('pe', lambda e, kc=kc: e.transpose(pT[:, kc * 128:kc * 128 + P], hh[0:P, kc * 128:(kc + 1) * 128], ident_b[0:P, 0:P]), reads=[hh, ident_b], writes=[pT])
        op('act', lambda e: e.activation(gatedT[:, :, 0:P], pT[:].rearrange("p (a b) -> p a b", b=128)[:, :, 0:P], AF.Copy), reads=[pT], writes=[gatedT])
        wout_block(b, P, wts)

    def q_proj(N):
        for s in range(6):
            wt = wslab('w_q', s)
            for half in range(2):
                p = nextps()
                for q in range(2):
                    cc = half * 2 + q
                    for kc in range(8):
                        op('pe', lambda e, kc=kc, cc=cc, q=q, p=p, wt=wt: e.matmul(
                            p[:, q * NT:q * NT + N], wt[:, kc, cc * 128:(cc + 1) * 128], hT[:, kc, 0:N], start=(kc == 0), stop=(kc == 7)),
                            reads=[hT, wt], writes=[p])
                fc = s * 4 + half * 2
                pin = p[:].rearrange("p (a b) -> p a b", b=NT)[:, :, 0:N]
                op('act', lambda e, fc=fc, pin=pin: e.activation(QT[:, fc:fc + 2, 0:N], pin, AF.Copy), reads=[p], writes=[QT])

    blocks_p = [(0, 128), (128, 128)]
    load_state = {'mods': None}

    def ensure_mods(layer, sample):
        if load_state['mods'] != (layer, sample):
            load_mods(layer, sample)
            load_state['mods'] = (layer, sample)

    def front_pieces(ti):
        tok0 = ti * NT
        P_ = []

        def nrm(b):
            def f():
                if b == 0:
                    load_gs(0, 'mix', False)
                dma('sp', xstage[:], xp[tok0 + b * 128:tok0 + (b + 1) * 128, :], writes=[xstage], chan=xstage)
                norm_mod_T(xstage, 128, "gm1", "sh1", b * 128)
                if b == 1 and ti > 0:
                    load_gs(1, 'mlp', False)
            return f
        P_.append(nrm(0))
        P_.append(nrm(1))
        for part in ('qk', 'v', 'o', 'g'):
            P_.append(lambda part=part: w_in_proj(NT, blocks_p, False, parts=(part,)))

        def pconv_piece():
            for s in range(4):
                wt = wslab('w_in', s)
                p = nextps()
                for kc in range(8):
                    op('pe', lambda e, kc=kc, p=p, wt=wt: e.matmul(p[0:3, :], hT[:, kc, NT - 3:NT], wt[:, kc, :], start=(kc == 0), stop=(kc == 7)), reads=[hT, wt], writes=[p])
                kt = kvtoks[s % 2]
                op('act', lambda e, p=p, kt=kt: e.activation(kt[0:3, :], p[0:3, :], AF.Copy), reads=[p], writes=[kt])
                dma('sp', pconv[:, s * 512:(s + 1) * 512], kt[0:3, :], reads=[kt], chan=kt)
        if ti == ntile - 1:
            P_.append(pconv_piece)

        def conv_piece():
            conv_silu(NT, False)
            op('dve', lambda e: e.tensor_copy(qkpre[:, :, 0:3], qkpre[:, :, NT:NT + 3]), reads=[qkpre], writes=[qkpre])
            gate_rows_prompt(NT)
        P_.append(conv_piece)
        P_.append(lambda: mlstm_chunk(0, 0, True))

        def chunk1():
            mlstm_chunk(1, 128, False)
            op('dve', lambda e: e.tensor_copy(rcol["Bprev"][:, 0:1], R["B"][:, NT - 1:NT]), reads=[R["B"]], writes=[rcol["Bprev"]])
            op('dve', lambda e: e.tensor_copy(rcol["gprev"][:, 0:1], R["g"][:, NT - 1:NT]), reads=[R["g"]], writes=[rcol["gprev"]])
        P_.append(chunk1)
        return P_

    def back(ti):
        tok0 = ti * NT
        load_gg(0, 'mix', False)
        for b in range(2):
            dma('sp', xblk[b][:], xp[tok0 + b * 128:tok0 + (b + 1) * 128, :], writes=[xblk[b].name], chan=xblk[b])
        wts = [wslab('w_out', 0), wslab('w_out', 1)]
        for b in range(2):
            gated_T(b, b * 128, 128)
            wout_block(b, 128, wts)
            post_norm_res(pmix[b], xblk[b], 128, "gg1")
        if ti == 0:
            load_gs(0, 'mlp', False)
        load_gg(0, 'mlp', False)
        mlp(0, blocks_p, NT, after_norm=lambda: load_gs(0, 'kv', False))
        kv_proj(blocks_p, tok0, False, after_norm=lambda: load_gs(1, 'mix', False))

    def layer1_tile(ti, pieces):
        tok0 = ti * NT
        load_gg(1, 'mix', False)
        for b in range(2):
            norm_mod_T(xblk[b], 128, "gm1", "sh1", b * 128)
        if not pieces:
            load_gs(1, 'mlp', False)
        q_proj(NT)
        steps = []
        okeys = []
        for g, (win, dil) in enumerate(GROUPS):
            nq = 128 if dil == 1 else NT // dil
            nsteps = NT // (nq * dil)
            for r in range(dil):
                for st in range(nsteps):
                    tq0 = tok0 + st * 128 * (1 if dil == 1 else 0) + r
                    j0 = (tq0 - r) // dil
                    nh = min(128, j0)
                    kb, vb = g * D, 3 * D + g * D
                    rows_c = slice(tq0, tq0 + (nq - 1) * dil + 1, dil)
                    kh = vh_ = None
                    if nh > 0:
                        th0 = tq0 - nh * dil
                        rows_h = slice(th0, th0 + (nh - 1) * dil + 1, dil)
                        kh = kvscr[rows_h, kb:kb + D]
                        vh_ = kvscr[rows_h, vb:vb + D]
                    if dil == 1:
                        q3 = QT[:, g * 8:(g + 1) * 8, st * 128:st * 128 + 128]
                    else:
                        q3 = QT[:, g * 8:(g + 1) * 8, r:NT:dil]
                    lo_blk = max(0, (tq0 - 128 * dil)) // 128
                    rd = [('dram', 'kvscr', bb) for bb in range(lo_blk, (tok0 + NT) // 128)]
                    key = ('dram', 'oscr', ti, len(steps))
                    okeys.append(key)
                    steps.append(dict(q3=q3, nq=nq, nh=nh, khist=kh, vhist=vh_, kcur=kvscr[rows_c, kb:kb + D], vcur=kvscr[rows_c, vb:vb + D],
                                      odst=oscr[g, rows_c, :], rd=rd, wr=[key]))
        run_steps(steps, pieces)
        wts = [wslab('w_o', 0), wslab('w_o', 1)]
        for b in range(2):
            t0 = tok0 + b * 128
            attn_combine(128, oscr[:, t0:t0 + 128, :], okeys, b, wts)
            post_norm_res(pmix[b], xblk[b], 128, "gg1")
        load_gg(1, 'mlp', False)
        mlp(1, blocks_p, NT, after_norm=(lambda: load_gs(0, 'mlp', False)) if ti < ntile - 1 else None)
        for b in range(2):
            t0 = tok0 + b * 128
            dma('sp', yp[t0:t0 + 128, :], xblk[b][:], reads=[xblk[b].name], chan=xblk[b])


    colsS = sb([4, 64], F32, "colsS")
    acols = sb([128, 32], F32, "acols")
    rhs32 = sb([4, 32], F32, "rhs32")
    Ktok = Khist

    def sample_tile():
        blocks_s = [(0, NS)]
        P = NS
        load_gs(0, 'mix', True)
        load_gg(0, 'mix', True)
        dma('sp', xblk[0][0:P, :], xs, writes=[xblk[0]], chan=xblk[0])
        norm_mod_T(xblk[0], P, "gm1", "sh1", 0)
        dma('sp', big[0:12, 0:2048], stconv.rearrange("i j c -> (i j) c"), writes=[big], chan=big)
        for fc in range(16):
            op('pe', lambda e, fc=fc: e.transpose(pS[:, fc * 12:fc * 12 + 12], big[0:12, fc * 128:(fc + 1) * 128], ident_f[0:12, 0:12]),
               reads=[big, ident_f], writes=[pS])
        op('act', lambda e: e.activation(qkpre[:, :, 0:16].rearrange("p f (i j) -> p f i j", j=4)[:, :, :, 0:3],
                                         pS[:, 0:192].rearrange("p (f i j) -> p f i j", i=4, j=3), AF.Copy), reads=[pS], writes=[qkpre])
        w_in_proj(P, blocks_s, True)
        dma('sp', sconvo[:, 0:2, :], stconv[:, 1:3, :], chan='sconv_copy')
        for s_ in range(4):
            wt = wslab('w_in', s_)
            p = nextps()
            for kc in range(8):
                op('pe', lambda e, kc=kc, p=p, wt=wt: e.matmul(p[0:P, :], hT[:, kc, 0:P], wt[:, kc, :], start=(kc == 0), stop=(kc == 7)), reads=[hT, wt], writes=[p])
            op('act', lambda e, p=p: e.activation(kvtok[0:P, :], p[0:P, :], AF.Copy), reads=[p], writes=[kvtok])
            dma('sp', sconvo[:, 2, s_ * 512:(s_ + 1) * 512], kvtok[0:P, :], reads=[kvtok], chan=kvtok)
        conv_silu(P, True)
        with nc.allow_non_contiguous_dma(reason="tiny"):
            dma('sp', R["B"][:, 0:4], stm.rearrange("i h -> h i"), writes=[R["B"]], chan=R["B"])
        op('dve', lambda e: e.tensor_tensor(R["a"][:, 0:4], R["lf"][:, 0:4], R["B"][:, 0:4], ALU.add), reads=[R["lf"], R["B"]], writes=[R["a"]])
        op('dve', lambda e: e.tensor_tensor(R["g"][:, 0:4], R["a"][:, 0:4], R["li"][:, 0:4], ALU.max), reads=[R["a"], R["li"]], writes=[R["g"]])
        with nc.allow_non_contiguous_dma(reason="tiny"):
            dma('sp', smo.rearrange("i h -> h i"), R["g"][:, 0:4], reads=[R["g"]], chan=R["g"])
        op('dve', lambda e: e.tensor_tensor(R["wk"][:, 0:4], R["a"][:, 0:4], R["g"][:, 0:4], ALU.subtract), reads=[R["a"], R["g"]], writes=[R["wk"]])
        op('act', lambda e: e.activation(R["wk"][:, 0:4], R["wk"][:, 0:4], AF.Exp), reads=[R["wk"]], writes=[R["wk"]])
        op('dve', lambda e: e.tensor_tensor(R["e"][:, 0:4], R["li"][:, 0:4], R["g"][:, 0:4], ALU.subtract), reads=[R["li"], R["g"]], writes=[R["e"]])
        op('act', lambda e: e.activation(R["e"][:, 0:4], R["e"][:, 0:4], AF.Exp), reads=[R["e"]], writes=[R["e"]])
        op('act', lambda e: e.activation(R["t"][:, 0:4], R["g"][:, 0:4], AF.Exp, scale=-1.0), reads=[R["g"]], writes=[R["t"]])
        for off, src in ((0, "wk"), (16, "t")):
            op('dve', lambda e, off=off, src=src: e.tensor_tensor(
                rhs32[:, off:off + 16].rearrange("p (h i) -> p h i", i=4),
                R[src][:, 0:4].unsqueeze(1).broadcast_to([4, 4, 4]),
                ident_f[0:4, 0:4].unsqueeze(2).broadcast_to([4, 4, 4]), ALU.mult), reads=[R[src], ident_f], writes=[rhs32])
        op('pe', lambda e: e.matmul(pS[:, 0:32], ones_f[0:4, :], rhs32[:, :], start=True, stop=True), reads=[ones_f, rhs32], writes=[pS])
        op('act', lambda e: e.activation(acols[:], pS[:, 0:32], AF.Copy), reads=[pS], writes=[acols])
        op('pe', lambda e: e.transpose(pS[0:4, 0:4], R["e"][:, 0:4], ident_f[0:4, 0:4]), reads=[R["e"], ident_f], writes=[pS])
        op('act', lambda e: e.activation(colsS[:, 0:4], pS[0:4, 0:4], AF.Copy, scale=1.0 / 16.0), reads=[pS], writes=[colsS])
        op('dve', lambda e: e.tensor_tensor(
            colsS[:, 16:32].rearrange("p (i h) -> p i h", h=4),
            colsS[:, 0:4].unsqueeze(1).broadcast_to([4, 4, 4]),
            ident_f[0:4, 0:4].unsqueeze(2).broadcast_to([4, 4, 4]), ALU.mult), reads=[colsS, ident_f], writes=[('colsS2',)])
        for j in range(8):
            op('pe', lambda e, j=j: e.transpose(pT[0:P, j * 128:(j + 1) * 128], qkT[:, 8 + j, 0:P], ident_b[:]), reads=[qkT, ident_b], writes=[pT])
        op('act', lambda e: e.activation(Ktok[0:P, :], pT[0:P, :], AF.Copy), reads=[pT], writes=[Ktok])
        for i in range(NS):
            dma('sp', Cst[:, :, :, 0:256], stC[i].rearrange("h (kc p) v -> p h kc v", p=128), writes=[Cst], chan=Cst)
            with nc.allow_non_contiguous_dma(reason="tiny"):
                dma('sp', Cst[:, :, :, 256], stn[i].rearrange("h (kc p) -> p h kc", p=128), writes=[Cst], chan=Cst)
            for h in range(4):
                op('dve', lambda e, h=h, i=i: e.tensor_scalar(Kp[0:P, :], Ktok[0:P, h * 256:(h + 1) * 256], colsS[:, 16 + i * 4 + h:17 + i * 4 + h], None, ALU.mult),
                   reads=[Ktok, ('colsS2',)], writes=[Kp])
                for kc in range(2):
                    pu = nextps()
                    op('pe', lambda e, kc=kc, pu=pu, h=h: e.matmul(pu[:, 0:257], Kp[0:P, kc * 128:(kc + 1) * 128], vext[0][0:P, h, :], start=True, stop=True),
                       reads=[Kp, vext[0]], writes=[pu])
                    op('dve', lambda e, kc=kc, pu=pu, h=h, i=i: e.scalar_tensor_tensor(Cst[:, h, kc, :], Cst[:, h, kc, :], acols[:, h * 4 + i:h * 4 + i + 1], pu[:, 0:257], ALU.mult, ALU.add),
                       reads=[pu, Cst, acols], writes=[Cst])
                    op('dve', lambda e, kc=kc, h=h: e.tensor_copy(Cbf[:, h, kc, :], Cst[:, h, kc, :]), reads=[Cst], writes=[Cbf])
                pn_ = nextps()
                for kc in range(2):
                    op('pe', lambda e, kc=kc, pn_=pn_, h=h, i=i: e.matmul(pn_[0:1, 0:257], qkT[:, 2 * h + kc, i:i + 1], Cbf[:, h, kc, :], start=(kc == 0), stop=(kc == 1)),
                       reads=[qkT, Cbf], writes=[pn_])
                op('act', lambda e, pn_=pn_: e.activation(cols[0:1, 12:13], pn_[0:1, 256:257], AF.Abs), reads=[pn_], writes=[('c12', 0)])
                op('dve', lambda e, h=h, i=i: e.tensor_tensor(cols[0:1, 12:13], cols[0:1, 12:13], acols[0:1, 16 + h * 4 + i:17 + h * 4 + i], ALU.max),
                   reads=[('c12', 0), acols], writes=[('c12', 0)])
                op('dve', lambda e: e.reciprocal(cols[0:1, 12:13], cols[0:1, 12:13]), reads=[('c12', 0)], writes=[('c12', 0)])
                op('act', lambda e, pn_=pn_, h=h: e.activation(hh[0:1, h * 256:(h + 1) * 256], pn_[0:1, 0:256], AF.Identity, scale=cols[0:1, 12:13]),
                   reads=[pn_, ('c12', 0)], writes=[hh])
            for h in range(4):
                for kc in range(2):
                    dma('sp', sCo[i, h, kc * 128:(kc + 1) * 128, :], Cst[:, h, kc, 0:256], reads=[Cst], chan=Cst)
            with nc.allow_non_contiguous_dma(reason="tiny"):
                dma('sp', sno[i].rearrange("h (kc p) -> p h kc", p=128), Cst[:, :, :, 256], reads=[Cst], chan=Cst)
            for kc in range(8):
                op('pe', lambda e, kc=kc, i=i: e.transpose(pT[:, kc * 128 + 2 * i:kc * 128 + 2 * i + 1], hh[0:1, kc * 128:(kc + 1) * 128], ident_b[0:1, 0:1]), reads=[hh, ident_b], writes=[pT])
        op('dve', lambda e: e.tensor_tensor(gatedT[:, :, 0:P], pT[:].rearrange("p (a b) -> p a b", b=128)[:, :, 0:2 * P:2], sigT[:, :, 0:P], ALU.mult),
           reads=[pT, sigT], writes=[gatedT])
        wts = [wslab('w_out', 0), wslab('w_out', 1)]
        wout_block(0, P, wts)
        post_norm_res(pmix[0], xblk[0], P, "gg1")
        load_gs(0, 'mlp', True)
        load_gg(0, 'mlp', True)
        mlp(0, blocks_s, P, after_norm=lambda: load_gs(0, 'kv', True))
        kv_proj(blocks_s, 0, True, after_norm=lambda: load_gs(1, 'mix', True))
        load_gg(1, 'mix', True)
        norm_mod_T(xblk[0], P, "gm1", "sh1", 0)
        load_gs(1, 'mlp', True)
        q_proj(P)
        steps = []
        for g, (win, dil) in enumerate(GROUPS):
            kb, vb = g * D, 3 * D + g * D
            for i in range(NS):
                steps.append(dict(q3=QT[:, g * 8:(g + 1) * 8, i:i + 1], nq=1, nh=128,
                                  khist=cache[('k', g)][i, 0:win:dil, :], vhist=cache[('v', g)][i, 0:win:dil, :],
                                  kcur=kvscr_s[i:i + 1, kb:kb + D], vcur=kvscr_s[i:i + 1, vb:vb + D],
                                  odst=oscr_s[g, i:i + 1, :], rd=[('dram', 'kvscr_s')], wr=[('dram', 'oscr_s', g, i)], hq='pool'))
        run_steps(steps)
        wts = [wslab('w_o', 0), wslab('w_o', 1)]
        attn_combine(P, oscr_s[:, 0:P, :], [('dram', 'oscr_s', g_, i_) for g_ in range(3) for i_ in range(NS)], 0, wts)
        post_norm_res(pmix[0], xblk[0], P, "gg1")
        load_gg(1, 'mlp', True)
        mlp(1, blocks_s, P)
        dma('sp', ys, xblk[0][0:P, :], reads=[xblk[0]], chan=xblk[0])

    import os as _os
    _nt = int(_os.environ.get("KDBG_TILES", ntile))
    nt_ = min(ntile, _nt)
    if nt_ > 0:
        for f_ in front_pieces(0):
            f_()
        back(0)
    for ti in range(nt_):
        layer1_tile(ti, front_pieces(ti + 1) if ti + 1 < nt_ else None)
        if ti + 1 < nt_:
            back(ti + 1)

    for h in range(4):
        for kc in range(2):
            dma('sp', pC[h, kc * 128:(kc + 1) * 128, :], Cst[:, h, kc, 0:256], reads=[Cst], chan=Cst)
    with nc.allow_non_contiguous_dma(reason="tiny"):
        dma('sp', pn.rearrange("h (kc p) -> p h kc", p=128), Cst[:, :, :, 256], reads=[Cst], chan=Cst)
    op('dve', lambda e: e.tensor_tensor(rcol["m"][:, 0:1], rcol["Bprev"][:, 0:1], rcol["gprev"][:, 0:1], ALU.add), reads=[rcol["Bprev"], rcol["gprev"]], writes=[rcol["m"]])
    dma('sp', pm, rcol["m"][:, 0:1], reads=[rcol["m"]], chan=rcol["m"])

    if do_samples and not _os.environ.get("KDBG_NOSAMP"):
        sample_tile()

    S.finish()
    if dry:
        return wreq
    return nc


def _consts():
    ik = np.arange(128)[:, None]
    iq = np.arange(128)[None, :]
    maskA = (ik >= iq).astype(np.float32)
    maskB = (ik <= iq).astype(np.float32)
    vtab = (np.arange(128)[:, None] >= (128 - np.arange(129)[None, :])).astype(np.float32)
    return np.eye(128, dtype=np.float32), maskA, maskB, vtab


def make_in_maps(inp, ncores, T):
    ident, maskA, maskB, vtab = _consts()
    maps = []
    f = lambda a: np.ascontiguousarray(np.asarray(a, dtype=np.float32))
    nb = inp['x_prompt'].shape[0]
    for c in range(ncores):
        b = c % nb
        sl = slice(NS * c, NS * c + NS)
        m = {
            "xp": f(inp['x_prompt'][b]), "xs": f(inp['x_sample'][sl, 0]),
            "c5": f(np.concatenate([inp['c_prompt'][b:b + 1], inp['c_sample'][sl]], 0)),
            "stC": f(inp['state_C'][0, sl]), "stn": f(inp['state_n'][0, sl]), "stm": f(inp['state_m'][0, sl]),
            "stconv": f(inp['state_conv'][0, sl]),
            "w_ada": f(inp['w_ada']), "b_ada": f(inp['b_ada']), "g_norm": f(inp['g_norm']),
            "w_mlp_up": f(inp['w_mlp_up']), "w_mlp_down": f(inp['w_mlp_down']), "w_a_in": f(inp['w_a_in'][0]),
            "b_a_gate": f(inp['b_a_gate'][0]),
            "wcb": f(np.concatenate([inp['w_a_conv'][0], inp['b_a_conv'][0][None]], 0)),
            "w_a_out": f(inp['w_a_out'][0]), "g_kv": f(inp['g_kv']), "w_ada_kv": f(inp['w_ada_kv']),
            "b_ada_kv": f(inp['b_ada_kv']), "w_kv": f(inp['w_kv']), "w_b_q": f(inp['w_b_q'][0]), "w_b_o": f(inp['w_b_o'][0]),
            "ident": ident, "maskA": maskA, "maskB": maskB, "vtab": vtab,
        }
        caches = ((inp['cache_k_g0'], inp['cache_v_g0']), (inp['cache_k_g1'], inp['cache_v_g1']),
                  (inp['cache_k_g2'], inp['cache_v_g2']))
        for g in range(3):
            m["ck%d" % g] = f(caches[g][0][sl]).reshape(NS, -1, D)
            m["cv%d" % g] = f(caches[g][1][sl]).reshape(NS, -1, D)
        maps.append(m)
    return maps


def assemble(results, nb, nsb, T):
    R0 = results
    ncores = len(R0)
    y_prompt = np.stack([R0[b]["yp"] for b in range(nb)])
    y_sample = np.concatenate([R0[c]["ys"] for c in range(ncores)])[:, None, :]
    p_C = np.stack([R0[b]["pC"] for b in range(nb)])[None]
    p_n = np.stack([R0[b]["pn"] for b in range(nb)])[None]
    p_m = np.stack([R0[b]["pm"][:, 0] for b in range(nb)])[None]
    p_conv = np.stack([R0[b]["pconv"] for b in range(nb)])[None]
    s_C = np.concatenate([R0[c]["sC"] for c in range(ncores)])[None]
    s_n = np.concatenate([R0[c]["sn"] for c in range(ncores)])[None]
    s_m = np.concatenate([R0[c]["sm"] for c in range(ncores)])[None]
    s_conv = np.concatenate([R0[c]["sconv"] for c in range(ncores)])[None]
    outs = [y_prompt, y_sample, p_C, p_n, p_m, p_conv, s_C, s_n, s_m, s_conv]
    for g in range(3):
        for kv in ("pk", "pv"):
            a = np.stack([R0[b]["%s%d" % (kv, g)] for b in range(nb)])
            outs.append(a.reshape(nb, a.shape[1], 8, 128))
    for g in range(3):
        for kv in ("sk", "sv"):
            a = np.concatenate([R0[c]["%s%d" % (kv, g)] for c in range(ncores)])
            outs.append(a.reshape(a.shape[0], 1, 8, 128))
    return tuple(np.ascontiguousarray(o, dtype=np.float32) for o in outs)


def kernel(**inputs):
    inp = {k: np.asarray(v) for k, v in inputs.items()}
    T = inp['x_prompt'].shape[1]
    nb = inp['x_prompt'].shape[0]
    ncores = 8
    nc = build(T)
    maps = make_in_maps(inp, ncores, T)
    res = run_bass_kernel_spmd(nc, maps, core_ids=list(range(ncores)))
    return assemble(res.results, nb, inp['x_sample'].shape[0], T)
```
